# Optimizing a Trainium2 kernel written in Bass

```python
import math
import jax, jax.numpy as jnp
from jax import lax
import numpy as np

D_MODEL = 1024
BATCH = 4
SEQ = 4096
DEPTH = 4

D_BRANCH = D_MODEL // 2
SSM_GROUP_SIZE = 16
SSM_GROUPS = D_BRANCH // SSM_GROUP_SIZE
SSM_STATE = 64
SB_HEAD_DIM = 64
SB_HEADS = D_BRANCH // SB_HEAD_DIM
DIFF_HEAD_DIM = 64
DIFF_HEADS = D_BRANCH // (2 * DIFF_HEAD_DIM)
N_BRANCHES = 3
Q_BLOCK = 128
ROPE_THETA = 10000.0
NORM_EPS = 1e-6
DT_MIN = 1e-3
DT_MAX = 1e-1
D_IN_PROJ = 10 * D_BRANCH + N_BRANCHES * D_MODEL

kernel_name = "hybrid_s5_stickbreak_diffattn_gated"


def _rmsnorm(x, w):
    xf = x.astype(jnp.float32)
    y = xf * lax.rsqrt(jnp.mean(xf * xf, axis=-1, keepdims=True) + NORM_EPS)
    return (y * w.astype(jnp.float32)).astype(x.dtype)


def _rope(x, positions):
    d = x.shape[-1]
    half = d // 2
    inv_freq = ROPE_THETA ** (-jnp.arange(half, dtype=jnp.float32) / half)
    ang = positions.astype(jnp.float32)[:, None] * inv_freq[None, :]
    bshape = (positions.shape[0],) + (1,) * (x.ndim - 3) + (half,)
    cos = jnp.cos(ang).reshape(bshape)
    sin = jnp.sin(ang).reshape(bshape)
    xf = x.astype(jnp.float32)
    x1, x2 = xf[..., :half], xf[..., half:]
    out = jnp.concatenate([x1 * cos - x2 * sin, x2 * cos + x1 * sin], axis=-1)
    return out.astype(x.dtype)


def _to_blocks(q):
    *lead, s, d = q.shape
    qb = q.reshape(*lead, s // Q_BLOCK, Q_BLOCK, d)
    return jnp.moveaxis(qb, -3, 0)


def _from_blocks(o):
    o = jnp.moveaxis(o, 0, -3)
    *lead, nb, qb, d = o.shape
    return o.reshape(*lead, nb * qb, d)


def _s5_scan(u, a_re, a_im, log_dt, b_re, b_im, c_re, c_im, d_skip):
    f32 = jnp.float32
    u = u.astype(f32)
    a_re, a_im = a_re.astype(f32), a_im.astype(f32)
    dt = jnp.exp(log_dt.astype(f32))[:, None]
    mag = jnp.exp(dt * a_re)
    abar_re = mag * jnp.cos(dt * a_im)
    abar_im = mag * jnp.sin(dt * a_im)
    denom = a_re * a_re + a_im * a_im
    coef_re = ((abar_re - 1.0) * a_re + abar_im * a_im) / denom
    coef_im = (abar_im * a_re - (abar_re - 1.0) * a_im) / denom
    b_re, b_im = b_re.astype(f32), b_im.astype(f32)
    bb_re = coef_re[..., None] * b_re - coef_im[..., None] * b_im
    bb_im = coef_re[..., None] * b_im + coef_im[..., None] * b_re
    bu_re = jnp.einsum('bsgc,gpc->bsgp', u, bb_re)
    bu_im = jnp.einsum('bsgc,gpc->bsgp', u, bb_im)
    ar = jnp.broadcast_to(abar_re, bu_re.shape)
    ai = jnp.broadcast_to(abar_im, bu_re.shape)

    def combine(e1, e2):
        a1r, a1i, b1r, b1i = e1
        a2r, a2i, b2r, b2i = e2
        return (a2r * a1r - a2i * a1i,
                a2r * a1i + a2i * a1r,
                a2r * b1r - a2i * b1i + b2r,
                a2r * b1i + a2i * b1r + b2i)

    _, _, xr, xi = lax.associative_scan(combine, (ar, ai, bu_re, bu_im), axis=1)
    y = (jnp.einsum('bsgp,gcp->bsgc', xr, c_re.astype(f32))
         - jnp.einsum('bsgp,gcp->bsgc', xi, c_im.astype(f32)))
    return y + d_skip.astype(f32) * u


def _stick_breaking(q, k, v):
    s_len, d = q.shape[2], q.shape[3]
    scale = 1.0 / math.sqrt(d)
    key_pos = jnp.arange(s_len, dtype=jnp.int32)

    def block(args):
        qb, start = args
        z = jnp.einsum('bhqd,bhkd->bhqk', qb, k).astype(jnp.float32) * scale
        q_pos = start + jnp.arange(Q_BLOCK, dtype=jnp.int32)
        mask = key_pos[None, :] < q_pos[:, None]
        log_keep = jnp.where(mask, jax.nn.log_sigmoid(-z), 0.0)
        after = lax.cumsum(log_keep, axis=3, reverse=True) - log_keep
        w = jnp.where(mask, jnp.exp(jax.nn.log_sigmoid(z) + after), 0.0)
        return jnp.einsum('bhqk,bhkd->bhqd', w.astype(v.dtype), v)

    starts = jnp.arange(s_len // Q_BLOCK, dtype=jnp.int32) * Q_BLOCK
    out = lax.map(block, (_to_blocks(q), starts))
    return _from_blocks(out)


def _diff_attention(q, k, v, lam):
    s_len, d = q.shape[3], q.shape[4]
    scale = 1.0 / math.sqrt(d)
    key_pos = jnp.arange(s_len, dtype=jnp.int32)

    def block(args):
        qb, start = args
        s = jnp.einsum('bhiqd,bhikd->bhiqk', qb, k).astype(jnp.float32) * scale
        q_pos = start + jnp.arange(Q_BLOCK, dtype=jnp.int32)
        mask = key_pos[None, :] <= q_pos[:, None]
        p = jax.nn.softmax(jnp.where(mask, s, -jnp.inf), axis=-1)
        a = p[:, :, 0] - lam * p[:, :, 1]
        return jnp.einsum('bhqk,bhkd->bhqd', a.astype(v.dtype), v)

    starts = jnp.arange(s_len // Q_BLOCK, dtype=jnp.int32) * Q_BLOCK
    out = lax.map(block, (_to_blocks(q), starts))
    return _from_blocks(out)


def _layer(x, layer_idx, norm_w, w_in, b_merge, a_re, a_im, log_dt, b_re, b_im,
           c_re, c_im, d_skip, w_glu, b_glu, lq1, lk1, lq2, lk2, subln_w,
           w_branch, w_out):
    bsz, s_len, _ = x.shape
    h = _rmsnorm(x, norm_w)
    proj = jnp.einsum('bsd,de->bse', h, w_in)
    (ssm_u, ssm_gate, sb_q, sb_k, sb_v, sb_gate,
     df_q, df_k, df_v, df_gate, merge_logits) = jnp.split(
        proj, [D_BRANCH * i for i in range(1, 11)], axis=-1)

    y = _s5_scan(ssm_u.reshape(bsz, s_len, SSM_GROUPS, SSM_GROUP_SIZE),
                 a_re, a_im, log_dt, b_re, b_im, c_re, c_im, d_skip)
    y = jax.nn.gelu(y.reshape(bsz, s_len, D_BRANCH).astype(x.dtype))
    y = y * jax.nn.sigmoid(y @ w_glu + b_glu)
    o_ssm = y * jax.nn.silu(ssm_gate)

    def sb_heads(t):
        return t.reshape(bsz, s_len, SB_HEADS, SB_HEAD_DIM).transpose(0, 2, 1, 3)
    o = _stick_breaking(sb_heads(sb_q), sb_heads(sb_k), sb_heads(sb_v))
    o_sb = o.transpose(0, 2, 1, 3).reshape(bsz, s_len, D_BRANCH) * jax.nn.silu(sb_gate)

    positions = jnp.arange(s_len, dtype=jnp.int32)
    q = _rope(df_q.reshape(bsz, s_len, DIFF_HEADS, 2, DIFF_HEAD_DIM), positions).transpose(0, 2, 3, 1, 4)
    k = _rope(df_k.reshape(bsz, s_len, DIFF_HEADS, 2, DIFF_HEAD_DIM), positions).transpose(0, 2, 3, 1, 4)
    v = df_v.reshape(bsz, s_len, DIFF_HEADS, 2 * DIFF_HEAD_DIM).transpose(0, 2, 1, 3)
    lam_init = 0.8 - 0.6 * math.exp(-0.3 * layer_idx)
    lam = (jnp.exp(jnp.sum(lq1.astype(jnp.float32) * lk1.astype(jnp.float32)))
           - jnp.exp(jnp.sum(lq2.astype(jnp.float32) * lk2.astype(jnp.float32))) + lam_init)
    o = _diff_attention(q, k, v, lam)
    o = _rmsnorm(o, subln_w) * (1.0 - lam_init)
    o_diff = o.transpose(0, 2, 1, 3).reshape(bsz, s_len, D_BRANCH) * jax.nn.silu(df_gate)

    branches = jnp.stack([o_ssm, o_sb, o_diff], axis=2)
    gates = jax.nn.sigmoid(merge_logits + b_merge).reshape(bsz, s_len, N_BRANCHES, D_MODEL)
    merged = jnp.einsum('bsnc,ncd->bsnd', branches, w_branch)
    merged = jnp.sum(gates * merged, axis=2)
    return x + merged @ w_out


def setup_inputs(seed: int = 0) -> dict:
    key = jax.random.key(seed)
    ks = jax.random.split(key, 24)
    f32 = jnp.float32
    n = jnp.arange(SSM_STATE, dtype=f32)
    g, p, c = SSM_GROUPS, SSM_STATE, SSM_GROUP_SIZE
    return {
        "x": jax.random.normal(ks[0], (BATCH, SEQ, D_MODEL), f32),
        "norm_w": 1.0 + 0.02 * jax.random.normal(ks[1], (DEPTH, D_MODEL), f32),
        "w_in": jax.random.normal(ks[2], (DEPTH, D_MODEL, D_IN_PROJ), f32) * D_MODEL ** -0.5,
        "b_merge": 0.01 * jax.random.normal(ks[3], (DEPTH, N_BRANCHES * D_MODEL), f32),
        "ssm_a_re": -0.5 + 0.01 * jax.random.normal(ks[4], (DEPTH, g, p), f32),
        "ssm_a_im": math.pi * n + 0.01 * jax.random.normal(ks[5], (DEPTH, g, p), f32),
        "ssm_log_dt": jax.random.uniform(ks[6], (DEPTH, g), f32,
                                         minval=math.log(DT_MIN), maxval=math.log(DT_MAX)),
        "ssm_b_re": jax.random.normal(ks[7], (DEPTH, g, p, c), f32) * (2 * c) ** -0.5,
        "ssm_b_im": jax.random.normal(ks[8], (DEPTH, g, p, c), f32) * (2 * c) ** -0.5,
        "ssm_c_re": jax.random.normal(ks[9], (DEPTH, g, c, p), f32) * p ** -0.5,
        "ssm_c_im": jax.random.normal(ks[10], (DEPTH, g, c, p), f32) * p ** -0.5,
        "ssm_d": jax.random.normal(ks[11], (DEPTH, g, c), f32),
        "ssm_w_glu": jax.random.normal(ks[12], (DEPTH, D_BRANCH, D_BRANCH), f32) * D_BRANCH ** -0.5,
        "ssm_b_glu": 0.01 * jax.random.normal(ks[13], (DEPTH, D_BRANCH), f32),
        "diff_lq1": 0.1 * jax.random.normal(ks[14], (DEPTH, DIFF_HEAD_DIM), f32),
        "diff_lk1": 0.1 * jax.random.normal(ks[15], (DEPTH, DIFF_HEAD_DIM), f32),
        "diff_lq2": 0.1 * jax.random.normal(ks[16], (DEPTH, DIFF_HEAD_DIM), f32),
        "diff_lk2": 0.1 * jax.random.normal(ks[17], (DEPTH, DIFF_HEAD_DIM), f32),
        "diff_subln_w": 1.0 + 0.02 * jax.random.normal(ks[18], (DEPTH, 2 * DIFF_HEAD_DIM), f32),
        "w_branch": jax.random.normal(ks[19], (DEPTH, N_BRANCHES, D_BRANCH, D_MODEL), f32) * D_BRANCH ** -0.5,
        "w_out": jax.random.normal(ks[20], (DEPTH, D_MODEL, D_MODEL), f32) * D_MODEL ** -0.5,
        "final_norm_w": 1.0 + 0.02 * jax.random.normal(ks[21], (D_MODEL,), f32),
    }


def reference(x, norm_w, w_in, b_merge, ssm_a_re, ssm_a_im, ssm_log_dt, ssm_b_re,
              ssm_b_im, ssm_c_re, ssm_c_im, ssm_d, ssm_w_glu, ssm_b_glu, diff_lq1,
              diff_lk1, diff_lq2, diff_lk2, diff_subln_w, w_branch, w_out, final_norm_w):
    for i in range(DEPTH):
        x = _layer(x, i, norm_w[i], w_in[i], b_merge[i], ssm_a_re[i], ssm_a_im[i],
                   ssm_log_dt[i], ssm_b_re[i], ssm_b_im[i], ssm_c_re[i], ssm_c_im[i],
                   ssm_d[i], ssm_w_glu[i], ssm_b_glu[i], diff_lq1[i], diff_lk1[i],
                   diff_lq2[i], diff_lk2[i], diff_subln_w[i], w_branch[i], w_out[i])
    return _rmsnorm(x, final_norm_w)
```

```python
import math
from contextlib import ExitStack

import numpy as np
import concourse.bass as bass
import concourse.mybir as mybir
from concourse.bass_utils import run_bass_kernel_spmd

F32 = mybir.dt.float32
BF16 = mybir.dt.bfloat16
I32 = mybir.dt.int32
AF = mybir.ActivationFunctionType
ALU = mybir.AluOpType
AX = mybir.AxisListType

P = 128
S = 4096
D = 1024
TT = 512
NT = S // TT
NB = S // P
DC = D // P
DEPTH = 4
NPROJ = 8192
EPS = 1e-6
MAGIC = 12582912.0
TWO_PI = 2.0 * math.pi
NEG = -30000.0
SEM_CAP = 30000


class Res:
    __slots__ = ("name", "lw", "readers", "excl", "slot")

    def __init__(self, name, excl=False):
        self.name = name
        self.lw = None
        self.readers = []
        self.excl = excl
        self.slot = None


class SemSlot:
    __slots__ = ("sem", "cnt")

    def __init__(self, sem):
        self.sem = sem
        self.cnt = 0


class Eng:
    def __init__(self, fw, name, handle):
        self.fw = fw
        self.name = name
        self.h = handle
        self.sems = []
        self.count = 0
        self.waited = {}

    def cur_event(self):
        n = self.count + 1
        k = (n - 1) // SEM_CAP
        while len(self.sems) <= k:
            self.sems.append(self.fw.new_sem(f"{self.name}{len(self.sems)}"))
        return (self.sems[k], n - k * SEM_CAP, self.name)

    def last_event(self):
        if self.count == 0:
            return None
        n = self.count
        k = (n - 1) // SEM_CAP
        return (self.sems[k], n - k * SEM_CAP, self.name)


class FW:
    def __init__(self, nc, es):
        self.nc = nc
        self.es = es
        self.nsem = 0
        self.E = {
            "pe": Eng(self, "pe", nc.tensor),
            "act": Eng(self, "act", nc.scalar),
            "dve": Eng(self, "dve", nc.vector),
            "pool": Eng(self, "pool", nc.gpsimd),
            "sp": Eng(self, "sp", nc.sync),
        }
        self.out_events = []
        self.free_slots = []
        self.all_slots = []
        self.phase_res = []
        self.ninst = 0

    def new_sem(self, name):
        self.nsem += 1
        return self.es.enter_context(self.nc.semaphore(f"s{self.nsem}_{name}"))

    def get_slot(self, res, phase_local=True):
        if res.slot is None:
            if self.free_slots:
                res.slot = self.free_slots.pop()
            else:
                res.slot = SemSlot(self.new_sem("dma"))
                self.all_slots.append(res.slot)
            if phase_local:
                self.phase_res.append(res)
        return res.slot

    def _need(self, eng, ev, waits):
        if ev is None:
            return
        sem, val, _ = ev
        key = id(sem)
        if eng.waited.get(key, 0) >= val:
            return
        cur = waits.get(key)
        if cur is None or cur[1] < val:
            waits[key] = (sem, val)

    def _collect(self, eng, reads, writes, waits):
        for r in reads:
            if r.lw is not None:
                self._need(eng, r.lw, waits)
            if r.excl:
                for ev in r.readers:
                    if ev[2] != eng.name:
                        self._need(eng, ev, waits)
        for w in writes:
            if w.lw is not None and w.lw[2] != eng.name:
                self._need(eng, w.lw, waits)
            for ev in w.readers:
                if ev[2] != eng.name:
                    self._need(eng, ev, waits)

    def _emit_waits(self, eng, waits):
        for key, (sem, val) in waits.items():
            eng.h.wait_ge(sem, val)
            eng.waited[key] = val
            self.ninst += 1

    def _update(self, ev, reads, writes):
        for r in reads:
            r.readers.append(ev)
            if len(r.readers) > 16:
                last = {}
                for e in r.readers:
                    k = (e[2], id(e[0]))
                    if k not in last or last[k][1] < e[1]:
                        last[k] = e
                r.readers = list(last.values())
        for w in writes:
            w.lw = ev
            w.readers = []

    def op(self, engname, fn, reads=(), writes=()):
        eng = self.E[engname]
        waits = {}
        self._collect(eng, reads, writes, waits)
        self._emit_waits(eng, waits)
        ins = fn(eng.h)
        ev = eng.cur_event()
        ins.then_inc(ev[0], 1)
        eng.count += 1
        self._update(ev, reads, writes)
        self.ninst += 1
        return ev

    def dma(self, out, in_, reads=(), writes=(), q="sp", semres=None, final=False, persistent=False, **kw):
        eng = self.E[q]
        if semres is None:
            semres = writes[0] if writes else reads[0]
        slot = self.get_slot(semres, phase_local=not persistent)
        if slot.cnt >= 1800:
            slot2 = SemSlot(self.new_sem("dma"))
            self.all_slots.append(slot2)
            semres.slot = slot2
            old = slot
            slot = slot2
            waits0 = {}
            self._need(eng, (old.sem, 16 * old.cnt, "dma"), waits0)
            self._emit_waits(eng, waits0)
        waits = {}
        self._collect(eng, reads, writes, waits)
        if slot.cnt > 0:
            self._need(eng, (slot.sem, 16 * slot.cnt, "dma"), waits)
        self._emit_waits(eng, waits)
        eng.h.dma_start(out=out, in_=in_, **kw).then_inc(slot.sem, 16)
        slot.cnt += 1
        ev = (slot.sem, 16 * slot.cnt, "dma")
        self._update(ev, reads, writes)
        if final:
            self.out_events.append(ev)
        self.ninst += 1
        return ev

    def barrier(self):
        evs = []
        for e in self.E.values():
            le = e.last_event()
            if le is not None:
                evs.append(le)
        for s in self.all_slots:
            if s.cnt > 0:
                evs.append((s.sem, 16 * s.cnt, "dma"))
        for e in self.E.values():
            waits = {}
            for ev in evs:
                if ev[2] == e.name:
                    continue
                self._need(e, ev, waits)
            self._emit_waits(e, waits)
        for r in self.phase_res:
            if r.slot is not None:
                self.free_slots.append(r.slot)
                r.slot = None
        self.phase_res = []

    def finish(self):
        eng = self.E["sp"]
        waits = {}
        for ev in self.out_events:
            self._need(eng, ev, waits)
        self._emit_waits(eng, waits)


def run_pipeline(units, stages):
    n = len(units)
    ns = len(stages)
    for step in range(n + ns - 1):
        for s in range(ns - 1, -1, -1):
            u = step - s
            if 0 <= u < n:
                stages[s](units[u])


def build_program(nlayers=DEPTH, dbg=None, stop_after=None):
    nc = bass.Bass("TRN2", target_bir_lowering=False)
    dbg = dbg or {}

    def din(name, shape):
        return nc.dram_tensor(name, list(shape), F32, kind="ExternalInput").ap()

    x_in = din("x", [S, D])
    w_in_d = din("w_in", [DEPTH, 64, P, DC, P])
    w_glu_d = din("w_glu", [DEPTH, P, 4, 512])
    w_br_d = din("w_br", [DEPTH, 3, P, 4, D])
    w_out_d = din("w_out", [DEPTH, P, DC, D])
    normw_d = din("norm_w", [DEPTH, P, DC])
    bmerge_d = din("b_merge", [DEPTH, P, 24])
    bglu_d = din("b_glu", [DEPTH, P, 4])
    subln_d = din("subln_w", [DEPTH, P, 1])
    fnorm_d = din("fnorm_w", [P, DC])
    are_d = din("a_re", [DEPTH, P, 16])
    aim_d = din("a_im", [DEPTH, P, 16])
    ldt_d = din("log_dt", [DEPTH, P, 16])
    bre_d = din("b_re", [DEPTH, P, 16, 16])
    bim_d = din("b_im", [DEPTH, P, 16, 16])
    cre_d = din("c_re", [DEPTH, P, 16, 16])
    cim_d = din("c_im", [DEPTH, P, 16, 16])
    dsk_d = din("d_skip", [DEPTH, P, 4])
    lqk_d = din("lqk", [DEPTH, P, 4, 64])

    y_out = nc.dram_tensor("y", [S, D], F32, kind="ExternalOutput").ap()
    dbg_out = {k: nc.dram_tensor(k, list(v), F32, kind="ExternalOutput").ap() for k, v in dbg.items()}

    xT_d = nc.dram_tensor("xT_scr", [P, DC, S], F32).ap()
    obr_d = nc.dram_tensor("obr_scr", [3, P, 4, S], BF16).ap()
    yg_d = nc.dram_tensor("yg_scr", [P, 4, S], BF16).ap()
    mg_d = nc.dram_tensor("mg_scr", [P, DC, S], BF16).ap()
    ropeC_d = nc.dram_tensor("ropeC_scr", [P, S], F32).ap()
    ropeS_d = nc.dram_tensor("ropeS_scr", [P, S], F32).ap()

    es = ExitStack()
    with es:
        fw = FW(nc, es)

        _uid = [0]

        def sbt(stack, name, shape, dt):
            _uid[0] += 1
            return stack.enter_context(nc.sbuf_tensor(f"{name}_{_uid[0]}", list(shape), dt))

        hT = sbt(es, "hT", [P, DC, S], BF16)
        R_hT = [Res(f"hT{j}") for j in range(NT)]
        ident_f = sbt(es, "ident_f", [P, P], F32)
        ident_b = sbt(es, "ident_b", [P, P], BF16)
        ones_f = sbt(es, "ones_f", [P, P], F32)
        ones_b = sbt(es, "ones_b", [P, P], BF16)
        nones_b = sbt(es, "nones_b", [P, P], BF16)
        ntri_b = sbt(es, "ntri_b", [P, P], BF16)
        mask_sb = sbt(es, "mask_sb", [P, P], BF16)
        mask_df = sbt(es, "mask_df", [P, P], BF16)
        iota_t = sbt(es, "iota_t", [P, TT], F32)
        halfpi = sbt(es, "halfpi", [P, 1], F32)
        R_const = Res("const")
        NST = 2
        wst = [sbt(es, f"wst{i}", [P, DC, P], F32) for i in range(NST)]
        R_wst = [Res(f"wst{i}") for i in range(NST)]
        wst_i = [0]
        nw = sbt(es, "nw", [P, DC], F32)
        R_nw = Res("nw")

        pb = [es.enter_context(nc.psum_tensor(f"pb{i}", [P, TT], F32)) for i in range(8)]
        R_pb = [Res(f"pb{i}", excl=True) for i in range(8)]

        def consts():
            with ExitStack() as ps:
                io_i = sbt(ps, "io_i", [P, P], I32)
                io_f = sbt(ps, "io_f", [P, P], F32)
                io2_i = sbt(ps, "io2_i", [P, TT], I32)
                R_a, R_b, R_c = Res("io_i"), Res("io_f"), Res("io2")
                fw.op("pool", lambda h: h.iota(io_i[:], pattern=[[1, P]], base=0, channel_multiplier=-1), writes=[R_a])
                fw.op("dve", lambda h: h.tensor_copy(out=io_f[:], in_=io_i[:]), reads=[R_a], writes=[R_b])
                W = [R_const]
                fw.op("dve", lambda h: h.tensor_scalar(out=ident_f[:], in0=io_f[:], scalar1=0.0, scalar2=None, op0=ALU.is_equal), reads=[R_b], writes=W)
                fw.op("dve", lambda h: h.tensor_scalar(out=ident_b[:], in0=io_f[:], scalar1=0.0, scalar2=None, op0=ALU.is_equal), reads=[R_b], writes=W)
                fw.op("dve", lambda h: h.tensor_scalar(out=ntri_b[:], in0=io_f[:], scalar1=0.0, scalar2=-1.0, op0=ALU.is_le, op1=ALU.mult), reads=[R_b], writes=W)
                fw.op("dve", lambda h: h.tensor_scalar(out=mask_sb[:], in0=io_f[:], scalar1=0.0, scalar2=NEG, op0=ALU.is_le, op1=ALU.mult), reads=[R_b], writes=W)
                fw.op("dve", lambda h: h.tensor_scalar(out=mask_df[:], in0=io_f[:], scalar1=0.0, scalar2=NEG, op0=ALU.is_lt, op1=ALU.mult), reads=[R_b], writes=W)
                fw.op("pool", lambda h: h.memset(ones_f[:], 1.0), writes=W)
                fw.op("pool", lambda h: h.memset(ones_b[:], 1.0), writes=W)
                fw.op("pool", lambda h: h.memset(nones_b[:], -1.0), writes=W)
                fw.op("pool", lambda h: h.memset(halfpi[:], math.pi / 2), writes=W)
                fw.op("pool", lambda h: h.iota(io2_i[:], pattern=[[1, TT]], base=0, channel_multiplier=0), writes=[R_c])
                fw.op("dve", lambda h: h.tensor_copy(out=iota_t[:], in_=io2_i[:]), reads=[R_c], writes=W)
                pidx_i = sbt(ps, "pidx_i", [P, 1], I32)
                pidx = sbt(ps, "pidx", [P, 1], F32)
                wfreq = sbt(ps, "wfreq", [P, 1], F32)
                R_p = Res("pidx")
                fw.op("pool", lambda h: h.iota(pidx_i[:], pattern=[[0, 1]], base=0, channel_multiplier=1), writes=[R_p])
                fw.op("dve", lambda h: h.tensor_copy(out=pidx[:], in_=pidx_i[:]), reads=[R_p], writes=[R_p])
                t0 = sbt(ps, "t0", [P, 1], F32)
                fw.op("dve", lambda h: h.tensor_scalar(out=t0[:], in0=pidx[:], scalar1=-15.5, scalar2=1.0 / 32, op0=ALU.add, op1=ALU.mult), reads=[R_p], writes=[R_p])
                fw.op("dve", lambda h: h.tensor_scalar(out=t0[:], in0=t0[:], scalar1=MAGIC, scalar2=MAGIC, op0=ALU.add, op1=ALU.subtract), reads=[R_p], writes=[R_p])
                fw.op("dve", lambda h: h.scalar_tensor_tensor(out=pidx[:], in0=t0[:], scalar=-32.0, in1=pidx[:], op0=ALU.mult, op1=ALU.add), reads=[R_p], writes=[R_p])
                fw.op("act", lambda h: h.activation(out=wfreq[:], in_=pidx[:], func=AF.Exp, scale=-math.log(10000.0) / 32), reads=[R_p], writes=[R_p])
                fw.op("dve", lambda h: h.tensor_scalar(out=wfreq[:], in0=wfreq[:], scalar1=1.0 / TWO_PI, scalar2=None, op0=ALU.mult), reads=[R_p], writes=[R_p])
                tpos = sbt(ps, "tpos", [P, TT], F32)
                t1 = sbt(ps, "t1", [P, TT], F32)
                kk = sbt(ps, "kk", [P, TT], F32)
                ff = sbt(ps, "ff", [P, TT], F32)
                tS = sbt(ps, "tS", [P, TT], F32)
                tC = sbt(ps, "tC", [P, TT], F32)
                R_t = Res("ropetmp")
                R_tS, R_tC = Res("tS"), Res("tC")
                for j in range(NT):
                    fw.op("dve", lambda h: h.tensor_scalar(out=tpos[:], in0=iota_t[:], scalar1=float(j * TT), scalar2=None, op0=ALU.add), reads=[R_const], writes=[R_t])
                    fw.op("dve", lambda h: h.tensor_scalar(out=t1[:], in0=tpos[:], scalar1=wfreq[:, 0:1], scalar2=MAGIC, op0=ALU.mult, op1=ALU.add), reads=[R_t, R_p], writes=[R_t])
                    fw.op("dve", lambda h: h.tensor_scalar(out=kk[:], in0=t1[:], scalar1=MAGIC, scalar2=None, op0=ALU.subtract), reads=[R_t], writes=[R_t])
                    fw.op("dve", lambda h: h.scalar_tensor_tensor(out=ff[:], in0=tpos[:], scalar=wfreq[:, 0:1], in1=kk[:], op0=ALU.mult, op1=ALU.subtract), reads=[R_t, R_p], writes=[R_t])
                    fw.op("act", lambda h: h.activation(out=tS[:], in_=ff[:], func=AF.Sin, scale=TWO_PI), reads=[R_t], writes=[R_tS])
                    fw.op("dve", lambda h: h.scalar_tensor_tensor(out=ff[:], in0=ff[:], scalar=-1.0, in1=ff[:], op0=ALU.mult, op1=ALU.max), reads=[R_t], writes=[R_t])
                    fw.op("act", lambda h: h.activation(out=tC[:], in_=ff[:], func=AF.Sin, scale=-TWO_PI, bias=halfpi[:]), reads=[R_t, R_const], writes=[R_tC])
                    fw.dma(ropeS_d[:, j * TT:(j + 1) * TT], tS[:], reads=[R_tS], writes=[R_ropeS[j]], semres=R_tS)
                    fw.dma(ropeC_d[:, j * TT:(j + 1) * TT], tC[:], reads=[R_tC], writes=[R_ropeC[j]], semres=R_tC)
                fw.barrier()

        R_ropeS = [Res(f"ropeS{j}") for j in range(NT)]
        R_ropeC = [Res(f"ropeC{j}") for j in range(NT)]
        R_xT = [Res(f"xT{j}") for j in range(NT)]
        R_obr = [[Res(f"obr{n}_{j}") for j in range(NT)] for n in range(3)]
        R_yg = [Res(f"yg{j}") for j in range(NT)]
        R_mgd = [Res(f"mgd{j}") for j in range(NT)]

        def load_wblock(l, cb, dst, R_dst, rot=False, scale_nw=True, eng="pool"):
            i = wst_i[0] % NST
            wst_i[0] += 1
            st, R_st = wst[i], R_wst[i]
            fw.dma(st[:], w_in_d[l, cb], writes=[R_st], persistent=True)
            nwb = nw[:].unsqueeze(2).broadcast_to([P, DC, P])
            if not rot:
                fw.op(eng, lambda h: h.tensor_tensor(out=dst, in0=st[:], in1=nwb, op=ALU.mult), reads=[R_st, R_nw], writes=[R_dst])
            else:
                for hh in range(2):
                    b0 = hh * 64
                    nwb32 = nw[:].unsqueeze(2).broadcast_to([P, DC, 32])
                    fw.op("dve", lambda h: h.scalar_tensor_tensor(out=dst[:, :, b0:b0 + 32], in0=st[:, :, b0 + 32:b0 + 64], scalar=-1.0, in1=nwb32, op0=ALU.mult, op1=ALU.mult),
                          reads=[R_st, R_nw], writes=[R_dst])
                    fw.op(eng, lambda h: h.tensor_tensor(out=dst[:, :, b0 + 32:b0 + 64], in0=st[:, :, b0:b0 + 32], in1=nwb32, op=ALU.mult),
                          reads=[R_st, R_nw], writes=[R_dst])

        def load_generic(src_ap, dst, R_dst, shape, eng="pool"):
            i = wst_i[0] % NST
            wst_i[0] += 1
            st, R_st = wst[i], R_wst[i]
            a, b = shape
            sv = st[:].rearrange("p c n -> p (c n)")[:, 0:a * b].rearrange("p (a b) -> p a b", a=a)
            fw.dma(sv, src_ap, writes=[R_st], persistent=True)
            fw.op(eng, lambda h: h.tensor_copy(out=dst, in_=sv), reads=[R_st], writes=[R_dst])

        def proj_fm(bank, wbf, R_w, j, M=P, c0=0):
            for c in range(DC):
                fw.op("pe", lambda h, c=c: h.matmul(pb[bank][0:M, :], lhsT=wbf[:, c, c0:c0 + M], rhs=hT[:, c, j * TT:(j + 1) * TT], start=(c == 0), stop=(c == DC - 1)),
                      reads=[R_w, R_hT[j]], writes=[R_pb[bank]])

        def dump(name, src_ap, R_src, dst_slice):
            if name in dbg_out:
                fw.dma(dst_slice(dbg_out[name]), src_ap, reads=[R_src], final=True, semres=R_src)

        def rsqrt_from_psum(bank, scale, dst, R_dst, tmp):
            fw.op("dve", lambda h: h.tensor_scalar(out=tmp, in0=pb[bank][:, :], scalar1=scale, scalar2=EPS, op0=ALU.mult, op1=ALU.add), reads=[R_pb[bank]], writes=[R_dst])
            fw.op("act", lambda h: h.activation(out=tmp, in_=tmp, func=AF.Sqrt), reads=[R_dst], writes=[R_dst])
            fw.op("dve", lambda h: h.reciprocal(out=dst, in_=tmp), reads=[R_dst], writes=[R_dst])

        def norm_tile(xt, R_xt, j, sq, R_sq, rstd, R_rstd, bank):
            for c in range(DC):
                i = c % 2
                fw.op("act", lambda h, c=c, i=i: h.activation(out=sq[i][:], in_=xt[:, c, :], func=AF.Square), reads=[R_xt], writes=[R_sq[i]])
                fw.op("pe", lambda h, c=c, i=i: h.matmul(pb[bank][:, :], lhsT=ones_f[:], rhs=sq[i][:], start=(c == 0), stop=(c == DC - 1)),
                      reads=[R_sq[i], R_const], writes=[R_pb[bank]])
            rsqrt_from_psum(bank, 1.0 / D, rstd[:], R_rstd, rstd[:])

        def phase0():
            with ExitStack() as ps:
                xin = [sbt(ps, f"xin{i}", [P, D], F32) for i in range(2)]
                R_xin = [Res(f"xin{i}") for i in range(2)]
                xt = [sbt(ps, f"xt0_{i}", [P, DC, TT], F32) for i in range(2)]
                R_xt = [Res(f"xt0_{i}") for i in range(2)]
                sq = [sbt(ps, f"sq0_{i}", [P, TT], F32) for i in range(2)]
                R_sq = [Res(f"sq0_{i}") for i in range(2)]
                rstd = sbt(ps, "rstd0", [P, TT], F32)
                R_rstd = Res("rstd0")
                fw.dma(nw[:], normw_d[0], writes=[R_nw], persistent=True)
                for j in range(NT):
                    xb_, R_xb = xt[j % 2], R_xt[j % 2]
                    for blk in range(4):
                        g = j * 4 + blk
                        xi, R_xi = xin[g % 2], R_xin[g % 2]
                        fw.dma(xi[:], x_in[g * P:(g + 1) * P, :], writes=[R_xi])
                        for half in range(2):
                            bank = (g % 2) * 2 + half
                            for cc in range(4):
                                c = half * 4 + cc
                                fw.op("pe", lambda h, c=c, cc=cc, bank=bank: h.transpose(pb[bank][:, cc * P:(cc + 1) * P], xi[:, c * P:(c + 1) * P], ident_f[:]),
                                      reads=[R_xi, R_const], writes=[R_pb[bank]])
                            eng = "act" if half == 0 else "dve"
                            src = pb[bank][:, :].rearrange("p (c t) -> p c t", c=4)
                            dst = xb_[:, half * 4:half * 4 + 4, blk * P:(blk + 1) * P]
                            if eng == "act":
                                fw.op("act", lambda h, src=src, dst=dst: h.copy(out=dst, in_=src), reads=[R_pb[bank]], writes=[R_xb])
                            else:
                                fw.op("dve", lambda h, src=src, dst=dst: h.tensor_copy(out=dst, in_=src), reads=[R_pb[bank]], writes=[R_xb])
                    fw.dma(xT_d[:, :, j * TT:(j + 1) * TT], xb_[:], reads=[R_xb], writes=[R_xT[j]], semres=R_xb)
                    norm_tile(xb_, R_xb, j, sq, R_sq, rstd, R_rstd, 4)
                    fw.op("dve", lambda h, xb_=xb_, j=j: h.tensor_tensor(out=hT[:, :, j * TT:(j + 1) * TT], in0=xb_[:], in1=rstd[:].unsqueeze(1).broadcast_to([P, DC, TT]), op=ALU.mult),
                          reads=[R_xb, R_rstd], writes=[R_hT[j]])
                fw.barrier()

        def ssm_phase(l):
            with ExitStack() as ps:
                w_u = [sbt(ps, f"w_u{i}", [P, DC, P], BF16) for i in range(4)]
                R_wu = [Res(f"w_u{i}") for i in range(4)]
                for cb in range(4):
                    load_wblock(l, cb, w_u[cb][:], R_wu[cb])
                def t16(name):
                    return sbt(ps, name, [P, 16], F32)
                are, aim, ldt = t16("are"), t16("aim"), t16("ldt")
                R_pp = Res("ssm_params")
                R_ld = [Res(f"ssm_ld{i}") for i in range(8)]
                fw.dma(are[:], are_d[l], writes=[R_ld[0]])
                fw.dma(aim[:], aim_d[l], writes=[R_ld[1]])
                fw.dma(ldt[:], ldt_d[l], writes=[R_ld[2]])
                cre = sbt(ps, "cre", [P, 16, 16], F32)
                cim = sbt(ps, "cim", [P, 16, 16], F32)
                dsk = sbt(ps, "dsk", [P, 4], F32)
                fw.dma(cre[:], cre_d[l], writes=[R_ld[5]])
                fw.dma(cim[:], cim_d[l], writes=[R_ld[6]])
                fw.dma(dsk[:], dsk_d[l], writes=[R_ld[7]])
                RL = R_ld
                dt_, th, lr, rho, wturn = t16("dt_"), t16("th"), t16("lr"), t16("rho"), t16("wturn")
                W = [R_pp]

                def dv(fn, reads=()):
                    fw.op("dve", fn, reads=list(reads) + [R_pp], writes=W)

                def ac(fn, reads=()):
                    fw.op("act", fn, reads=list(reads) + [R_pp], writes=W)

                ac(lambda h: h.activation(out=dt_[:], in_=ldt[:], func=AF.Exp), [RL[2]])
                dv(lambda h: h.tensor_tensor(out=th[:], in0=dt_[:], in1=aim[:], op=ALU.mult), [RL[1]])
                dv(lambda h: h.tensor_tensor(out=lr[:], in0=dt_[:], in1=are[:], op=ALU.mult), [RL[0]])
                ac(lambda h: h.activation(out=rho[:], in_=lr[:], func=AF.Exp))
                dv(lambda h: h.tensor_scalar(out=wturn[:], in0=th[:], scalar1=1.0 / TWO_PI, scalar2=None, op0=ALU.mult))

                def sincos(wt, s_out, c_out, tmpk, tmpf, mult=1.0):
                    dv(lambda h: h.tensor_scalar(out=tmpf[:], in0=wt[:], scalar1=mult, scalar2=None, op0=ALU.mult))
                    dv(lambda h: h.tensor_scalar(out=tmpk[:], in0=tmpf[:], scalar1=MAGIC, scalar2=MAGIC, op0=ALU.add, op1=ALU.subtract))
                    dv(lambda h: h.tensor_tensor(out=tmpf[:], in0=tmpf[:], in1=tmpk[:], op=ALU.subtract))
                    ac(lambda h: h.activation(out=s_out[:], in_=tmpf[:], func=AF.Sin, scale=TWO_PI))
                    dv(lambda h: h.scalar_tensor_tensor(out=tmpf[:], in0=tmpf[:], scalar=-1.0, in1=tmpf[:], op0=ALU.mult, op1=ALU.max))
                    ac(lambda h: h.activation(out=c_out[:], in_=tmpf[:], func=AF.Sin, scale=-TWO_PI, bias=halfpi[:]), [R_const])

                sin1, cos1, s512, c512, tk, tf = t16("sin1"), t16("cos1"), t16("s512"), t16("c512"), t16("tk"), t16("tf")
                sincos(wturn, sin1, cos1, tk, tf, 1.0)
                sincos(wturn, s512, c512, tk, tf, float(TT))
                CTr = sbt(ps, "CTr", [P, 4, 4, 4, 32], BF16)
                CTi = sbt(ps, "CTi", [P, 4, 4, 4, 32], BF16)
                R_CT = Res("CT")
                BTr = sbt(ps, "BTr", [P, 16, P], BF16)
                BTi = sbt(ps, "BTi", [P, 16, P], BF16)
                R_BT = Res("BT")
                carry = sbt(ps, "carry", [P, 16, 2], F32)
                R_carry = [Res(f"carry{i}") for i in range(16)]
                ctmp = sbt(ps, "ctmp", [P, 2], F32)
                R_ctmp = Res("ctmp")
                fw.op("pool", lambda h: h.memset(carry[:], 0.0), writes=R_carry)
                with ExitStack() as pp_:
                    bre = sbt(pp_, "bre", [P, 16, 16], F32)
                    bim = sbt(pp_, "bim", [P, 16, 16], F32)
                    fw.dma(bre[:], bre_d[l], writes=[R_ld[3]])
                    fw.dma(bim[:], bim_d[l], writes=[R_ld[4]])

                    def t16p(name):
                        return sbt(pp_, name, [P, 16], F32)
                    Are, Aim, den, am1, cr, ci, tmpa, tmpb = (t16p(n) for n in ("Are", "Aim", "den", "am1", "cr", "ci", "tmpa", "tmpb"))
                    dv(lambda h: h.tensor_tensor(out=Are[:], in0=rho[:], in1=cos1[:], op=ALU.mult))
                    dv(lambda h: h.tensor_tensor(out=Aim[:], in0=rho[:], in1=sin1[:], op=ALU.mult))
                    dv(lambda h: h.tensor_tensor(out=den[:], in0=are[:], in1=are[:], op=ALU.mult))
                    dv(lambda h: h.tensor_tensor(out=tmpa[:], in0=aim[:], in1=aim[:], op=ALU.mult))
                    dv(lambda h: h.tensor_tensor(out=den[:], in0=den[:], in1=tmpa[:], op=ALU.add))
                    dv(lambda h: h.reciprocal(out=den[:], in_=den[:]))
                    dv(lambda h: h.tensor_scalar(out=am1[:], in0=Are[:], scalar1=-1.0, scalar2=None, op0=ALU.add))
                    dv(lambda h: h.tensor_tensor(out=tmpa[:], in0=am1[:], in1=are[:], op=ALU.mult))
                    dv(lambda h: h.tensor_tensor(out=tmpb[:], in0=Aim[:], in1=aim[:], op=ALU.mult))
                    dv(lambda h: h.tensor_tensor(out=tmpa[:], in0=tmpa[:], in1=tmpb[:], op=ALU.add))
                    dv(lambda h: h.tensor_tensor(out=cr[:], in0=tmpa[:], in1=den[:], op=ALU.mult))
                    dv(lambda h: h.tensor_tensor(out=tmpa[:], in0=Aim[:], in1=are[:], op=ALU.mult))
                    dv(lambda h: h.tensor_tensor(out=tmpb[:], in0=am1[:], in1=aim[:], op=ALU.mult))
                    dv(lambda h: h.tensor_tensor(out=tmpa[:], in0=tmpa[:], in1=tmpb[:], op=ALU.subtract))
                    dv(lambda h: h.tensor_tensor(out=ci[:], in0=tmpa[:], in1=den[:], op=ALU.mult))
                    bbr = sbt(pp_, "bbr", [P, 16, 16], F32)
                    bbi = sbt(pp_, "bbi", [P, 16, 16], F32)
                    tb = sbt(pp_, "tb", [P, 16, 16], F32)
                    crb = cr[:].unsqueeze(2).broadcast_to([P, 16, 16])
                    cib = ci[:].unsqueeze(2).broadcast_to([P, 16, 16])
                    dv(lambda h: h.tensor_tensor(out=bbr[:], in0=bre[:], in1=crb, op=ALU.mult), [RL[3]])
                    dv(lambda h: h.tensor_tensor(out=tb[:], in0=bim[:], in1=cib, op=ALU.mult), [RL[4]])
                    dv(lambda h: h.tensor_tensor(out=bbr[:], in0=bbr[:], in1=tb[:], op=ALU.subtract))
                    dv(lambda h: h.tensor_tensor(out=bbi[:], in0=bim[:], in1=crb, op=ALU.mult))
                    dv(lambda h: h.tensor_tensor(out=tb[:], in0=bre[:], in1=cib, op=ALU.mult))
                    dv(lambda h: h.tensor_tensor(out=bbi[:], in0=bbi[:], in1=tb[:], op=ALU.add))
                    Zr = sbt(pp_, "Zr", [P, 4, 4, 4, 32], F32)
                    Zi = sbt(pp_, "Zi", [P, 4, 4, 4, 32], F32)
                    R_Z = Res("Z")
                    for z in (Zr, Zi):
                        fw.op("pool", lambda h, z=z: h.memset(z[:], 0.0), writes=[R_Z])
                    for z in (CTr, CTi):
                        fw.op("pool", lambda h, z=z: h.memset(z[:], 0.0), writes=[R_CT])
                    for q in range(4):
                        for gi in range(2):
                            p0 = gi * 64

                            def v4(t, p0=p0, q=q):
                                return t[p0:p0 + 64, :, :].rearrange("p (cb q) c -> p cb q c", q=4)[:, :, q, :]
                            fw.op("dve", lambda h: h.tensor_copy(out=Zr[p0:p0 + 64, :, q, q, gi * 16:gi * 16 + 16], in_=v4(bbr)), reads=[R_pp], writes=[R_Z])
                            fw.op("dve", lambda h: h.tensor_copy(out=Zi[p0:p0 + 64, :, q, q, gi * 16:gi * 16 + 16], in_=v4(bbi)), reads=[R_pp], writes=[R_Z])
                            fw.op("dve", lambda h: h.tensor_copy(out=CTr[p0:p0 + 64, :, q, q, gi * 16:gi * 16 + 16], in_=v4(cre)), reads=[RL[5]], writes=[R_CT])
                            fw.op("dve", lambda h: h.tensor_scalar(out=CTi[p0:p0 + 64, :, q, q, gi * 16:gi * 16 + 16], in0=v4(cim), scalar1=-1.0, scalar2=None, op0=ALU.mult), reads=[RL[6]], writes=[R_CT])
                    for ri, (Z, BT) in enumerate(((Zr, BTr), (Zi, BTi))):
                        for g4 in range(4):
                            bank = 6 + (g4 % 2)
                            for k in range(4):
                                pair = g4 * 4 + k
                                cbi, qi = pair // 4, pair % 4
                                src = Z[:, cbi, qi, :, :].rearrange("p a b -> p (a b)")
                                fw.op("pe", lambda h, src=src, k=k, bank=bank: h.transpose(pb[bank][:, k * P:(k + 1) * P], src, ident_f[:]), reads=[R_Z, R_const], writes=[R_pb[bank]])
                            fw.op("act", lambda h, BT=BT, g4=g4, bank=bank: h.copy(out=BT[:, g4 * 4:g4 * 4 + 4, :], in_=pb[bank][:, :].rearrange("p (k n) -> p k n", k=4)), reads=[R_pb[bank]], writes=[R_BT])
                    fw.barrier()
                with ExitStack() as pw:
                    NR = 2
                    NU = 3
                    tcos = [sbt(pw, f"tcos{i}", [P, TT], F32) for i in range(4)]
                    tsin = [sbt(pw, f"tsin{i}", [P, TT], F32) for i in range(4)]
                    R_tab = [Res(f"tab{i}") for i in range(4)]
                    tA = sbt(pw, "tabA", [P, TT], F32)
                    tB = sbt(pw, "tabB", [P, TT], F32)
                    R_tt = Res("tabtmp")
                    utf = [sbt(pw, f"utf{i}", [P, TT], F32) for i in range(NU)]
                    utb = [sbt(pw, f"utb{i}", [P, TT], BF16) for i in range(NU)]
                    R_utf = [Res(f"utf{i}") for i in range(NU)]
                    R_utb = [Res(f"utb{i}") for i in range(NU)]
                    mm = [[sbt(pw, f"m{k}_{i}", [P, TT], F32) for k in range(4)] for i in range(NR)]
                    R_mm = [Res(f"mm{i}") for i in range(NR)]
                    bp = [[sbt(pw, f"bp{k}_{i}", [P, TT], F32) for k in range(2)] for i in range(NR)]
                    R_bp = [Res(f"bp{i}") for i in range(NR)]
                    yy = [[sbt(pw, f"yy{k}_{i}", [P, TT], F32) for k in range(2)] for i in range(NR)]
                    R_yy = [Res(f"yy{i}") for i in range(NR)]
                    m2 = [[sbt(pw, f"n{k}_{i}", [P, TT], F32) for k in range(2)] for i in range(2)]
                    R_m2 = [Res(f"m2_{i}") for i in range(2)]
                    xre = [sbt(pw, f"xre{i}", [P, 4, TT], BF16) for i in range(2)]
                    xim = [sbt(pw, f"xim{i}", [P, 4, TT], BF16) for i in range(2)]
                    R_xx = [[Res(f"xx{i}_{q}") for q in range(4)] for i in range(2)]
                    yv = [sbt(pw, f"yv{i}", [P, TT], F32) for i in range(2)]
                    g1 = sbt(pw, "g1", [P, TT], F32)
                    g2 = sbt(pw, "g2", [P, TT], F32)
                    ygb = [sbt(pw, f"ygb{i}", [P, TT], BF16) for i in range(2)]
                    R_yv = [Res(f"yv{i}") for i in range(2)]
                    R_g = Res("gel")
                    R_ygb = [Res(f"ygb{i}") for i in range(2)]

                    ucount = [0]
                    for cb in range(4):
                        for q in range(4):
                            pair = cb * 4 + q
                            wcol = wturn[:, pair:pair + 1]
                            fw.op("dve", lambda h: h.tensor_scalar(out=tA[:], in0=iota_t[:], scalar1=wcol, scalar2=MAGIC, op0=ALU.mult, op1=ALU.add), reads=[R_const, R_pp], writes=[R_tt])
                            fw.op("dve", lambda h: h.tensor_scalar(out=tB[:], in0=tA[:], scalar1=MAGIC, scalar2=None, op0=ALU.subtract), reads=[R_tt], writes=[R_tt])
                            fw.op("dve", lambda h: h.scalar_tensor_tensor(out=tA[:], in0=iota_t[:], scalar=wcol, in1=tB[:], op0=ALU.mult, op1=ALU.subtract), reads=[R_const, R_pp, R_tt], writes=[R_tt])
                            fw.op("act", lambda h: h.activation(out=tsin[q][:], in_=tA[:], func=AF.Sin, scale=TWO_PI), reads=[R_tt], writes=[R_tab[q]])
                            fw.op("dve", lambda h: h.scalar_tensor_tensor(out=tB[:], in0=tA[:], scalar=-1.0, in1=tA[:], op0=ALU.mult, op1=ALU.max), reads=[R_tt], writes=[R_tt])
                            fw.op("act", lambda h: h.activation(out=tcos[q][:], in_=tB[:], func=AF.Sin, scale=-TWO_PI, bias=halfpi[:]), reads=[R_tt, R_const], writes=[R_tab[q]])
                        units = []
                        for seg in range(NT):
                            for q in range(4):
                                units.append(dict(q=q, pair=cb * 4 + q, n=ucount[0], seg=seg, sg=cb * NT + seg))
                                ucount[0] += 1

                        def sA(u):
                            if u["q"] != 0:
                                return
                            si = u["sg"] % NU
                            proj_fm(0, w_u[cb], R_wu[cb], u["seg"])
                            fw.op("act", lambda h: h.copy(out=utf[si][:], in_=pb[0][:, :]), reads=[R_pb[0]], writes=[R_utf[si]])
                            fw.op("pool", lambda h: h.tensor_copy(out=utb[si][:], in_=utf[si][:]), reads=[R_utf[si]], writes=[R_utb[si]])

                        def st0(u):
                            k = u["n"] % 2
                            bR, bI = 1 + 2 * k, 2 + 2 * k
                            u["bR"], u["bI"] = bR, bI
                            pair = u["pair"]
                            si = u["sg"] % NU
                            fw.op("pe", lambda h: h.matmul(pb[bR][:, :], lhsT=BTr[:, pair, :], rhs=utb[si][:], start=True, stop=True), reads=[R_BT, R_utb[si]], writes=[R_pb[bR]])
                            fw.op("pe", lambda h: h.matmul(pb[bI][:, :], lhsT=BTi[:, pair, :], rhs=utb[si][:], start=True, stop=True), reads=[R_BT, R_utb[si]], writes=[R_pb[bI]])

                        def st1(u):
                            i = u["n"] % NR
                            q = u["q"]
                            bR, bI = u["bR"], u["bI"]
                            m = mm[i]
                            fw.op("dve", lambda h: h.tensor_tensor(out=m[0][:], in0=pb[bR][:, :], in1=tcos[q][:], op=ALU.mult), reads=[R_pb[bR], R_tab[q]], writes=[R_mm[i]])
                            fw.op("dve", lambda h: h.tensor_tensor(out=m[3][:], in0=pb[bR][:, :], in1=tsin[q][:], op=ALU.mult), reads=[R_pb[bR], R_tab[q]], writes=[R_mm[i]])
                            fw.op("dve", lambda h: h.tensor_tensor(out=m[1][:], in0=pb[bI][:, :], in1=tsin[q][:], op=ALU.mult), reads=[R_pb[bI], R_tab[q]], writes=[R_mm[i]])
                            fw.op("dve", lambda h: h.tensor_tensor(out=m[2][:], in0=pb[bI][:, :], in1=tcos[q][:], op=ALU.mult), reads=[R_pb[bI], R_tab[q]], writes=[R_mm[i]])

                        def st2(u):
                            i = u["n"] % NR
                            m = mm[i]
                            fw.op("pool", lambda h: h.tensor_tensor(out=bp[i][0][:], in0=m[0][:], in1=m[1][:], op=ALU.add), reads=[R_mm[i]], writes=[R_bp[i]])
                            fw.op("pool", lambda h: h.tensor_tensor(out=bp[i][1][:], in0=m[2][:], in1=m[3][:], op=ALU.subtract), reads=[R_mm[i]], writes=[R_bp[i]])

                        def st3(u):
                            i = u["n"] % NR
                            pair = u["pair"]
                            rb = rho[:, pair:pair + 1].broadcast_to([P, TT])
                            Rc = R_carry[pair]
                            fw.op("dve", lambda h: h.tensor_tensor_scan(out=yy[i][0][:], data0=rb, data1=bp[i][0][:], initial=carry[:, pair, 0:1], op0=ALU.mult, op1=ALU.add), reads=[R_bp[i], R_pp, Rc], writes=[R_yy[i]])
                            fw.op("dve", lambda h: h.tensor_tensor_scan(out=yy[i][1][:], data0=rb, data1=bp[i][1][:], initial=carry[:, pair, 1:2], op0=ALU.mult, op1=ALU.add), reads=[R_bp[i], R_pp, Rc], writes=[R_yy[i]])
                            yl_r = yy[i][0][:, TT - 1:TT]
                            yl_i = yy[i][1][:, TT - 1:TT]
                            fw.op("dve", lambda h: h.tensor_tensor(out=ctmp[:, 0:1], in0=yl_i, in1=s512[:, pair:pair + 1], op=ALU.mult), reads=[R_yy[i], R_pp], writes=[R_ctmp])
                            fw.op("dve", lambda h: h.tensor_tensor(out=ctmp[:, 1:2], in0=yl_r, in1=s512[:, pair:pair + 1], op=ALU.mult), reads=[R_yy[i], R_pp], writes=[R_ctmp])
                            fw.op("dve", lambda h: h.scalar_tensor_tensor(out=carry[:, pair, 0:1], in0=yl_r, scalar=c512[:, pair:pair + 1], in1=ctmp[:, 0:1], op0=ALU.mult, op1=ALU.subtract), reads=[R_yy[i], R_pp, R_ctmp], writes=[Rc])
                            fw.op("dve", lambda h: h.scalar_tensor_tensor(out=carry[:, pair, 1:2], in0=yl_i, scalar=c512[:, pair:pair + 1], in1=ctmp[:, 1:2], op0=ALU.mult, op1=ALU.add), reads=[R_yy[i], R_pp, R_ctmp], writes=[Rc])

                        def st4(u):
                            i = u["n"] % NR
                            i2 = u["n"] % 2
                            q = u["q"]
                            xi = u["sg"] % 2
                            n = m2[i2]
                            yr, yi = yy[i][0], yy[i][1]
                            fw.op("pool", lambda h: h.tensor_tensor(out=n[0][:], in0=yr[:], in1=tcos[q][:], op=ALU.mult), reads=[R_yy[i], R_tab[q]], writes=[R_m2[i2]])
                            fw.op("pool", lambda h: h.tensor_tensor(out=n[1][:], in0=yi[:], in1=tsin[q][:], op=ALU.mult), reads=[R_yy[i], R_tab[q]], writes=[R_m2[i2]])
                            fw.op("pool", lambda h: h.tensor_tensor(out=xre[xi][:, q, :], in0=n[0][:], in1=n[1][:], op=ALU.subtract), reads=[R_m2[i2]], writes=[R_xx[xi][q]])
                            fw.op("pool", lambda h: h.tensor_tensor(out=n[0][:], in0=yi[:], in1=tcos[q][:], op=ALU.mult), reads=[R_yy[i], R_tab[q]], writes=[R_m2[i2]])
                            fw.op("pool", lambda h: h.tensor_tensor(out=n[1][:], in0=yr[:], in1=tsin[q][:], op=ALU.mult), reads=[R_yy[i], R_tab[q]], writes=[R_m2[i2]])
                            fw.op("pool", lambda h: h.tensor_tensor(out=xim[xi][:, q, :], in0=n[0][:], in1=n[1][:], op=ALU.add), reads=[R_m2[i2]], writes=[R_xx[xi][q]])

                        def sG(u):
                            if u["q"] != 3:
                                return
                            seg = u["seg"]
                            si = u["sg"] % NU
                            xi = u["sg"] % 2
                            yi_ = u["sg"] % 2
                            for q in range(4):
                                lr_ = CTr[:, cb, q, :, :].rearrange("p a b -> p (a b)")
                                li_ = CTi[:, cb, q, :, :].rearrange("p a b -> p (a b)")
                                fw.op("pe", lambda h: h.matmul(pb[5][:, :], lhsT=lr_, rhs=xre[xi][:, q, :], start=(q == 0), stop=False), reads=[R_CT, R_xx[xi][q]], writes=[R_pb[5]])
                                fw.op("pe", lambda h: h.matmul(pb[5][:, :], lhsT=li_, rhs=xim[xi][:, q, :], start=False, stop=(q == 3)), reads=[R_CT, R_xx[xi][q]], writes=[R_pb[5]])
                            fw.op("dve", lambda h: h.scalar_tensor_tensor(out=yv[yi_][:], in0=utf[si][:], scalar=dsk[:, cb:cb + 1], in1=pb[5][:, :], op0=ALU.mult, op1=ALU.add),
                                  reads=[R_utf[si], R_pb[5], RL[7]], writes=[R_yv[yi_]])
                            dump("dbg_ssm_y", yv[yi_][:], R_yv[yi_], lambda o: o[cb * P:(cb + 1) * P, seg * TT:(seg + 1) * TT])
                            fw.op("pool", lambda h: h.tensor_tensor(out=g1[:], in0=yv[yi_][:], in1=yv[yi_][:], op=ALU.mult), reads=[R_yv[yi_]], writes=[R_g])
                            fw.op("pool", lambda h: h.tensor_scalar(out=g1[:], in0=g1[:], scalar1=0.044715, scalar2=1.0, op0=ALU.mult, op1=ALU.add), reads=[R_g], writes=[R_g])
                            fw.op("pool", lambda h: h.tensor_tensor(out=g1[:], in0=g1[:], in1=yv[yi_][:], op=ALU.mult), reads=[R_g, R_yv[yi_]], writes=[R_g])
                            fw.op("act", lambda h: h.activation(out=g2[:], in_=g1[:], func=AF.Sigmoid, scale=1.5957691216057308), reads=[R_g], writes=[R_g])
                            fw.op("pool", lambda h: h.tensor_tensor(out=ygb[yi_][:], in0=g2[:], in1=yv[yi_][:], op=ALU.mult), reads=[R_g, R_yv[yi_]], writes=[R_ygb[yi_]])
                            fw.dma(yg_d[:, cb, seg * TT:(seg + 1) * TT], ygb[yi_][:], reads=[R_ygb[yi_]], writes=[R_yg[seg]], semres=R_ygb[yi_])

                        run_pipeline(units, [sA, st0, st1, st2, st3, st4, sG])
                    fw.barrier()

            with ExitStack() as ps:
                w_sg = [sbt(ps, f"w_sg{i}", [P, DC, P], BF16) for i in range(4)]
                R_wsg = [Res(f"w_sg{i}") for i in range(4)]
                for cb in range(4):
                    load_wblock(l, 4 + cb, w_sg[cb][:], R_wsg[cb])
                wglu = sbt(ps, "wglu", [P, 4, 512], BF16)
                R_wglu = Res("wglu")
                for ob in range(4):
                    load_generic(w_glu_d[l, :, :, ob * P:(ob + 1) * P], wglu[:, :, ob * P:(ob + 1) * P], R_wglu, (4, P))
                bglu = sbt(ps, "bglu", [P, 4], F32)
                R_bglu = Res("bglu")
                fw.dma(bglu[:], bglu_d[l], writes=[R_bglu])
                ygt = [sbt(ps, f"ygt{i}", [P, 4, TT], BF16) for i in range(2)]
                R_ygt = [Res(f"ygt{i}") for i in range(2)]
                s1 = [sbt(ps, f"s1_{i}", [P, TT], F32) for i in range(2)]
                s2 = [sbt(ps, f"s2_{i}", [P, TT], F32) for i in range(2)]
                R_s = [Res(f"s12_{i}") for i in range(2)]
                osb = [sbt(ps, f"osb{i}", [P, 4, TT], BF16) for i in range(2)]
                R_osb = [Res(f"osb{i}") for i in range(2)]
                cnt = 0
                for j in range(NT):
                    yi_ = j % 2
                    fw.dma(ygt[yi_][:], yg_d[:, :, j * TT:(j + 1) * TT], reads=[R_yg[j]], writes=[R_ygt[yi_]])
                    for ob in range(4):
                        k = cnt % 2
                        cnt += 1
                        bz, bg = 6 + k, 0 + k
                        for c in range(4):
                            fw.op("pe", lambda h, c=c, ob=ob, bz=bz: h.matmul(pb[bz][:, :], lhsT=wglu[:, c, ob * P:(ob + 1) * P], rhs=ygt[yi_][:, c, :], start=(c == 0), stop=(c == 3)),
                                  reads=[R_wglu, R_ygt[yi_]], writes=[R_pb[bz]])
                        proj_fm(bg, w_sg[ob], R_wsg[ob], j)
                        fw.op("act", lambda h, k=k, ob=ob, bz=bz: h.activation(out=s1[k][:], in_=pb[bz][:, :], func=AF.Sigmoid, bias=bglu[:, ob:ob + 1]), reads=[R_pb[bz], R_bglu], writes=[R_s[k]])
                        fw.op("act", lambda h, k=k, bg=bg: h.activation(out=s2[k][:], in_=pb[bg][:, :], func=AF.Silu), reads=[R_pb[bg]], writes=[R_s[k]])
                        fw.op("dve", lambda h, k=k, ob=ob: h.tensor_tensor(out=s1[k][:], in0=s1[k][:], in1=ygt[yi_][:, ob, :], op=ALU.mult), reads=[R_s[k], R_ygt[yi_]], writes=[R_s[k]])
                        fw.op("dve", lambda h, k=k, ob=ob: h.tensor_tensor(out=osb[yi_][:, ob, :], in0=s1[k][:], in1=s2[k][:], op=ALU.mult), reads=[R_s[k]], writes=[R_osb[yi_]])
                    fw.dma(obr_d[0, :, :, j * TT:(j + 1) * TT], osb[yi_][:], reads=[R_osb[yi_]], writes=[R_obr[0][j]], semres=R_osb[yi_])
                    if "dbg_o0" in dbg_out:
                        pass
                fw.barrier()

        def sb_phase(l):
            with ExitStack() as ps:
                wq = sbt(ps, "sb_wq", [P, DC, P], BF16)
                wk = sbt(ps, "sb_wk", [P, DC, P], BF16)
                wv = sbt(ps, "sb_wv", [P, DC, P], BF16)
                wg = sbt(ps, "sb_wg", [P, DC, P], BF16)
                R_wq, R_wk, R_wv, R_wg = Res("sb_wq"), Res("sb_wk"), Res("sb_wv"), Res("sb_wg")
                qT = sbt(ps, "sb_qT", [P, S], BF16)
                kT = sbt(ps, "sb_kT", [P, S], BF16)
                V = sbt(ps, "sb_V", [P, NB, P], BF16)
                R_q = [Res(f"sb_q{j}") for j in range(NT)]
                R_k = [Res(f"sb_k{j}") for j in range(NT)]
                R_v = [Res(f"sb_v{j}") for j in range(NT)]
                NR = 3
                e1 = [sbt(ps, f"sb_e1_{i}", [P, TT], F32) for i in range(NR)]
                sp = [sbt(ps, f"sb_sp_{i}", [P, TT], BF16) for i in range(NR)]
                ww = [sbt(ps, f"sb_w_{i}", [P, TT], BF16) for i in range(NR)]
                R_e1 = [Res(f"sb_e1_{i}") for i in range(NR)]
                R_sp = [Res(f"sb_sp_{i}") for i in range(NR)]
                R_ww = [Res(f"sb_w_{i}") for i in range(NR)]
                NA = 3
                acc = [sbt(ps, f"sb_acc{i}", [P, TT], BF16) for i in range(NA)]
                R_acc = [Res(f"sb_acc{i}") for i in range(NA)]
                sg = [sbt(ps, f"sb_sg{i}", [P, TT], F32) for i in range(2)]
                R_sg = [Res(f"sb_sg{i}") for i in range(2)]
                og = [sbt(ps, f"sb_og{i}", [P, TT], BF16) for i in range(2)]
                R_og = [Res(f"sb_og{i}") for i in range(2)]
                tcount = [0]
                gcount = [0]
                for hp in range(4):
                    load_wblock(l, 8 + hp, wq[:], R_wq)
                    load_wblock(l, 12 + hp, wk[:], R_wk)
                    load_wblock(l, 16 + hp, wv[:], R_wv)
                    load_wblock(l, 20 + hp, wg[:], R_wg)
                    for j in range(NT):
                        proj_fm(6, wq, R_wq, j)
                        fw.op("act", lambda h, j=j: h.activation(out=qT[:, j * TT:(j + 1) * TT], in_=pb[6][:, :], func=AF.Copy, scale=0.125), reads=[R_pb[6]], writes=[R_q[j]])
                        proj_fm(7, wk, R_wk, j)
                        fw.op("dve", lambda h, j=j: h.tensor_copy(out=kT[:, j * TT:(j + 1) * TT], in_=pb[7][:, :]), reads=[R_pb[7]], writes=[R_k[j]])
                    for j in range(NT):
                        bank = 6 + (j % 2)
                        for b4 in range(4):
                            blk = j * 4 + b4
                            for c in range(DC):
                                fw.op("pe", lambda h, c=c, blk=blk, b4=b4, bank=bank: h.matmul(pb[bank][:, b4 * P:(b4 + 1) * P], lhsT=hT[:, c, blk * P:(blk + 1) * P], rhs=wv[:, c, :], start=(c == 0), stop=(c == DC - 1)),
                                      reads=[R_wv, R_hT[j]], writes=[R_pb[bank]])
                        src = pb[bank][:, :].rearrange("p (b n) -> p b n", b=4)
                        if j % 2 == 0:
                            fw.op("act", lambda h, j=j, src=src: h.copy(out=V[:, j * 4:(j + 1) * 4, :], in_=src), reads=[R_pb[bank]], writes=[R_v[j]])
                        else:
                            fw.op("dve", lambda h, j=j, src=src: h.tensor_copy(out=V[:, j * 4:(j + 1) * 4, :], in_=src), reads=[R_pb[bank]], writes=[R_v[j]])
                    units = []
                    for qt in range(NT):
                        for e in range(2):
                            ai = gcount[0] % NA
                            gcount[0] += 1
                            kbs = list(range(4 * qt + 3, -1, -1))
                            for idx, kb in enumerate(kbs):
                                jd = kb - 4 * qt
                                u = dict(qt=qt, e=e, kb=kb, col0=(P * jd if jd >= 0 else 0), diag=(jd >= 0), first=(idx == 0), last=(kb == 0),
                                         ai=ai, n=tcount[0], ob=4 + (qt % 2))
                                tcount[0] += 1
                                units.append(u)

                    def s0(u):
                        k = u["n"] % 2
                        u["bA"] = 0 + k
                        u["bB"] = 2 + k
                        bA, e, kb, qt, c0 = u["bA"], u["e"], u["kb"], u["qt"], u["col0"]
                        pr = slice(64 * e, 64 * e + 64)
                        t0 = qt * TT
                        if u["first"]:
                            fw.op("pool", lambda h: h.memset(acc[u["ai"]][:], 0.0), writes=[R_acc[u["ai"]]])
                        fw.op("pe", lambda h: h.matmul(pb[bA][:, c0:TT], lhsT=kT[pr, kb * P:(kb + 1) * P], rhs=qT[pr, t0 + c0:t0 + TT], start=True, stop=not u["diag"]),
                              reads=[R_k[kb // 4], R_q[qt]], writes=[R_pb[bA]])
                        if u["diag"]:
                            fw.op("pe", lambda h: h.matmul(pb[bA][:, c0:c0 + P], lhsT=ident_b[:], rhs=mask_sb[:], start=False, stop=True), reads=[R_const], writes=[R_pb[bA]])

                    def s1(u):
                        i = u["n"] % NR
                        bA, c0 = u["bA"], u["col0"]
                        fw.op("act", lambda h: h.activation(out=e1[i][:, c0:TT], in_=pb[bA][:, c0:TT], func=AF.Exp), reads=[R_pb[bA]], writes=[R_e1[i]])
                        fw.op("act", lambda h: h.activation(out=sp[i][:, c0:TT], in_=e1[i][:, c0:TT], func=AF.Ln, bias=1.0), reads=[R_e1[i]], writes=[R_sp[i]])

                    def s2(u):
                        i = u["n"] % NR
                        bB, e, kb, qt, c0 = u["bB"], u["e"], u["kb"], u["qt"], u["col0"]
                        pr = slice(64 * e, 64 * e + 64)
                        t0 = qt * TT
                        fw.op("pe", lambda h: h.matmul(pb[bB][:, c0:TT], lhsT=kT[pr, kb * P:(kb + 1) * P], rhs=qT[pr, t0 + c0:t0 + TT], start=True, stop=False),
                              reads=[R_k[kb // 4], R_q[qt]], writes=[R_pb[bB]])
                        lastmm = u["first"] and not u["diag"]
                        fw.op("pe", lambda h: h.matmul(pb[bB][:, c0:TT], lhsT=ntri_b[:], rhs=sp[i][:, c0:TT], start=False, stop=lastmm), reads=[R_const, R_sp[i]], writes=[R_pb[bB]])
                        if not u["first"]:
                            fw.op("pe", lambda h: h.matmul(pb[bB][:, c0:TT], lhsT=nones_b[:], rhs=acc[u["ai"]][:, c0:TT], start=False, stop=not u["diag"]),
                                  reads=[R_const, R_acc[u["ai"]]], writes=[R_pb[bB]])
                        if u["diag"]:
                            fw.op("pe", lambda h: h.matmul(pb[bB][:, c0:c0 + P], lhsT=ident_b[:], rhs=mask_sb[:], start=False, stop=True), reads=[R_const], writes=[R_pb[bB]])

                    def s3(u):
                        i = u["n"] % NR
                        bB, c0 = u["bB"], u["col0"]
                        fw.op("act", lambda h: h.activation(out=ww[i][:, c0:TT], in_=pb[bB][:, c0:TT], func=AF.Exp), reads=[R_pb[bB]], writes=[R_ww[i]])
                        if not u["last"]:
                            a = acc[u["ai"]]
                            fw.op("pool", lambda h: h.tensor_tensor(out=a[:, c0:TT], in0=a[:, c0:TT], in1=sp[i][:, c0:TT], op=ALU.add), reads=[R_sp[i], R_acc[u["ai"]]], writes=[R_acc[u["ai"]]])

                    def s4(u):
                        i = u["n"] % NR
                        e, kb, qt, c0, ob = u["e"], u["kb"], u["qt"], u["col0"], u["ob"]
                        pr = slice(64 * e, 64 * e + 64)
                        fw.op("pe", lambda h: h.matmul(pb[ob][pr, c0:TT], lhsT=V[:, kb, 64 * e:64 * e + 64], rhs=ww[i][:, c0:TT], start=u["first"], stop=u["last"]),
                              reads=[R_v[kb // 4], R_ww[i]], writes=[R_pb[ob]])
                        if u["last"] and e == 1:
                            k = qt % 2
                            proj_fm(6 + k, wg, R_wg, qt)
                            fw.op("act", lambda h: h.activation(out=sg[k][:], in_=pb[6 + k][:, :], func=AF.Silu), reads=[R_pb[6 + k]], writes=[R_sg[k]])
                            fw.op("dve", lambda h: h.tensor_tensor(out=og[k][:], in0=pb[ob][:, :], in1=sg[k][:], op=ALU.mult), reads=[R_pb[ob], R_sg[k]], writes=[R_og[k]])
                            fw.dma(obr_d[1, :, hp, qt * TT:(qt + 1) * TT], og[k][:], reads=[R_og[k]], writes=[R_obr[1][qt]], semres=R_og[k])

                    run_pipeline(units, [s0, s1, s2, s3, s4])
                fw.barrier()

        def diff_phase(l):
            lam_init = 0.8 - 0.6 * math.exp(-0.3 * l)
            with ExitStack() as ps:
                names = ["wq", "wqr", "wk", "wkr", "wv", "wg"]
                wt = {n: sbt(ps, "df_" + n, [P, DC, P], BF16) for n in names}
                R_w = {n: Res("df_" + n) for n in names}
                qT = sbt(ps, "df_qT", [P, S], BF16)
                kT = sbt(ps, "df_kT", [P, S], BF16)
                V = sbt(ps, "df_V", [P, NB, P], BF16)
                R_q = [Res(f"df_q{j}") for j in range(NT)]
                R_k = [Res(f"df_k{j}") for j in range(NT)]
                R_v = [Res(f"df_v{j}") for j in range(NT)]
                rc = [sbt(ps, f"df_rc{i}", [P, TT], F32) for i in range(2)]
                rs = [sbt(ps, f"df_rs{i}", [P, TT], F32) for i in range(2)]
                R_rc = [Res(f"df_rc{i}") for i in range(2)]
                R_rs = [Res(f"df_rs{i}") for i in range(2)]
                ta = [sbt(ps, f"df_ta{i}", [P, TT], F32) for i in range(2)]
                tb = [sbt(ps, f"df_tb{i}", [P, TT], F32) for i in range(2)]
                R_ta = [Res(f"df_ta{i}") for i in range(2)]
                NR = 3
                pp = [sbt(ps, f"df_p{i}", [P, TT], BF16) for i in range(NR)]
                R_pp_ = [Res(f"df_p{i}") for i in range(NR)]
                rr = sbt(ps, "df_rr", [P, TT], F32)
                R_rr = Res("df_rr")
                o12 = [[sbt(ps, f"df_o{i}_{k}", [P, TT], F32) for i in range(2)] for k in range(2)]
                R_o12 = [[Res(f"df_o{i}_{k}") for i in range(2)] for k in range(2)]
                oo = sbt(ps, "df_oo", [P, TT], F32)
                osq = sbt(ps, "df_osq", [P, TT], F32)
                rstd = sbt(ps, "df_rstd", [P, TT], F32)
                sg = sbt(ps, "df_sg", [P, TT], F32)
                R_fin = Res("df_fin")
                R_sgd = Res("df_sg")
                ofin = [sbt(ps, f"df_ofin{i}", [P, TT], BF16) for i in range(2)]
                R_ofin = [Res(f"df_ofin{i}") for i in range(2)]
                lqk = sbt(ps, "df_lqk", [P, 4, 64], F32)
                R_lqk = Res("df_lqk")
                fw.dma(lqk[:], lqk_d[l], writes=[R_lqk])
                sub = sbt(ps, "df_sub", [P, 1], F32)
                R_sub = Res("df_sub")
                fw.dma(sub[:], subln_d[l], writes=[R_sub])
                pr1 = sbt(ps, "df_pr1", [P, 64], F32)
                sm = sbt(ps, "df_sm", [P, 4], F32)
                R_lam = Res("df_lam")
                fw.op("dve", lambda h: h.tensor_tensor(out=pr1[:], in0=lqk[:, 0, :], in1=lqk[:, 1, :], op=ALU.mult), reads=[R_lqk], writes=[R_lam])
                fw.op("dve", lambda h: h.reduce_sum(out=sm[:, 0:1], in_=pr1[:], axis=AX.X), reads=[R_lam], writes=[R_lam])
                fw.op("dve", lambda h: h.tensor_tensor(out=pr1[:], in0=lqk[:, 2, :], in1=lqk[:, 3, :], op=ALU.mult), reads=[R_lqk, R_lam], writes=[R_lam])
                fw.op("dve", lambda h: h.reduce_sum(out=sm[:, 1:2], in_=pr1[:], axis=AX.X), reads=[R_lam], writes=[R_lam])
                fw.op("act", lambda h: h.activation(out=sm[:, 0:2], in_=sm[:, 0:2], func=AF.Exp), reads=[R_lam], writes=[R_lam])
                fw.op("dve", lambda h: h.tensor_tensor(out=sm[:, 2:3], in0=sm[:, 1:2], in1=sm[:, 0:1], op=ALU.subtract), reads=[R_lam], writes=[R_lam])
                fw.op("dve", lambda h: h.tensor_scalar(out=sm[:, 2:3], in0=sm[:, 2:3], scalar1=-lam_init, scalar2=None, op0=ALU.add), reads=[R_lam], writes=[R_lam])
                fw.op("dve", lambda h: h.tensor_scalar(out=sub[:], in0=sub[:], scalar1=(1.0 - lam_init), scalar2=None, op0=ALU.mult), reads=[R_sub], writes=[R_sub])
                nlam = sm[:, 2:3]
                tcount = [0]
                gcount = [0]
                for hd in range(4):
                    load_wblock(l, 24 + hd, wt["wq"][:], R_w["wq"])
                    load_wblock(l, 24 + hd, wt["wqr"][:], R_w["wqr"], rot=True)
                    load_wblock(l, 28 + hd, wt["wk"][:], R_w["wk"])
                    load_wblock(l, 28 + hd, wt["wkr"][:], R_w["wkr"], rot=True)
                    load_wblock(l, 32 + hd, wt["wv"][:], R_w["wv"])
                    load_wblock(l, 36 + hd, wt["wg"][:], R_w["wg"])
                    for j in range(NT):
                        i = j % 2
                        fw.dma(rc[i][:], ropeC_d[:, j * TT:(j + 1) * TT], reads=[R_ropeC[j]], writes=[R_rc[i]])
                        fw.dma(rs[i][:], ropeS_d[:, j * TT:(j + 1) * TT], reads=[R_ropeS[j]], writes=[R_rs[i]])
                        for which, (wn, wr, dstT, R_dst, scl) in enumerate((("wq", "wqr", qT, R_q, 0.125), ("wk", "wkr", kT, R_k, 1.0))):
                            k = which
                            proj_fm(6, wt[wn], R_w[wn], j)
                            proj_fm(7, wt[wr], R_w[wr], j)
                            fw.op("dve", lambda h, k=k, scl=scl: h.scalar_tensor_tensor(out=ta[k][:], in0=pb[6][:, :], scalar=scl, in1=rc[i][:], op0=ALU.mult, op1=ALU.mult), reads=[R_pb[6], R_rc[i]], writes=[R_ta[k]])
                            fw.op("dve", lambda h, k=k, scl=scl: h.scalar_tensor_tensor(out=tb[k][:], in0=pb[7][:, :], scalar=scl, in1=rs[i][:], op0=ALU.mult, op1=ALU.mult), reads=[R_pb[7], R_rs[i]], writes=[R_ta[k]])
                            fw.op("pool", lambda h, k=k, dstT=dstT, j=j: h.tensor_tensor(out=dstT[:, j * TT:(j + 1) * TT], in0=ta[k][:], in1=tb[k][:], op=ALU.add), reads=[R_ta[k]], writes=[R_dst[j]])
                    for j in range(NT):
                        bank = 6 + (j % 2)
                        for b4 in range(4):
                            blk = j * 4 + b4
                            for c in range(DC):
                                fw.op("pe", lambda h, c=c, blk=blk, b4=b4, bank=bank: h.matmul(pb[bank][:, b4 * P:(b4 + 1) * P], lhsT=hT[:, c, blk * P:(blk + 1) * P], rhs=wt["wv"][:, c, :], start=(c == 0), stop=(c == DC - 1)),
                                      reads=[R_w["wv"], R_hT[j]], writes=[R_pb[bank]])
                        src = pb[bank][:, :].rearrange("p (b n) -> p b n", b=4)
                        if j % 2 == 0:
                            fw.op("act", lambda h, j=j, src=src: h.copy(out=V[:, j * 4:(j + 1) * 4, :], in_=src), reads=[R_pb[bank]], writes=[R_v[j]])
                        else:
                            fw.op("dve", lambda h, j=j, src=src: h.tensor_copy(out=V[:, j * 4:(j + 1) * 4, :], in_=src), reads=[R_pb[bank]], writes=[R_v[j]])
                    if "dbg_dfq" in dbg_out and hd == 0:
                        pass
                    units = []
                    for qt in range(NT):
                        for i in range(2):
                            gi = gcount[0] % 2
                            gcount[0] += 1
                            nkb = 4 * qt + 4
                            for kb in range(nkb):
                                jd = kb - 4 * qt
                                units.append(dict(qt=qt, i=i, kb=kb, col0=(P * jd if jd >= 0 else 0), diag=(jd >= 0), first=(kb == 0), last=(kb == nkb - 1),
                                                  n=tcount[0], bN=2 + 2 * gi, bD=3 + 2 * gi, gi=gi))
                                tcount[0] += 1

                    def s0(u):
                        k = u["n"] % 2
                        u["bA"] = k
                        bA, i, kb, qt, c0 = u["bA"], u["i"], u["kb"], u["qt"], u["col0"]
                        pr = slice(64 * i, 64 * i + 64)
                        t0 = qt * TT
                        fw.op("pe", lambda h: h.matmul(pb[bA][:, c0:TT], lhsT=kT[pr, kb * P:(kb + 1) * P], rhs=qT[pr, t0 + c0:t0 + TT], start=True, stop=not u["diag"]),
                              reads=[R_k[kb // 4], R_q[qt]], writes=[R_pb[bA]])
                        if u["diag"]:
                            fw.op("pe", lambda h: h.matmul(pb[bA][:, c0:c0 + P], lhsT=ident_b[:], rhs=mask_df[:], start=False, stop=True), reads=[R_const], writes=[R_pb[bA]])

                    def s1(u):
                        ii = u["n"] % NR
                        bA, c0 = u["bA"], u["col0"]
                        fw.op("act", lambda h: h.activation(out=pp[ii][:, c0:TT], in_=pb[bA][:, c0:TT], func=AF.Exp), reads=[R_pb[bA]], writes=[R_pp_[ii]])

                    def s2(u):
                        ii = u["n"] % NR
                        kb, qt, c0, bN, bD, i = u["kb"], u["qt"], u["col0"], u["bN"], u["bD"], u["i"]
                        fw.op("pe", lambda h: h.matmul(pb[bN][:, c0:TT], lhsT=V[:, kb, :], rhs=pp[ii][:, c0:TT], start=u["first"], stop=u["last"]), reads=[R_v[kb // 4], R_pp_[ii]], writes=[R_pb[bN]])
                        fw.op("pe", lambda h: h.matmul(pb[bD][:, c0:TT], lhsT=ones_b[:], rhs=pp[ii][:, c0:TT], start=u["first"], stop=u["last"]), reads=[R_const, R_pp_[ii]], writes=[R_pb[bD]])
                        if u["last"]:
                            k2 = qt % 2
                            o_i = o12[k2][i]
                            fw.op("dve", lambda h: h.reciprocal(out=rr[:], in_=pb[bD][:, :]), reads=[R_pb[bD]], writes=[R_rr])
                            fw.op("dve", lambda h: h.tensor_tensor(out=o_i[:], in0=pb[bN][:, :], in1=rr[:], op=ALU.mult), reads=[R_pb[bN], R_rr], writes=[R_o12[k2][i]])
                            if i == 1:
                                o1, o2 = o12[k2][0], o12[k2][1]
                                fw.op("dve", lambda h: h.scalar_tensor_tensor(out=oo[:], in0=o2[:], scalar=nlam, in1=o1[:], op0=ALU.mult, op1=ALU.add), reads=[R_o12[k2][0], R_o12[k2][1], R_lam], writes=[R_fin])
                                fw.op("act", lambda h: h.activation(out=osq[:], in_=oo[:], func=AF.Square), reads=[R_fin], writes=[R_fin])
                                fw.op("pe", lambda h: h.matmul(pb[6][:, :], lhsT=ones_f[:], rhs=osq[:], start=True, stop=True), reads=[R_fin, R_const], writes=[R_pb[6]])
                                rsqrt_from_psum(6, 1.0 / P, rstd[:], R_fin, rstd[:])
                                proj_fm(7, wt["wg"], R_w["wg"], qt)
                                fw.op("act", lambda h: h.activation(out=sg[:], in_=pb[7][:, :], func=AF.Silu), reads=[R_pb[7]], writes=[R_sgd])
                                fw.op("dve", lambda h: h.tensor_tensor(out=oo[:], in0=oo[:], in1=rstd[:], op=ALU.mult), reads=[R_fin], writes=[R_fin])
                                fw.op("dve", lambda h: h.scalar_tensor_tensor(out=ofin[k2][:], in0=oo[:], scalar=sub[:, 0:1], in1=sg[:], op0=ALU.mult, op1=ALU.mult), reads=[R_fin, R_sub, R_sgd], writes=[R_ofin[k2]])
                                fw.dma(obr_d[2, :, hd, qt * TT:(qt + 1) * TT], ofin[k2][:], reads=[R_ofin[k2]], writes=[R_obr[2][qt]], semres=R_ofin[k2])

                    run_pipeline(units, [s0, s1, s2])
                fw.barrier()

        def merge_phase(l, last):
            with ExitStack() as ps:
                wm = sbt(ps, "mg_wm", [P, DC, 3 * D], BF16)
                R_wm = [Res(f"mg_wm{i}") for i in range(24)]
                wbr = sbt(ps, "mg_wbr", [P, 3, 4, D], BF16)
                R_wbr = Res("mg_wbr")
                bm = sbt(ps, "mg_bm", [P, 24], F32)
                R_bm = Res("mg_bm")
                fw.dma(bm[:], bmerge_d[l], writes=[R_bm])
                for cb in range(24):
                    load_wblock(l, 40 + cb, wm[:, :, cb * P:(cb + 1) * P], R_wm[cb], eng=("pool" if cb % 2 == 0 else "dve"))
                for n in range(3):
                    for mb in range(4):
                        load_generic(w_br_d[l, n, :, :, mb * 256:(mb + 1) * 256], wbr[:, n, :, mb * 256:(mb + 1) * 256], R_wbr, (4, 256), eng=("pool" if mb % 2 == 0 else "dve"))
                ot = [[sbt(ps, f"mg_ot{n}_{i}", [P, 4, TT], BF16) for n in range(3)] for i in range(2)]
                R_ot = [[Res(f"mg_ot{n}_{i}") for n in range(3)] for i in range(2)]
                mg = [sbt(ps, f"mg_mg{i}", [P, DC, TT], BF16) for i in range(2)]
                R_mgt = [Res(f"mg_mg{i}") for i in range(2)]
                gt = [sbt(ps, f"mg_gt{i}", [P, TT], F32) for i in range(3)]
                R_gt = [Res(f"mg_gt{i}") for i in range(3)]
                macc = [sbt(ps, f"mg_macc{i}", [P, TT], F32) for i in range(2)]
                R_macc = [Res(f"mg_macc{i}") for i in range(2)]
                pcount = 0
                for j in range(NT):
                    ji = j % 2
                    for n in range(3):
                        fw.dma(ot[ji][n][:], obr_d[n, :, :, j * TT:(j + 1) * TT], reads=[R_obr[n][j]], writes=[R_ot[ji][n]])
                    for db in range(DC):
                        ma = macc[db % 2]
                        R_ma = R_macc[db % 2]
                        for n in range(3):
                            k = pcount % 2
                            pcount += 1
                            bP, bL = 0 + k, 2 + k
                            for c in range(4):
                                fw.op("pe", lambda h: h.matmul(pb[bP][:, :], lhsT=wbr[:, n, c, db * P:(db + 1) * P], rhs=ot[ji][n][:, c, :], start=(c == 0), stop=(c == 3)),
                                      reads=[R_wbr, R_ot[ji][n]], writes=[R_pb[bP]])
                            cbm = n * 8 + db
                            for c in range(DC):
                                fw.op("pe", lambda h: h.matmul(pb[bL][:, :], lhsT=wm[:, c, cbm * P:(cbm + 1) * P], rhs=hT[:, c, j * TT:(j + 1) * TT], start=(c == 0), stop=(c == DC - 1)),
                                      reads=[R_wm[cbm], R_hT[j]], writes=[R_pb[bL]])
                            fw.op("act", lambda h: h.activation(out=gt[n][:], in_=pb[bL][:, :], func=AF.Sigmoid, bias=bm[:, cbm:cbm + 1]), reads=[R_pb[bL], R_bm], writes=[R_gt[n]])
                            if n == 0:
                                fw.op("dve", lambda h: h.tensor_tensor(out=ma[:], in0=pb[bP][:, :], in1=gt[n][:], op=ALU.mult), reads=[R_pb[bP], R_gt[n]], writes=[R_ma])
                            else:
                                fw.op("dve", lambda h: h.tensor_tensor(out=gt[n][:], in0=pb[bP][:, :], in1=gt[n][:], op=ALU.mult), reads=[R_pb[bP], R_gt[n]], writes=[R_gt[n]])
                                if n == 1:
                                    fw.op("pool", lambda h: h.tensor_tensor(out=ma[:], in0=ma[:], in1=gt[n][:], op=ALU.add), reads=[R_gt[n], R_ma], writes=[R_ma])
                                else:
                                    fw.op("pool", lambda h: h.tensor_tensor(out=mg[ji][:, db, :], in0=ma[:], in1=gt[n][:], op=ALU.add), reads=[R_gt[n], R_ma], writes=[R_mgt[ji]])
                    fw.dma(mg_d[:, :, j * TT:(j + 1) * TT], mg[ji][:], reads=[R_mgt[ji]], writes=[R_mgd[j]], semres=R_mgt[ji])
                fw.barrier()
            with ExitStack() as ps:
                wo = sbt(ps, "mg_wo", [P, DC, D], BF16)
                R_wo = Res("mg_wo")
                for mb in range(8):
                    load_generic(w_out_d[l, :, :, mb * P:(mb + 1) * P], wo[:, :, mb * P:(mb + 1) * P], R_wo, (DC, P), eng=("pool" if mb % 2 == 0 else "dve"))
                fnw = sbt(ps, "mg_fnw", [P, DC], F32)
                R_fnw = Res("mg_fnw")
                if last:
                    fw.dma(fnw[:], fnorm_d[:, :], writes=[R_fnw])
                xt = [sbt(ps, f"mg_xt{i}", [P, DC, TT], F32) for i in range(2)]
                R_xt = [Res(f"mg_xt{i}") for i in range(2)]
                mgi = [sbt(ps, f"mg_mgi{i}", [P, DC, TT], BF16) for i in range(2)]
                R_mgi = [Res(f"mg_mgi{i}") for i in range(2)]
                sq = [sbt(ps, f"mg_sq{i}", [P, TT], F32) for i in range(2)]
                R_sq = [Res(f"mg_sq{i}") for i in range(2)]
                rstd = sbt(ps, "mg_rstd", [P, TT], F32)
                R_rstd = Res("mg_rstd")
                ostage = [sbt(ps, f"mg_ost{i}", [P, D], F32) for i in range(2)] if last else None
                R_ost = [Res(f"mg_ost{i}") for i in range(2)]
                for j in range(NT):
                    ji = j % 2
                    x_, R_x = xt[ji], R_xt[ji]
                    fw.dma(mgi[ji][:], mg_d[:, :, j * TT:(j + 1) * TT], reads=[R_mgd[j]], writes=[R_mgi[ji]])
                    fw.dma(x_[:], xT_d[:, :, j * TT:(j + 1) * TT], reads=[R_xT[j]], writes=[R_x])
                    for ob in range(DC):
                        bO = 4 + (ob % 2)
                        for c in range(DC):
                            fw.op("pe", lambda h: h.matmul(pb[bO][:, :], lhsT=wo[:, c, ob * P:(ob + 1) * P], rhs=mgi[ji][:, c, :], start=(c == 0), stop=(c == DC - 1)),
                                  reads=[R_wo, R_mgi[ji]], writes=[R_pb[bO]])
                        fw.op("dve", lambda h: h.tensor_tensor(out=x_[:, ob, :], in0=x_[:, ob, :], in1=pb[bO][:, :], op=ALU.add), reads=[R_pb[bO], R_x], writes=[R_x])
                    dump("dbg_x1", x_[:], R_x, lambda o: o[:, :, j * TT:(j + 1) * TT])
                    norm_tile(x_, R_x, j, sq, R_sq, rstd, R_rstd, 6)
                    if not last:
                        fw.dma(xT_d[:, :, j * TT:(j + 1) * TT], x_[:], reads=[R_x], writes=[R_xT[j]], semres=R_x)
                        fw.op("dve", lambda h: h.tensor_tensor(out=hT[:, :, j * TT:(j + 1) * TT], in0=x_[:], in1=rstd[:].unsqueeze(1).broadcast_to([P, DC, TT]), op=ALU.mult),
                              reads=[R_x, R_rstd], writes=[R_hT[j]])
                    else:
                        for c in range(DC):
                            fw.op("dve", lambda h: h.scalar_tensor_tensor(out=x_[:, c, :], in0=x_[:, c, :], scalar=fnw[:, c:c + 1], in1=rstd[:], op0=ALU.mult, op1=ALU.mult), reads=[R_x, R_rstd, R_fnw], writes=[R_x])
                        for blk in range(4):
                            oi = blk % 2
                            for half in range(2):
                                bank = 0 + 2 * (blk % 2) + half
                                for cc in range(4):
                                    c = half * 4 + cc
                                    fw.op("pe", lambda h: h.transpose(pb[bank][:, cc * P:(cc + 1) * P], x_[:, c, blk * P:(blk + 1) * P], ident_f[:]),
                                          reads=[R_x, R_const], writes=[R_pb[bank]])
                                if half == 0:
                                    fw.op("act", lambda h: h.copy(out=ostage[oi][:, 0:512], in_=pb[bank][:, :]), reads=[R_pb[bank]], writes=[R_ost[oi]])
                                else:
                                    fw.op("dve", lambda h: h.tensor_copy(out=ostage[oi][:, 512:1024], in_=pb[bank][:, :]), reads=[R_pb[bank]], writes=[R_ost[oi]])
                            g = j * 4 + blk
                            fw.dma(y_out[g * P:(g + 1) * P, :], ostage[oi][:], reads=[R_ost[oi]], final=True, semres=R_ost[oi])
                fw.barrier()

        consts()
        phase0()
        for l in range(nlayers):
            if l > 0:
                fw.dma(nw[:], normw_d[l], writes=[R_nw], persistent=True)
            ssm_phase(l)
            if stop_after == "ssm":
                break
            sb_phase(l)
            if stop_after == "sb":
                break
            diff_phase(l)
            if stop_after == "diff":
                break
            merge_phase(l, last=(l == nlayers - 1))
        fw.finish()
        build_program.ninst = fw.ninst
        build_program.nsem = fw.nsem
    return nc


def prep_inputs(x, norm_w, w_in, b_merge, ssm_a_re, ssm_a_im, ssm_log_dt, ssm_b_re, ssm_b_im, ssm_c_re, ssm_c_im, ssm_d,
                ssm_w_glu, ssm_b_glu, diff_lq1, diff_lk1, diff_lq2, diff_lk2, diff_subln_w, w_branch, w_out, final_norm_w):
    f = lambda a: np.ascontiguousarray(np.asarray(a, dtype=np.float32))
    L = DEPTH
    w_in = np.asarray(w_in, dtype=np.float32)
    shared = {}
    shared["w_in"] = f(w_in.reshape(L, DC, P, 64, P).transpose(0, 3, 2, 1, 4))
    shared["w_glu"] = f(np.asarray(ssm_w_glu, np.float32).reshape(L, 4, P, 512).transpose(0, 2, 1, 3))
    shared["w_br"] = f(np.asarray(w_branch, np.float32).reshape(L, 3, 4, P, D).transpose(0, 1, 3, 2, 4))
    shared["w_out"] = f(np.asarray(w_out, np.float32).reshape(L, DC, P, D).transpose(0, 2, 1, 3))
    shared["norm_w"] = f(np.asarray(norm_w, np.float32).reshape(L, DC, P).transpose(0, 2, 1))
    shared["b_merge"] = f(np.asarray(b_merge, np.float32).reshape(L, 24, P).transpose(0, 2, 1))
    shared["b_glu"] = f(np.asarray(ssm_b_glu, np.float32).reshape(L, 4, P).transpose(0, 2, 1))
    shared["subln_w"] = f(np.asarray(diff_subln_w, np.float32).reshape(L, P, 1))
    shared["fnorm_w"] = f(np.asarray(final_norm_w, np.float32).reshape(DC, P).T)
    def gp(a):
        return f(np.asarray(a, np.float32).reshape(L, 16, 2, 64).transpose(0, 2, 3, 1).reshape(L, P, 16))
    shared["a_re"] = gp(ssm_a_re)
    shared["a_im"] = gp(ssm_a_im)
    shared["log_dt"] = gp(np.broadcast_to(np.asarray(ssm_log_dt, np.float32)[:, :, None], (L, 32, 64)))
    def gpc(a):
        return f(np.asarray(a, np.float32).reshape(L, 16, 2, 64, 16).transpose(0, 2, 3, 1, 4).reshape(L, P, 16, 16))
    shared["b_re"] = gpc(ssm_b_re)
    shared["b_im"] = gpc(ssm_b_im)
    shared["c_re"] = gpc(np.asarray(ssm_c_re, np.float32).transpose(0, 1, 3, 2))
    shared["c_im"] = gpc(np.asarray(ssm_c_im, np.float32).transpose(0, 1, 3, 2))
    shared["d_skip"] = f(np.asarray(ssm_d, np.float32).reshape(L, 4, 8, 16).transpose(0, 2, 3, 1).reshape(L, P, 4))
    lqk = np.stack([np.asarray(a, np.float32) for a in (diff_lq1, diff_lk1, diff_lq2, diff_lk2)], axis=1)
    shared["lqk"] = f(np.broadcast_to(lqk[:, None, :, :], (L, P, 4, 64)))
    xs = np.asarray(x, dtype=np.float32)
    return shared, xs


_PROGRAM_CACHE = {}


def kernel(**inputs):
    shared, xs = prep_inputs(**inputs)
    B = xs.shape[0]
    if "nc" not in _PROGRAM_CACHE:
        _PROGRAM_CACHE["nc"] = build_program()
    nc = _PROGRAM_CACHE["nc"]
    n_cores = 8
    in_maps = []
    for c in range(n_cores):
        m = dict(shared)
        m["x"] = np.ascontiguousarray(xs[c % B])
        in_maps.append(m)
    res = run_bass_kernel_spmd(nc, in_maps, core_ids=list(range(n_cores)))
    out = np.stack([np.asarray(res.results[b]["y"], dtype=np.float32) for b in range(B)], axis=0)
    return out
```

```python
import math
from contextlib import ExitStack

import numpy as np
import concourse.bass as bass
import concourse.mybir as mybir
from concourse.bass_utils import run_bass_kernel_spmd

F32 = mybir.dt.float32
BF16 = mybir.dt.bfloat16
I32 = mybir.dt.int32
AF = mybir.ActivationFunctionType
ALU = mybir.AluOpType
AX = mybir.AxisListType

P = 128
S = 4096
D = 1024
TT = 512
NT = S // TT
NB = S // P
DC = D // P
DEPTH = 4
NPROJ = 8192
EPS = 1e-6
MAGIC = 12582912.0
TWO_PI = 2.0 * math.pi
NEG = -30000.0
SEM_CAP = 30000


class Res:
    __slots__ = ("name", "lw", "readers", "excl", "slot")

    def __init__(self, name, excl=False):
        self.name = name
        self.lw = None
        self.readers = []
        self.excl = excl
        self.slot = None


class SemSlot:
    __slots__ = ("sem", "cnt")

    def __init__(self, sem):
        self.sem = sem
        self.cnt = 0


class Eng:
    def __init__(self, fw, name, handle):
        self.fw = fw
        self.name = name
        self.h = handle
        self.sems = []
        self.count = 0
        self.waited = {}

    def cur_event(self):
        n = self.count + 1
        k = (n - 1) // SEM_CAP
        while len(self.sems) <= k:
            self.sems.append(self.fw.new_sem(f"{self.name}{len(self.sems)}"))
        return (self.sems[k], n - k * SEM_CAP, self.name)

    def last_event(self):
        if self.count == 0:
            return None
        n = self.count
        k = (n - 1) // SEM_CAP
        return (self.sems[k], n - k * SEM_CAP, self.name)


class FW:
    def __init__(self, nc, es):
        self.nc = nc
        self.es = es
        self.nsem = 0
        self.E = {
            "pe": Eng(self, "pe", nc.tensor),
            "act": Eng(self, "act", nc.scalar),
            "dve": Eng(self, "dve", nc.vector),
            "pool": Eng(self, "pool", nc.gpsimd),
            "sp": Eng(self, "sp", nc.sync),
        }
        self.out_events = []
        self.free_slots = []
        self.all_slots = []
        self.phase_res = []
        self.ninst = 0

    def new_sem(self, name):
        self.nsem += 1
        return self.es.enter_context(self.nc.semaphore(f"s{self.nsem}_{name}"))

    def get_slot(self, res, phase_local=True):
        if res.slot is None:
            if self.free_slots:
                res.slot = self.free_slots.pop()
            else:
                res.slot = SemSlot(self.new_sem("dma"))
                self.all_slots.append(res.slot)
            if phase_local:
                self.phase_res.append(res)
        return res.slot

    def _need(self, eng, ev, waits):
        if ev is None:
            return
        sem, val, _ = ev
        key = id(sem)
        if eng.waited.get(key, 0) >= val:
            return
        cur = waits.get(key)
        if cur is None or cur[1] < val:
            waits[key] = (sem, val)

    def _collect(self, eng, reads, writes, waits):
        for r in reads:
            if r.lw is not None:
                self._need(eng, r.lw, waits)
            if r.excl:
                for ev in r.readers:
                    if ev[2] != eng.name:
                        self._need(eng, ev, waits)
        for w in writes:
            if w.lw is not None and w.lw[2] != eng.name:
                self._need(eng, w.lw, waits)
            for ev in w.readers:
                if ev[2] != eng.name:
                    self._need(eng, ev, waits)

    def _emit_waits(self, eng, waits):
        for key, (sem, val) in waits.items():
            eng.h.wait_ge(sem, val)
            eng.waited[key] = val
            self.ninst += 1

    def _update(self, ev, reads, writes):
        for r in reads:
            r.readers.append(ev)
            if len(r.readers) > 16:
                last = {}
                for e in r.readers:
                    k = (e[2], id(e[0]))
                    if k not in last or last[k][1] < e[1]:
                        last[k] = e
                r.readers = list(last.values())
        for w in writes:
            w.lw = ev
            w.readers = []

    def op(self, engname, fn, reads=(), writes=()):
        eng = self.E[engname]
        waits = {}
        self._collect(eng, reads, writes, waits)
        self._emit_waits(eng, waits)
        ins = fn(eng.h)
        ev = eng.cur_event()
        ins.then_inc(ev[0], 1)
        eng.count += 1
        self._update(ev, reads, writes)
        self.ninst += 1
        return ev

    def dma(self, out, in_, reads=(), writes=(), q="sp", semres=None, final=False, persistent=False, **kw):
        eng = self.E[q]
        if semres is None:
            semres = writes[0] if writes else reads[0]
        slot = self.get_slot(semres, phase_local=not persistent)
        if slot.cnt >= 1800:
            slot2 = SemSlot(self.new_sem("dma"))
            self.all_slots.append(slot2)
            semres.slot = slot2
            old = slot
            slot = slot2
            waits0 = {}
            self._need(eng, (old.sem, 16 * old.cnt, "dma"), waits0)
            self._emit_waits(eng, waits0)
        waits = {}
        self._collect(eng, reads, writes, waits)
        if slot.cnt > 0:
            self._need(eng, (slot.sem, 16 * slot.cnt, "dma"), waits)
        self._emit_waits(eng, waits)
        eng.h.dma_start(out=out, in_=in_, **kw).then_inc(slot.sem, 16)
        slot.cnt += 1
        ev = (slot.sem, 16 * slot.cnt, "dma")
        self._update(ev, reads, writes)
        if final:
            self.out_events.append(ev)
        self.ninst += 1
        return ev

    def barrier(self):
        evs = []
        for e in self.E.values():
            le = e.last_event()
            if le is not None:
                evs.append(le)
        for s in self.all_slots:
            if s.cnt > 0:
                evs.append((s.sem, 16 * s.cnt, "dma"))
        for e in self.E.values():
            waits = {}
            for ev in evs:
                if ev[2] == e.name:
                    continue
                self._need(e, ev, waits)
            self._emit_waits(e, waits)
        for r in self.phase_res:
            if r.slot is not None:
                self.free_slots.append(r.slot)
                r.slot = None
        self.phase_res = []

    def finish(self):
        eng = self.E["sp"]
        waits = {}
        for ev in self.out_events:
            self._need(eng, ev, waits)
        self._emit_waits(eng, waits)


def run_pipeline(units, stages):
    n = len(units)
    ns = len(stages)
    for step in range(n + ns - 1):
        for s in range(ns - 1, -1, -1):
            u = step - s
            if 0 <= u < n:
                stages[s](units[u])


def build_program(nlayers=DEPTH, dbg=None, stop_after=None):
    nc = bass.Bass("TRN2", target_bir_lowering=False)
    dbg = dbg or {}

    def din(name, shape):
        return nc.dram_tensor(name, list(shape), F32, kind="ExternalInput").ap()

    x_in = din("x", [S, D])
    w_in_d = din("w_in", [DEPTH, 64, P, DC, P])
    w_glu_d = din("w_glu", [DEPTH, P, 4, 512])
    w_br_d = din("w_br", [DEPTH, 3, P, 4, D])
    w_out_d = din("w_out", [DEPTH, P, DC, D])
    normw_d = din("norm_w", [DEPTH, P, DC])
    bmerge_d = din("b_merge", [DEPTH, P, 24])
    bglu_d = din("b_glu", [DEPTH, P, 4])
    subln_d = din("subln_w", [DEPTH, P, 1])
    fnorm_d = din("fnorm_w", [P, DC])
    are_d = din("a_re", [DEPTH, P, 16])
    aim_d = din("a_im", [DEPTH, P, 16])
    ldt_d = din("log_dt", [DEPTH, P, 16])
    bre_d = din("b_re", [DEPTH, P, 16, 16])
    bim_d = din("b_im", [DEPTH, P, 16, 16])
    cre_d = din("c_re", [DEPTH, P, 16, 16])
    cim_d = din("c_im", [DEPTH, P, 16, 16])
    dsk_d = din("d_skip", [DEPTH, P, 4])
    lqk_d = din("lqk", [DEPTH, P, 4, 64])

    y_out = nc.dram_tensor("y", [S, D], F32, kind="ExternalOutput").ap()
    dbg_out = {k: nc.dram_tensor(k, list(v), F32, kind="ExternalOutput").ap() for k, v in dbg.items()}

    xT_d = nc.dram_tensor("xT_scr", [P, DC, S], F32).ap()
    obr_d = nc.dram_tensor("obr_scr", [3, P, 4, S], BF16).ap()
    yg_d = nc.dram_tensor("yg_scr", [P, 4, S], BF16).ap()
    mg_d = nc.dram_tensor("mg_scr", [P, DC, S], BF16).ap()
    ropeC_d = nc.dram_tensor("ropeC_scr", [P, S], F32).ap()
    ropeS_d = nc.dram_tensor("ropeS_scr", [P, S], F32).ap()

    es = ExitStack()
    with es:
        fw = FW(nc, es)

        _uid = [0]

        def sbt(stack, name, shape, dt):
            _uid[0] += 1
            return stack.enter_context(nc.sbuf_tensor(f"{name}_{_uid[0]}", list(shape), dt))

        hT = sbt(es, "hT", [P, DC, S], BF16)
        R_hT = [Res(f"hT{j}") for j in range(NT)]
        ident_f = sbt(es, "ident_f", [P, P], F32)
        ident_b = sbt(es, "ident_b", [P, P], BF16)
        ones_f = sbt(es, "ones_f", [P, P], F32)
        ones_b = sbt(es, "ones_b", [P, P], BF16)
        nones_b = sbt(es, "nones_b", [P, P], BF16)
        ntri_b = sbt(es, "ntri_b", [P, P], BF16)
        mask_sb = sbt(es, "mask_sb", [P, P], BF16)
        mask_df = sbt(es, "mask_df", [P, P], BF16)
        iota_t = sbt(es, "iota_t", [P, TT], F32)
        halfpi = sbt(es, "halfpi", [P, 1], F32)
        R_const = Res("const")
        NST = 2
        wst = [sbt(es, f"wst{i}", [P, DC, P], F32) for i in range(NST)]
        R_wst = [Res(f"wst{i}") for i in range(NST)]
        wst_i = [0]
        nw = sbt(es, "nw", [P, DC], F32)
        R_nw = Res("nw")

        pb = [es.enter_context(nc.psum_tensor(f"pb{i}", [P, TT], F32)) for i in range(8)]
        R_pb = [Res(f"pb{i}", excl=True) for i in range(8)]

        def consts():
            with ExitStack() as ps:
                io_i = sbt(ps, "io_i", [P, P], I32)
                io_f = sbt(ps, "io_f", [P, P], F32)
                io2_i = sbt(ps, "io2_i", [P, TT], I32)
                R_a, R_b, R_c = Res("io_i"), Res("io_f"), Res("io2")
                fw.op("pool", lambda h: h.iota(io_i[:], pattern=[[1, P]], base=0, channel_multiplier=-1), writes=[R_a])
                fw.op("dve", lambda h: h.tensor_copy(out=io_f[:], in_=io_i[:]), reads=[R_a], writes=[R_b])
                W = [R_const]
                fw.op("dve", lambda h: h.tensor_scalar(out=ident_f[:], in0=io_f[:], scalar1=0.0, scalar2=None, op0=ALU.is_equal), reads=[R_b], writes=W)
                fw.op("dve", lambda h: h.tensor_scalar(out=ident_b[:], in0=io_f[:], scalar1=0.0, scalar2=None, op0=ALU.is_equal), reads=[R_b], writes=W)
                fw.op("dve", lambda h: h.tensor_scalar(out=ntri_b[:], in0=io_f[:], scalar1=0.0, scalar2=-1.0, op0=ALU.is_le, op1=ALU.mult), reads=[R_b], writes=W)
                fw.op("dve", lambda h: h.tensor_scalar(out=mask_sb[:], in0=io_f[:], scalar1=0.0, scalar2=NEG, op0=ALU.is_le, op1=ALU.mult), reads=[R_b], writes=W)
                fw.op("dve", lambda h: h.tensor_scalar(out=mask_df[:], in0=io_f[:], scalar1=0.0, scalar2=NEG, op0=ALU.is_lt, op1=ALU.mult), reads=[R_b], writes=W)
                fw.op("pool", lambda h: h.memset(ones_f[:], 1.0), writes=W)
                fw.op("pool", lambda h: h.memset(ones_b[:], 1.0), writes=W)
                fw.op("pool", lambda h: h.memset(nones_b[:], -1.0), writes=W)
                fw.op("pool", lambda h: h.memset(halfpi[:], math.pi / 2), writes=W)
                fw.op("pool", lambda h: h.iota(io2_i[:], pattern=[[1, TT]], base=0, channel_multiplier=0), writes=[R_c])
                fw.op("dve", lambda h: h.tensor_copy(out=iota_t[:], in_=io2_i[:]), reads=[R_c], writes=W)
                pidx_i = sbt(ps, "pidx_i", [P, 1], I32)
                pidx = sbt(ps, "pidx", [P, 1], F32)
                wfreq = sbt(ps, "wfreq", [P, 1], F32)
                R_p = Res("pidx")
                fw.op("pool", lambda h: h.iota(pidx_i[:], pattern=[[0, 1]], base=0, channel_multiplier=1), writes=[R_p])
                fw.op("dve", lambda h: h.tensor_copy(out=pidx[:], in_=pidx_i[:]), reads=[R_p], writes=[R_p])
                t0 = sbt(ps, "t0", [P, 1], F32)
                fw.op("dve", lambda h: h.tensor_scalar(out=t0[:], in0=pidx[:], scalar1=-15.5, scalar2=1.0 / 32, op0=ALU.add, op1=ALU.mult), reads=[R_p], writes=[R_p])
                fw.op("dve", lambda h: h.tensor_scalar(out=t0[:], in0=t0[:], scalar1=MAGIC, scalar2=MAGIC, op0=ALU.add, op1=ALU.subtract), reads=[R_p], writes=[R_p])
                fw.op("dve", lambda h: h.scalar_tensor_tensor(out=pidx[:], in0=t0[:], scalar=-32.0, in1=pidx[:], op0=ALU.mult, op1=ALU.add), reads=[R_p], writes=[R_p])
                fw.op("act", lambda h: h.activation(out=wfreq[:], in_=pidx[:], func=AF.Exp, scale=-math.log(10000.0) / 32), reads=[R_p], writes=[R_p])
                fw.op("dve", lambda h: h.tensor_scalar(out=wfreq[:], in0=wfreq[:], scalar1=1.0 / TWO_PI, scalar2=None, op0=ALU.mult), reads=[R_p], writes=[R_p])
                tpos = sbt(ps, "tpos", [P, TT], F32)
                t1 = sbt(ps, "t1", [P, TT], F32)
                kk = sbt(ps, "kk", [P, TT], F32)
                ff = sbt(ps, "ff", [P, TT], F32)
                tS = sbt(ps, "tS", [P, TT], F32)
                tC = sbt(ps, "tC", [P, TT], F32)
                R_t = Res("ropetmp")
                R_tS, R_tC = Res("tS"), Res("tC")
                for j in range(NT):
                    fw.op("dve", lambda h: h.tensor_scalar(out=tpos[:], in0=iota_t[:], scalar1=float(j * TT), scalar2=None, op0=ALU.add), reads=[R_const], writes=[R_t])
                    fw.op("dve", lambda h: h.tensor_scalar(out=t1[:], in0=tpos[:], scalar1=wfreq[:, 0:1], scalar2=MAGIC, op0=ALU.mult, op1=ALU.add), reads=[R_t, R_p], writes=[R_t])
                    fw.op("dve", lambda h: h.tensor_scalar(out=kk[:], in0=t1[:], scalar1=MAGIC, scalar2=None, op0=ALU.subtract), reads=[R_t], writes=[R_t])
                    fw.op("dve", lambda h: h.scalar_tensor_tensor(out=ff[:], in0=tpos[:], scalar=wfreq[:, 0:1], in1=kk[:], op0=ALU.mult, op1=ALU.subtract), reads=[R_t, R_p], writes=[R_t])
                    fw.op("act", lambda h: h.activation(out=tS[:], in_=ff[:], func=AF.Sin, scale=TWO_PI), reads=[R_t], writes=[R_tS])
                    fw.op("dve", lambda h: h.scalar_tensor_tensor(out=ff[:], in0=ff[:], scalar=-1.0, in1=ff[:], op0=ALU.mult, op1=ALU.max), reads=[R_t], writes=[R_t])
                    fw.op("act", lambda h: h.activation(out=tC[:], in_=ff[:], func=AF.Sin, scale=-TWO_PI, bias=halfpi[:]), reads=[R_t, R_const], writes=[R_tC])
                    fw.dma(ropeS_d[:, j * TT:(j + 1) * TT], tS[:], reads=[R_tS], writes=[R_ropeS[j]], semres=R_tS)
                    fw.dma(ropeC_d[:, j * TT:(j + 1) * TT], tC[:], reads=[R_tC], writes=[R_ropeC[j]], semres=R_tC)
                fw.barrier()

        R_ropeS = [Res(f"ropeS{j}") for j in range(NT)]
        R_ropeC = [Res(f"ropeC{j}") for j in range(NT)]
        R_xT = [Res(f"xT{j}") for j in range(NT)]
        R_obr = [[Res(f"obr{n}_{j}") for j in range(NT)] for n in range(3)]
        R_yg = [Res(f"yg{j}") for j in range(NT)]
        R_mgd = [Res(f"mgd{j}") for j in range(NT)]

        def load_wblock(l, cb, dst, R_dst, rot=False, scale_nw=True, eng="pool"):
            i = wst_i[0] % NST
            wst_i[0] += 1
            st, R_st = wst[i], R_wst[i]
            fw.dma(st[:], w_in_d[l, cb], writes=[R_st], persistent=True)
            nwb = nw[:].unsqueeze(2).broadcast_to([P, DC, P])
            if not rot:
                fw.op(eng, lambda h: h.tensor_tensor(out=dst, in0=st[:], in1=nwb, op=ALU.mult), reads=[R_st, R_nw], writes=[R_dst])
            else:
                for hh in range(2):
                    b0 = hh * 64
                    nwb32 = nw[:].unsqueeze(2).broadcast_to([P, DC, 32])
                    fw.op("dve", lambda h: h.scalar_tensor_tensor(out=dst[:, :, b0:b0 + 32], in0=st[:, :, b0 + 32:b0 + 64], scalar=-1.0, in1=nwb32, op0=ALU.mult, op1=ALU.mult),
                          reads=[R_st, R_nw], writes=[R_dst])
                    fw.op(eng, lambda h: h.tensor_tensor(out=dst[:, :, b0 + 32:b0 + 64], in0=st[:, :, b0:b0 + 32], in1=nwb32, op=ALU.mult),
                          reads=[R_st, R_nw], writes=[R_dst])

        def load_generic(src_ap, dst, R_dst, shape, eng="pool"):
            i = wst_i[0] % NST
            wst_i[0] += 1
            st, R_st = wst[i], R_wst[i]
            a, b = shape
            sv = st[:].rearrange("p c n -> p (c n)")[:, 0:a * b].rearrange("p (a b) -> p a b", a=a)
            fw.dma(sv, src_ap, writes=[R_st], persistent=True)
            fw.op(eng, lambda h: h.tensor_copy(out=dst, in_=sv), reads=[R_st], writes=[R_dst])

        def proj_fm(bank, wbf, R_w, j, M=P, c0=0):
            for c in range(DC):
                fw.op("pe", lambda h, c=c: h.matmul(pb[bank][0:M, :], lhsT=wbf[:, c, c0:c0 + M], rhs=hT[:, c, j * TT:(j + 1) * TT], start=(c == 0), stop=(c == DC - 1)),
                      reads=[R_w, R_hT[j]], writes=[R_pb[bank]])

        def dump(name, src_ap, R_src, dst_slice):
            if name in dbg_out:
                fw.dma(dst_slice(dbg_out[name]), src_ap, reads=[R_src], final=True, semres=R_src)

        def rsqrt_from_psum(bank, scale, dst, R_dst, tmp):
            fw.op("dve", lambda h: h.tensor_scalar(out=tmp, in0=pb[bank][:, :], scalar1=scale, scalar2=EPS, op0=ALU.mult, op1=ALU.add), reads=[R_pb[bank]], writes=[R_dst])
            fw.op("act", lambda h: h.activation(out=tmp, in_=tmp, func=AF.Sqrt), reads=[R_dst], writes=[R_dst])
            fw.op("dve", lambda h: h.reciprocal(out=dst, in_=tmp), reads=[R_dst], writes=[R_dst])

        def norm_tile(xt, R_xt, j, sq, R_sq, rstd, R_rstd, bank):
            for c in range(DC):
                i = c % 2
                fw.op("act", lambda h, c=c, i=i: h.activation(out=sq[i][:], in_=xt[:, c, :], func=AF.Square), reads=[R_xt], writes=[R_sq[i]])
                fw.op("pe", lambda h, c=c, i=i: h.matmul(pb[bank][:, :], lhsT=ones_f[:], rhs=sq[i][:], start=(c == 0), stop=(c == DC - 1)),
                      reads=[R_sq[i], R_const], writes=[R_pb[bank]])
            rsqrt_from_psum(bank, 1.0 / D, rstd[:], R_rstd, rstd[:])

        def phase0():
            with ExitStack() as ps:
                xin = [sbt(ps, f"xin{i}", [P, D], F32) for i in range(2)]
                R_xin = [Res(f"xin{i}") for i in range(2)]
                xt = [sbt(ps, f"xt0_{i}", [P, DC, TT], F32) for i in range(2)]
                R_xt = [Res(f"xt0_{i}") for i in range(2)]
                sq = [sbt(ps, f"sq0_{i}", [P, TT], F32) for i in range(2)]
                R_sq = [Res(f"sq0_{i}") for i in range(2)]
                rstd = sbt(ps, "rstd0", [P, TT], F32)
                R_rstd = Res("rstd0")
                fw.dma(nw[:], normw_d[0], writes=[R_nw], persistent=True)
                for j in range(NT):
                    xb_, R_xb = xt[j % 2], R_xt[j % 2]
                    for blk in range(4):
                        g = j * 4 + blk
                        xi, R_xi = xin[g % 2], R_xin[g % 2]
                        fw.dma(xi[:], x_in[g * P:(g + 1) * P, :], writes=[R_xi])
                        for half in range(2):
                            bank = (g % 2) * 2 + half
                            for cc in range(4):
                                c = half * 4 + cc
                                fw.op("pe", lambda h, c=c, cc=cc, bank=bank: h.transpose(pb[bank][:, cc * P:(cc + 1) * P], xi[:, c * P:(c + 1) * P], ident_f[:]),
                                      reads=[R_xi, R_const], writes=[R_pb[bank]])
                            eng = "act" if half == 0 else "dve"
                            src = pb[bank][:, :].rearrange("p (c t) -> p c t", c=4)
                            dst = xb_[:, half * 4:half * 4 + 4, blk * P:(blk + 1) * P]
                            if eng == "act":
                                fw.op("act", lambda h, src=src, dst=dst: h.copy(out=dst, in_=src), reads=[R_pb[bank]], writes=[R_xb])
                            else:
                                fw.op("dve", lambda h, src=src, dst=dst: h.tensor_copy(out=dst, in_=src), reads=[R_pb[bank]], writes=[R_xb])
                    fw.dma(xT_d[:, :, j * TT:(j + 1) * TT], xb_[:], reads=[R_xb], writes=[R_xT[j]], semres=R_xb)
                    norm_tile(xb_, R_xb, j, sq, R_sq, rstd, R_rstd, 4)
                    fw.op("dve", lambda h, xb_=xb_, j=j: h.tensor_tensor(out=hT[:, :, j * TT:(j + 1) * TT], in0=xb_[:], in1=rstd[:].unsqueeze(1).broadcast_to([P, DC, TT]), op=ALU.mult),
                          reads=[R_xb, R_rstd], writes=[R_hT[j]])
                fw.barrier()

        def ssm_phase(l):
            with ExitStack() as ps:
                w_u = [sbt(ps, f"w_u{i}", [P, DC, P], BF16) for i in range(4)]
                R_wu = [Res(f"w_u{i}") for i in range(4)]
                for cb in range(4):
                    load_wblock(l, cb, w_u[cb][:], R_wu[cb])
                def t16(name):
                    return sbt(ps, name, [P, 16], F32)
                are, aim, ldt = t16("are"), t16("aim"), t16("ldt")
                R_pp = Res("ssm_params")
                R_ld = [Res(f"ssm_ld{i}") for i in range(8)]
                fw.dma(are[:], are_d[l], writes=[R_ld[0]])
                fw.dma(aim[:], aim_d[l], writes=[R_ld[1]])
                fw.dma(ldt[:], ldt_d[l], writes=[R_ld[2]])
                cre = sbt(ps, "cre", [P, 16, 16], F32)
                cim = sbt(ps, "cim", [P, 16, 16], F32)
                dsk = sbt(ps, "dsk", [P, 4], F32)
                fw.dma(cre[:], cre_d[l], writes=[R_ld[5]])
                fw.dma(cim[:], cim_d[l], writes=[R_ld[6]])
                fw.dma(dsk[:], dsk_d[l], writes=[R_ld[7]])
                RL = R_ld
                dt_, th, lr, rho, wturn = t16("dt_"), t16("th"), t16("lr"), t16("rho"), t16("wturn")
                W = [R_pp]

                def dv(fn, reads=()):
                    fw.op("dve", fn, reads=list(reads) + [R_pp], writes=W)

                def ac(fn, reads=()):
                    fw.op("act", fn, reads=list(reads) + [R_pp], writes=W)

                ac(lambda h: h.activation(out=dt_[:], in_=ldt[:], func=AF.Exp), [RL[2]])
                dv(lambda h: h.tensor_tensor(out=th[:], in0=dt_[:], in1=aim[:], op=ALU.mult), [RL[1]])
                dv(lambda h: h.tensor_tensor(out=lr[:], in0=dt_[:], in1=are[:], op=ALU.mult), [RL[0]])
                ac(lambda h: h.activation(out=rho[:], in_=lr[:], func=AF.Exp))
                dv(lambda h: h.tensor_scalar(out=wturn[:], in0=th[:], scalar1=1.0 / TWO_PI, scalar2=None, op0=ALU.mult))

                def sincos(wt, s_out, c_out, tmpk, tmpf, mult=1.0):
                    dv(lambda h: h.tensor_scalar(out=tmpf[:], in0=wt[:], scalar1=mult, scalar2=None, op0=ALU.mult))
                    dv(lambda h: h.tensor_scalar(out=tmpk[:], in0=tmpf[:], scalar1=MAGIC, scalar2=MAGIC, op0=ALU.add, op1=ALU.subtract))
                    dv(lambda h: h.tensor_tensor(out=tmpf[:], in0=tmpf[:], in1=tmpk[:], op=ALU.subtract))
                    ac(lambda h: h.activation(out=s_out[:], in_=tmpf[:], func=AF.Sin, scale=TWO_PI))
                    dv(lambda h: h.scalar_tensor_tensor(out=tmpf[:], in0=tmpf[:], scalar=-1.0, in1=tmpf[:], op0=ALU.mult, op1=ALU.max))
                    ac(lambda h: h.activation(out=c_out[:], in_=tmpf[:], func=AF.Sin, scale=-TWO_PI, bias=halfpi[:]), [R_const])

                sin1, cos1, s512, c512, tk, tf = t16("sin1"), t16("cos1"), t16("s512"), t16("c512"), t16("tk"), t16("tf")
                sincos(wturn, sin1, cos1, tk, tf, 1.0)
                sincos(wturn, s512, c512, tk, tf, float(TT))
                CTr = sbt(ps, "CTr", [P, 4, 4, 4, 32], BF16)
                CTi = sbt(ps, "CTi", [P, 4, 4, 4, 32], BF16)
                CTn = sbt(ps, "CTn", [P, 4, 4, 4, 32], BF16)
                R_CT = Res("CT")
                BTr = sbt(ps, "BTr", [P, 16, P], BF16)
                BTi = sbt(ps, "BTi", [P, 16, P], BF16)
                R_BT = Res("BT")
                carry = sbt(ps, "carry", [P, 16, 2], F32)
                R_carry = [Res(f"carry{i}") for i in range(16)]
                ctmp = sbt(ps, "ctmp", [P, 2], F32)
                R_ctmp = Res("ctmp")
                fw.op("pool", lambda h: h.memset(carry[:], 0.0), writes=R_carry)
                with ExitStack() as pp_:
                    bre = sbt(pp_, "bre", [P, 16, 16], F32)
                    bim = sbt(pp_, "bim", [P, 16, 16], F32)
                    fw.dma(bre[:], bre_d[l], writes=[R_ld[3]])
                    fw.dma(bim[:], bim_d[l], writes=[R_ld[4]])

                    def t16p(name):
                        return sbt(pp_, name, [P, 16], F32)
                    Are, Aim, den, am1, cr, ci, tmpa, tmpb = (t16p(n) for n in ("Are", "Aim", "den", "am1", "cr", "ci", "tmpa", "tmpb"))
                    dv(lambda h: h.tensor_tensor(out=Are[:], in0=rho[:], in1=cos1[:], op=ALU.mult))
                    dv(lambda h: h.tensor_tensor(out=Aim[:], in0=rho[:], in1=sin1[:], op=ALU.mult))
                    dv(lambda h: h.tensor_tensor(out=den[:], in0=are[:], in1=are[:], op=ALU.mult))
                    dv(lambda h: h.tensor_tensor(out=tmpa[:], in0=aim[:], in1=aim[:], op=ALU.mult))
                    dv(lambda h: h.tensor_tensor(out=den[:], in0=den[:], in1=tmpa[:], op=ALU.add))
                    dv(lambda h: h.reciprocal(out=den[:], in_=den[:]))
                    dv(lambda h: h.tensor_scalar(out=am1[:], in0=Are[:], scalar1=-1.0, scalar2=None, op0=ALU.add))
                    dv(lambda h: h.tensor_tensor(out=tmpa[:], in0=am1[:], in1=are[:], op=ALU.mult))
                    dv(lambda h: h.tensor_tensor(out=tmpb[:], in0=Aim[:], in1=aim[:], op=ALU.mult))
                    dv(lambda h: h.tensor_tensor(out=tmpa[:], in0=tmpa[:], in1=tmpb[:], op=ALU.add))
                    dv(lambda h: h.tensor_tensor(out=cr[:], in0=tmpa[:], in1=den[:], op=ALU.mult))
                    dv(lambda h: h.tensor_tensor(out=tmpa[:], in0=Aim[:], in1=are[:], op=ALU.mult))
                    dv(lambda h: h.tensor_tensor(out=tmpb[:], in0=am1[:], in1=aim[:], op=ALU.mult))
                    dv(lambda h: h.tensor_tensor(out=tmpa[:], in0=tmpa[:], in1=tmpb[:], op=ALU.subtract))
                    dv(lambda h: h.tensor_tensor(out=ci[:], in0=tmpa[:], in1=den[:], op=ALU.mult))
                    bbr = sbt(pp_, "bbr", [P, 16, 16], F32)
                    bbi = sbt(pp_, "bbi", [P, 16, 16], F32)
                    tb = sbt(pp_, "tb", [P, 16, 16], F32)
                    crb = cr[:].unsqueeze(2).broadcast_to([P, 16, 16])
                    cib = ci[:].unsqueeze(2).broadcast_to([P, 16, 16])
                    dv(lambda h: h.tensor_tensor(out=bbr[:], in0=bre[:], in1=crb, op=ALU.mult), [RL[3]])
                    dv(lambda h: h.tensor_tensor(out=tb[:], in0=bim[:], in1=cib, op=ALU.mult), [RL[4]])
                    dv(lambda h: h.tensor_tensor(out=bbr[:], in0=bbr[:], in1=tb[:], op=ALU.subtract))
                    dv(lambda h: h.tensor_tensor(out=bbi[:], in0=bim[:], in1=crb, op=ALU.mult))
                    dv(lambda h: h.tensor_tensor(out=tb[:], in0=bre[:], in1=cib, op=ALU.mult))
                    dv(lambda h: h.tensor_tensor(out=bbi[:], in0=bbi[:], in1=tb[:], op=ALU.add))
                    Zr = sbt(pp_, "Zr", [P, 4, 4, 4, 32], F32)
                    Zi = sbt(pp_, "Zi", [P, 4, 4, 4, 32], F32)
                    R_Z = Res("Z")
                    for z in (Zr, Zi):
                        fw.op("pool", lambda h, z=z: h.memset(z[:], 0.0), writes=[R_Z])
                    for z in (CTr, CTi, CTn):
                        fw.op("pool", lambda h, z=z: h.memset(z[:], 0.0), writes=[R_CT])
                    for q in range(4):
                        for gi in range(2):
                            p0 = gi * 64

                            def v4(t, p0=p0, q=q):
                                return t[p0:p0 + 64, :, :].rearrange("p (cb q) c -> p cb q c", q=4)[:, :, q, :]
                            fw.op("dve", lambda h: h.tensor_copy(out=Zr[p0:p0 + 64, :, q, q, gi * 16:gi * 16 + 16], in_=v4(bbr)), reads=[R_pp], writes=[R_Z])
                            fw.op("dve", lambda h: h.tensor_copy(out=Zi[p0:p0 + 64, :, q, q, gi * 16:gi * 16 + 16], in_=v4(bbi)), reads=[R_pp], writes=[R_Z])
                            fw.op("dve", lambda h: h.tensor_copy(out=CTr[p0:p0 + 64, :, q, q, gi * 16:gi * 16 + 16], in_=v4(cre)), reads=[RL[5]], writes=[R_CT])
                            fw.op("dve", lambda h: h.tensor_scalar(out=CTi[p0:p0 + 64, :, q, q, gi * 16:gi * 16 + 16], in0=v4(cim), scalar1=-1.0, scalar2=None, op0=ALU.mult), reads=[RL[6]], writes=[R_CT])
                            fw.op("dve", lambda h: h.tensor_scalar(out=CTn[p0:p0 + 64, :, q, q, gi * 16:gi * 16 + 16], in0=v4(cre), scalar1=-1.0, scalar2=None, op0=ALU.mult), reads=[RL[5]], writes=[R_CT])
                    for ri, (Z, BT) in enumerate(((Zr, BTr), (Zi, BTi))):
                        for g4 in range(4):
                            bank = 6 + (g4 % 2)
                            for k in range(4):
                                pair = g4 * 4 + k
                                cbi, qi = pair // 4, pair % 4
                                src = Z[:, cbi, qi, :, :].rearrange("p a b -> p (a b)")
                                fw.op("pe", lambda h, src=src, k=k, bank=bank: h.transpose(pb[bank][:, k * P:(k + 1) * P], src, ident_f[:]), reads=[R_Z, R_const], writes=[R_pb[bank]])
                            fw.op("act", lambda h, BT=BT, g4=g4, bank=bank: h.copy(out=BT[:, g4 * 4:g4 * 4 + 4, :], in_=pb[bank][:, :].rearrange("p (k n) -> p k n", k=4)), reads=[R_pb[bank]], writes=[R_BT])
                    fw.barrier()
                with ExitStack() as pw:
                    NR = 2
                    NU = 3
                    tcos = [sbt(pw, f"tcos{i}", [P, TT], F32) for i in range(4)]
                    tsin = [sbt(pw, f"tsin{i}", [P, TT], F32) for i in range(4)]
                    R_tab = [Res(f"tab{i}") for i in range(4)]
                    tA = sbt(pw, "tabA", [P, TT], F32)
                    tB = sbt(pw, "tabB", [P, TT], F32)
                    R_tt = Res("tabtmp")
                    utf = [sbt(pw, f"utf{i}", [P, TT], F32) for i in range(NU)]
                    utb = [sbt(pw, f"utb{i}", [P, TT], BF16) for i in range(NU)]
                    R_utf = [Res(f"utf{i}") for i in range(NU)]
                    R_utb = [Res(f"utb{i}") for i in range(NU)]
                    mm = [[sbt(pw, f"m{k}_{i}", [P, TT], F32) for k in range(4)] for i in range(NR)]
                    R_mm = [Res(f"mm{i}") for i in range(NR)]
                    bp = [[sbt(pw, f"bp{k}_{i}", [P, TT], F32) for k in range(2)] for i in range(NR)]
                    R_bp = [Res(f"bp{i}") for i in range(NR)]
                    yy = [[sbt(pw, f"yy{k}_{i}", [P, TT], F32) for k in range(2)] for i in range(NR)]
                    R_yy = [Res(f"yy{i}") for i in range(NR)]
                    xA = [sbt(pw, f"xA{i}", [P, 4, TT], BF16) for i in range(2)]
                    xB = [sbt(pw, f"xB{i}", [P, 4, TT], BF16) for i in range(2)]
                    xC = [sbt(pw, f"xC{i}", [P, 4, TT], BF16) for i in range(2)]
                    xD = [sbt(pw, f"xD{i}", [P, 4, TT], BF16) for i in range(2)]
                    R_xx = [[Res(f"xx{i}_{q}") for q in range(4)] for i in range(2)]
                    yv = [sbt(pw, f"yv{i}", [P, TT], F32) for i in range(2)]
                    ygb = [sbt(pw, f"ygb{i}", [P, TT], BF16) for i in range(2)]
                    g1, g2 = tA, tB
                    R_yv = [Res(f"yv{i}") for i in range(2)]
                    R_g = R_tt
                    R_ygb = [Res(f"ygb{i}") for i in range(2)]

                    ucount = [0]
                    for cb in range(4):
                        for q in range(4):
                            pair = cb * 4 + q
                            wcol = wturn[:, pair:pair + 1]
                            fw.op("dve", lambda h: h.tensor_scalar(out=tA[:], in0=iota_t[:], scalar1=wcol, scalar2=MAGIC, op0=ALU.mult, op1=ALU.add), reads=[R_const, R_pp], writes=[R_tt])
                            fw.op("dve", lambda h: h.tensor_scalar(out=tB[:], in0=tA[:], scalar1=MAGIC, scalar2=None, op0=ALU.subtract), reads=[R_tt], writes=[R_tt])
                            fw.op("dve", lambda h: h.scalar_tensor_tensor(out=tA[:], in0=iota_t[:], scalar=wcol, in1=tB[:], op0=ALU.mult, op1=ALU.subtract), reads=[R_const, R_pp, R_tt], writes=[R_tt])
                            fw.op("act", lambda h: h.activation(out=tsin[q][:], in_=tA[:], func=AF.Sin, scale=TWO_PI), reads=[R_tt], writes=[R_tab[q]])
                            fw.op("dve", lambda h: h.scalar_tensor_tensor(out=tB[:], in0=tA[:], scalar=-1.0, in1=tA[:], op0=ALU.mult, op1=ALU.max), reads=[R_tt], writes=[R_tt])
                            fw.op("act", lambda h: h.activation(out=tcos[q][:], in_=tB[:], func=AF.Sin, scale=-TWO_PI, bias=halfpi[:]), reads=[R_tt, R_const], writes=[R_tab[q]])
                        units = []
                        for seg in range(NT):
                            for q in range(4):
                                units.append(dict(q=q, pair=cb * 4 + q, n=ucount[0], seg=seg, sg=cb * NT + seg))
                                ucount[0] += 1

                        def sA(u):
                            if u["q"] != 0:
                                return
                            si = u["sg"] % NU
                            proj_fm(0, w_u[cb], R_wu[cb], u["seg"])
                            fw.op("act", lambda h: h.copy(out=utf[si][:], in_=pb[0][:, :]), reads=[R_pb[0]], writes=[R_utf[si]])
                            fw.op("pool", lambda h: h.tensor_copy(out=utb[si][:], in_=utf[si][:]), reads=[R_utf[si]], writes=[R_utb[si]])

                        def st0(u):
                            k = u["n"] % 2
                            bR, bI = 1 + 2 * k, 2 + 2 * k
                            u["bR"], u["bI"] = bR, bI
                            pair = u["pair"]
                            si = u["sg"] % NU
                            fw.op("pe", lambda h: h.matmul(pb[bR][:, :], lhsT=BTr[:, pair, :], rhs=utb[si][:], start=True, stop=True), reads=[R_BT, R_utb[si]], writes=[R_pb[bR]])
                            fw.op("pe", lambda h: h.matmul(pb[bI][:, :], lhsT=BTi[:, pair, :], rhs=utb[si][:], start=True, stop=True), reads=[R_BT, R_utb[si]], writes=[R_pb[bI]])

                        def st1(u):
                            i = u["n"] % NR
                            q = u["q"]
                            bR, bI = u["bR"], u["bI"]
                            m = mm[i]
                            fw.op("dve", lambda h: h.tensor_tensor(out=m[0][:], in0=pb[bR][:, :], in1=tcos[q][:], op=ALU.mult), reads=[R_pb[bR], R_tab[q]], writes=[R_mm[i]])
                            fw.op("dve", lambda h: h.tensor_tensor(out=m[3][:], in0=pb[bR][:, :], in1=tsin[q][:], op=ALU.mult), reads=[R_pb[bR], R_tab[q]], writes=[R_mm[i]])
                            fw.op("dve", lambda h: h.tensor_tensor(out=m[1][:], in0=pb[bI][:, :], in1=tsin[q][:], op=ALU.mult), reads=[R_pb[bI], R_tab[q]], writes=[R_mm[i]])
                            fw.op("dve", lambda h: h.tensor_tensor(out=m[2][:], in0=pb[bI][:, :], in1=tcos[q][:], op=ALU.mult), reads=[R_pb[bI], R_tab[q]], writes=[R_mm[i]])

                        def st2(u):
                            i = u["n"] % NR
                            m = mm[i]
                            fw.op("pool", lambda h: h.tensor_tensor(out=bp[i][0][:], in0=m[0][:], in1=m[1][:], op=ALU.add), reads=[R_mm[i]], writes=[R_bp[i]])
                            fw.op("pool", lambda h: h.tensor_tensor(out=bp[i][1][:], in0=m[2][:], in1=m[3][:], op=ALU.subtract), reads=[R_mm[i]], writes=[R_bp[i]])

                        def st3(u):
                            i = u["n"] % NR
                            pair = u["pair"]
                            rb = rho[:, pair:pair + 1].broadcast_to([P, TT])
                            Rc = R_carry[pair]
                            fw.op("dve", lambda h: h.tensor_tensor_scan(out=yy[i][0][:], data0=rb, data1=bp[i][0][:], initial=carry[:, pair, 0:1], op0=ALU.mult, op1=ALU.add), reads=[R_bp[i], R_pp, Rc], writes=[R_yy[i]])
                            fw.op("dve", lambda h: h.tensor_tensor_scan(out=yy[i][1][:], data0=rb, data1=bp[i][1][:], initial=carry[:, pair, 1:2], op0=ALU.mult, op1=ALU.add), reads=[R_bp[i], R_pp, Rc], writes=[R_yy[i]])
                            yl_r = yy[i][0][:, TT - 1:TT]
                            yl_i = yy[i][1][:, TT - 1:TT]
                            fw.op("dve", lambda h: h.tensor_tensor(out=ctmp[:, 0:1], in0=yl_i, in1=s512[:, pair:pair + 1], op=ALU.mult), reads=[R_yy[i], R_pp], writes=[R_ctmp])
                            fw.op("dve", lambda h: h.tensor_tensor(out=ctmp[:, 1:2], in0=yl_r, in1=s512[:, pair:pair + 1], op=ALU.mult), reads=[R_yy[i], R_pp], writes=[R_ctmp])
                            fw.op("dve", lambda h: h.scalar_tensor_tensor(out=carry[:, pair, 0:1], in0=yl_r, scalar=c512[:, pair:pair + 1], in1=ctmp[:, 0:1], op0=ALU.mult, op1=ALU.subtract), reads=[R_yy[i], R_pp, R_ctmp], writes=[Rc])
                            fw.op("dve", lambda h: h.scalar_tensor_tensor(out=carry[:, pair, 1:2], in0=yl_i, scalar=c512[:, pair:pair + 1], in1=ctmp[:, 1:2], op0=ALU.mult, op1=ALU.add), reads=[R_yy[i], R_pp, R_ctmp], writes=[Rc])

                        def st4(u):
                            i = u["n"] % NR
                            q = u["q"]
                            xi = u["sg"] % 2
                            yr, yi = yy[i][0], yy[i][1]
                            fw.op("pool", lambda h: h.tensor_tensor(out=xA[xi][:, q, :], in0=yr[:], in1=tcos[q][:], op=ALU.mult), reads=[R_yy[i], R_tab[q]], writes=[R_xx[xi][q]])
                            fw.op("pool", lambda h: h.tensor_tensor(out=xB[xi][:, q, :], in0=yi[:], in1=tsin[q][:], op=ALU.mult), reads=[R_yy[i], R_tab[q]], writes=[R_xx[xi][q]])
                            fw.op("pool", lambda h: h.tensor_tensor(out=xC[xi][:, q, :], in0=yi[:], in1=tcos[q][:], op=ALU.mult), reads=[R_yy[i], R_tab[q]], writes=[R_xx[xi][q]])
                            fw.op("pool", lambda h: h.tensor_tensor(out=xD[xi][:, q, :], in0=yr[:], in1=tsin[q][:], op=ALU.mult), reads=[R_yy[i], R_tab[q]], writes=[R_xx[xi][q]])

                        def sG(u):
                            if u["q"] != 3:
                                return
                            seg = u["seg"]
                            si = u["sg"] % NU
                            xi = u["sg"] % 2
                            yi_ = u["sg"] % 2
                            for q in range(4):
                                lr_ = CTr[:, cb, q, :, :].rearrange("p a b -> p (a b)")
                                ln_ = CTn[:, cb, q, :, :].rearrange("p a b -> p (a b)")
                                li_ = CTi[:, cb, q, :, :].rearrange("p a b -> p (a b)")
                                fw.op("pe", lambda h: h.matmul(pb[5][:, :], lhsT=lr_, rhs=xA[xi][:, q, :], start=(q == 0), stop=False), reads=[R_CT, R_xx[xi][q]], writes=[R_pb[5]])
                                fw.op("pe", lambda h: h.matmul(pb[5][:, :], lhsT=ln_, rhs=xB[xi][:, q, :], start=False, stop=False), reads=[R_CT, R_xx[xi][q]], writes=[R_pb[5]])
                                fw.op("pe", lambda h: h.matmul(pb[5][:, :], lhsT=li_, rhs=xC[xi][:, q, :], start=False, stop=False), reads=[R_CT, R_xx[xi][q]], writes=[R_pb[5]])
                                fw.op("pe", lambda h: h.matmul(pb[5][:, :], lhsT=li_, rhs=xD[xi][:, q, :], start=False, stop=(q == 3)), reads=[R_CT, R_xx[xi][q]], writes=[R_pb[5]])
                            fw.op("dve", lambda h: h.scalar_tensor_tensor(out=yv[yi_][:], in0=utf[si][:], scalar=dsk[:, cb:cb + 1], in1=pb[5][:, :], op0=ALU.mult, op1=ALU.add),
                                  reads=[R_utf[si], R_pb[5], RL[7]], writes=[R_yv[yi_]])
                            dump("dbg_ssm_y", yv[yi_][:], R_yv[yi_], lambda o: o[cb * P:(cb + 1) * P, seg * TT:(seg + 1) * TT])
                            fw.op("act", lambda h: h.activation(out=g1[:], in_=yv[yi_][:], func=AF.Square), reads=[R_yv[yi_]], writes=[R_g])
                            fw.op("dve", lambda h: h.tensor_scalar(out=g1[:], in0=g1[:], scalar1=0.044715, scalar2=1.0, op0=ALU.mult, op1=ALU.add), reads=[R_g], writes=[R_g])
                            fw.op("pool", lambda h: h.tensor_tensor(out=g1[:], in0=g1[:], in1=yv[yi_][:], op=ALU.mult), reads=[R_g, R_yv[yi_]], writes=[R_g])
                            fw.op("act", lambda h: h.activation(out=g2[:], in_=g1[:], func=AF.Sigmoid, scale=1.5957691216057308), reads=[R_g], writes=[R_g])
                            fw.op("pool", lambda h: h.tensor_tensor(out=ygb[yi_][:], in0=g2[:], in1=yv[yi_][:], op=ALU.mult), reads=[R_g, R_yv[yi_]], writes=[R_ygb[yi_]])
                            fw.dma(yg_d[:, cb, seg * TT:(seg + 1) * TT], ygb[yi_][:], reads=[R_ygb[yi_]], writes=[R_yg[seg]], semres=R_ygb[yi_])

                        run_pipeline(units, [sA, st0, st1, st2, st3, st4, sG])
                    fw.barrier()

            with ExitStack() as ps:
                w_sg = [sbt(ps, f"w_sg{i}", [P, DC, P], BF16) for i in range(4)]
                R_wsg = [Res(f"w_sg{i}") for i in range(4)]
                for cb in range(4):
                    load_wblock(l, 4 + cb, w_sg[cb][:], R_wsg[cb])
                wglu = sbt(ps, "wglu", [P, 4, 512], BF16)
                R_wglu = Res("wglu")
                for ob in range(4):
                    load_generic(w_glu_d[l, :, :, ob * P:(ob + 1) * P], wglu[:, :, ob * P:(ob + 1) * P], R_wglu, (4, P))
                bglu = sbt(ps, "bglu", [P, 4], F32)
                R_bglu = Res("bglu")
                fw.dma(bglu[:], bglu_d[l], writes=[R_bglu])
                ygt = [sbt(ps, f"ygt{i}", [P, 4, TT], BF16) for i in range(2)]
                R_ygt = [Res(f"ygt{i}") for i in range(2)]
                s1 = [sbt(ps, f"s1_{i}", [P, TT], F32) for i in range(2)]
                s2 = [sbt(ps, f"s2_{i}", [P, TT], F32) for i in range(2)]
                R_s = [Res(f"s12_{i}") for i in range(2)]
                osb = [sbt(ps, f"osb{i}", [P, 4, TT], BF16) for i in range(2)]
                R_osb = [Res(f"osb{i}") for i in range(2)]
                cnt = 0
                for j in range(NT):
                    yi_ = j % 2
                    fw.dma(ygt[yi_][:], yg_d[:, :, j * TT:(j + 1) * TT], reads=[R_yg[j]], writes=[R_ygt[yi_]])
                    for ob in range(4):
                        k = cnt % 2
                        cnt += 1
                        bz, bg = 6 + k, 0 + k
                        for c in range(4):
                            fw.op("pe", lambda h, c=c, ob=ob, bz=bz: h.matmul(pb[bz][:, :], lhsT=wglu[:, c, ob * P:(ob + 1) * P], rhs=ygt[yi_][:, c, :], start=(c == 0), stop=(c == 3)),
                                  reads=[R_wglu, R_ygt[yi_]], writes=[R_pb[bz]])
                        proj_fm(bg, w_sg[ob], R_wsg[ob], j)
                        fw.op("act", lambda h, k=k, ob=ob, bz=bz: h.activation(out=s1[k][:], in_=pb[bz][:, :], func=AF.Sigmoid, bias=bglu[:, ob:ob + 1]), reads=[R_pb[bz], R_bglu], writes=[R_s[k]])
                        fw.op("act", lambda h, k=k, bg=bg: h.activation(out=s2[k][:], in_=pb[bg][:, :], func=AF.Silu), reads=[R_pb[bg]], writes=[R_s[k]])
                        fw.op("dve", lambda h, k=k, ob=ob: h.tensor_tensor(out=s1[k][:], in0=s1[k][:], in1=ygt[yi_][:, ob, :], op=ALU.mult), reads=[R_s[k], R_ygt[yi_]], writes=[R_s[k]])
                        fw.op("dve", lambda h, k=k, ob=ob: h.tensor_tensor(out=osb[yi_][:, ob, :], in0=s1[k][:], in1=s2[k][:], op=ALU.mult), reads=[R_s[k]], writes=[R_osb[yi_]])
                    fw.dma(obr_d[0, :, :, j * TT:(j + 1) * TT], osb[yi_][:], reads=[R_osb[yi_]], writes=[R_obr[0][j]], semres=R_osb[yi_])
                    if "dbg_o0" in dbg_out:
                        pass
                fw.barrier()

        def sb_phase(l):
            with ExitStack() as ps:
                wq = sbt(ps, "sb_wq", [P, DC, P], BF16)
                wk = sbt(ps, "sb_wk", [P, DC, P], BF16)
                wv = sbt(ps, "sb_wv", [P, DC, P], BF16)
                wg = sbt(ps, "sb_wg", [P, DC, P], BF16)
                R_wq, R_wk, R_wv, R_wg = Res("sb_wq"), Res("sb_wk"), Res("sb_wv"), Res("sb_wg")
                qT = sbt(ps, "sb_qT", [P, S], BF16)
                kT = sbt(ps, "sb_kT", [P, S], BF16)
                V = sbt(ps, "sb_V", [P, NB, P], BF16)
                R_q = [Res(f"sb_q{j}") for j in range(NT)]
                R_k = [Res(f"sb_k{j}") for j in range(NT)]
                R_v = [Res(f"sb_v{j}") for j in range(NT)]
                NR = 3
                e1 = [sbt(ps, f"sb_e1_{i}", [P, TT], F32) for i in range(NR)]
                sp = [sbt(ps, f"sb_sp_{i}", [P, TT], BF16) for i in range(NR)]
                ww = [sbt(ps, f"sb_w_{i}", [P, TT], BF16) for i in range(NR)]
                R_e1 = [Res(f"sb_e1_{i}") for i in range(NR)]
                R_sp = [Res(f"sb_sp_{i}") for i in range(NR)]
                R_ww = [Res(f"sb_w_{i}") for i in range(NR)]
                NA = 3
                acc = [sbt(ps, f"sb_acc{i}", [P, TT], BF16) for i in range(NA)]
                R_acc = [Res(f"sb_acc{i}") for i in range(NA)]
                sg = [sbt(ps, f"sb_sg{i}", [P, TT], F32) for i in range(2)]
                R_sg = [Res(f"sb_sg{i}") for i in range(2)]
                og = [sbt(ps, f"sb_og{i}", [P, TT], BF16) for i in range(2)]
                R_og = [Res(f"sb_og{i}") for i in range(2)]
                tcount = [0]
                gcount = [0]
                for hp in range(4):
                    load_wblock(l, 8 + hp, wq[:], R_wq)
                    load_wblock(l, 12 + hp, wk[:], R_wk)
                    load_wblock(l, 16 + hp, wv[:], R_wv)
                    load_wblock(l, 20 + hp, wg[:], R_wg)
                    for j in range(NT):
                        proj_fm(6, wq, R_wq, j)
                        fw.op("act", lambda h, j=j: h.activation(out=qT[:, j * TT:(j + 1) * TT], in_=pb[6][:, :], func=AF.Copy, scale=0.125), reads=[R_pb[6]], writes=[R_q[j]])
                        proj_fm(7, wk, R_wk, j)
                        fw.op("dve", lambda h, j=j: h.tensor_copy(out=kT[:, j * TT:(j + 1) * TT], in_=pb[7][:, :]), reads=[R_pb[7]], writes=[R_k[j]])
                    for j in range(NT):
                        bank = 6 + (j % 2)
                        for b4 in range(4):
                            blk = j * 4 + b4
                            for c in range(DC):
                                fw.op("pe", lambda h, c=c, blk=blk, b4=b4, bank=bank: h.matmul(pb[bank][:, b4 * P:(b4 + 1) * P], lhsT=hT[:, c, blk * P:(blk + 1) * P], rhs=wv[:, c, :], start=(c == 0), stop=(c == DC - 1)),
                                      reads=[R_wv, R_hT[j]], writes=[R_pb[bank]])
                        src = pb[bank][:, :].rearrange("p (b n) -> p b n", b=4)
                        if j % 2 == 0:
                            fw.op("act", lambda h, j=j, src=src: h.copy(out=V[:, j * 4:(j + 1) * 4, :], in_=src), reads=[R_pb[bank]], writes=[R_v[j]])
                        else:
                            fw.op("dve", lambda h, j=j, src=src: h.tensor_copy(out=V[:, j * 4:(j + 1) * 4, :], in_=src), reads=[R_pb[bank]], writes=[R_v[j]])
                    units = []
                    for qt in range(NT):
                        for e in range(2):
                            ai = gcount[0] % NA
                            gcount[0] += 1
                            kbs = list(range(4 * qt + 3, -1, -1))
                            for idx, kb in enumerate(kbs):
                                jd = kb - 4 * qt
                                u = dict(qt=qt, e=e, kb=kb, col0=(P * jd if jd >= 0 else 0), diag=(jd >= 0), first=(idx == 0), last=(kb == 0),
                                         ai=ai, n=tcount[0], ob=4 + (qt % 2))
                                tcount[0] += 1
                                units.append(u)

                    def s0(u):
                        k = u["n"] % 2
                        u["bA"] = 0 + k
                        u["bB"] = 2 + k
                        bA, e, kb, qt, c0 = u["bA"], u["e"], u["kb"], u["qt"], u["col0"]
                        pr = slice(64 * e, 64 * e + 64)
                        t0 = qt * TT
                        if u["first"]:
                            fw.op("pool", lambda h: h.memset(acc[u["ai"]][:], 0.0), writes=[R_acc[u["ai"]]])
                        fw.op("pe", lambda h: h.matmul(pb[bA][:, c0:TT], lhsT=kT[pr, kb * P:(kb + 1) * P], rhs=qT[pr, t0 + c0:t0 + TT], start=True, stop=not u["diag"]),
                              reads=[R_k[kb // 4], R_q[qt]], writes=[R_pb[bA]])
                        if u["diag"]:
                            fw.op("pe", lambda h: h.matmul(pb[bA][:, c0:c0 + P], lhsT=ident_b[:], rhs=mask_sb[:], start=False, stop=True), reads=[R_const], writes=[R_pb[bA]])

                    def s1(u):
                        i = u["n"] % NR
                        bA, c0 = u["bA"], u["col0"]
                        fw.op("act", lambda h: h.activation(out=e1[i][:, c0:TT], in_=pb[bA][:, c0:TT], func=AF.Exp), reads=[R_pb[bA]], writes=[R_e1[i]])
                        fw.op("act", lambda h: h.activation(out=sp[i][:, c0:TT], in_=e1[i][:, c0:TT], func=AF.Ln, bias=1.0), reads=[R_e1[i]], writes=[R_sp[i]])

                    def s2(u):
                        i = u["n"] % NR
                        bB, e, kb, qt, c0 = u["bB"], u["e"], u["kb"], u["qt"], u["col0"]
                        pr = slice(64 * e, 64 * e + 64)
                        t0 = qt * TT
                        fw.op("pe", lambda h: h.matmul(pb[bB][:, c0:TT], lhsT=kT[pr, kb * P:(kb + 1) * P], rhs=qT[pr, t0 + c0:t0 + TT], start=True, stop=False),
                              reads=[R_k[kb // 4], R_q[qt]], writes=[R_pb[bB]])
                        lastmm = u["first"] and not u["diag"]
                        fw.op("pe", lambda h: h.matmul(pb[bB][:, c0:TT], lhsT=ntri_b[:], rhs=sp[i][:, c0:TT], start=False, stop=lastmm), reads=[R_const, R_sp[i]], writes=[R_pb[bB]])
                        if not u["first"]:
                            fw.op("pe", lambda h: h.matmul(pb[bB][:, c0:TT], lhsT=nones_b[:], rhs=acc[u["ai"]][:, c0:TT], start=False, stop=not u["diag"]),
                                  reads=[R_const, R_acc[u["ai"]]], writes=[R_pb[bB]])
                        if u["diag"]:
                            fw.op("pe", lambda h: h.matmul(pb[bB][:, c0:c0 + P], lhsT=ident_b[:], rhs=mask_sb[:], start=False, stop=True), reads=[R_const], writes=[R_pb[bB]])

                    def s3(u):
                        i = u["n"] % NR
                        bB, c0 = u["bB"], u["col0"]
                        fw.op("act", lambda h: h.activation(out=ww[i][:, c0:TT], in_=pb[bB][:, c0:TT], func=AF.Exp), reads=[R_pb[bB]], writes=[R_ww[i]])
                        if not u["last"]:
                            a = acc[u["ai"]]
                            fw.op("pool", lambda h: h.tensor_tensor(out=a[:, c0:TT], in0=a[:, c0:TT], in1=sp[i][:, c0:TT], op=ALU.add), reads=[R_sp[i], R_acc[u["ai"]]], writes=[R_acc[u["ai"]]])

                    def s4(u):
                        i = u["n"] % NR
                        e, kb, qt, c0, ob = u["e"], u["kb"], u["qt"], u["col0"], u["ob"]
                        pr = slice(64 * e, 64 * e + 64)
                        fw.op("pe", lambda h: h.matmul(pb[ob][pr, c0:TT], lhsT=V[:, kb, 64 * e:64 * e + 64], rhs=ww[i][:, c0:TT], start=u["first"], stop=u["last"]),
                              reads=[R_v[kb // 4], R_ww[i]], writes=[R_pb[ob]])
                        if u["last"] and e == 1:
                            k = qt % 2
                            proj_fm(6 + k, wg, R_wg, qt)
                            fw.op("act", lambda h: h.activation(out=sg[k][:], in_=pb[6 + k][:, :], func=AF.Silu), reads=[R_pb[6 + k]], writes=[R_sg[k]])
                            fw.op("dve", lambda h: h.tensor_tensor(out=og[k][:], in0=pb[ob][:, :], in1=sg[k][:], op=ALU.mult), reads=[R_pb[ob], R_sg[k]], writes=[R_og[k]])
                            fw.dma(obr_d[1, :, hp, qt * TT:(qt + 1) * TT], og[k][:], reads=[R_og[k]], writes=[R_obr[1][qt]], semres=R_og[k])

                    run_pipeline(units, [s0, s1, s2, s3, s4])
                fw.barrier()

        def diff_phase(l):
            lam_init = 0.8 - 0.6 * math.exp(-0.3 * l)
            with ExitStack() as ps:
                names = ["wq", "wqr", "wk", "wkr", "wv", "wg"]
                wt = {n: sbt(ps, "df_" + n, [P, DC, P], BF16) for n in names}
                R_w = {n: Res("df_" + n) for n in names}
                qT = sbt(ps, "df_qT", [P, S], BF16)
                kT = sbt(ps, "df_kT", [P, S], BF16)
                V = sbt(ps, "df_V", [P, NB, P], BF16)
                R_q = [Res(f"df_q{j}") for j in range(NT)]
                R_k = [Res(f"df_k{j}") for j in range(NT)]
                R_v = [Res(f"df_v{j}") for j in range(NT)]
                rc = [sbt(ps, f"df_rc{i}", [P, TT], F32) for i in range(2)]
                rs = [sbt(ps, f"df_rs{i}", [P, TT], F32) for i in range(2)]
                R_rc = [Res(f"df_rc{i}") for i in range(2)]
                R_rs = [Res(f"df_rs{i}") for i in range(2)]
                ta = [sbt(ps, f"df_ta{i}", [P, TT], F32) for i in range(2)]
                tb = [sbt(ps, f"df_tb{i}", [P, TT], F32) for i in range(2)]
                R_ta = [Res(f"df_ta{i}") for i in range(2)]
                NR = 3
                pp = [sbt(ps, f"df_p{i}", [P, TT], BF16) for i in range(NR)]
                R_pp_ = [Res(f"df_p{i}") for i in range(NR)]
                rr = sbt(ps, "df_rr", [P, TT], F32)
                R_rr = Res("df_rr")
                o12 = [[sbt(ps, f"df_o{i}_{k}", [P, TT], F32) for i in range(2)] for k in range(2)]
                R_o12 = [[Res(f"df_o{i}_{k}") for i in range(2)] for k in range(2)]
                oo = sbt(ps, "df_oo", [P, TT], F32)
                osq = sbt(ps, "df_osq", [P, TT], F32)
                rstd = sbt(ps, "df_rstd", [P, TT], F32)
                sg = sbt(ps, "df_sg", [P, TT], F32)
                R_fin = Res("df_fin")
                R_sgd = Res("df_sg")
                ofin = [sbt(ps, f"df_ofin{i}", [P, TT], BF16) for i in range(2)]
                R_ofin = [Res(f"df_ofin{i}") for i in range(2)]
                lqk = sbt(ps, "df_lqk", [P, 4, 64], F32)
                R_lqk = Res("df_lqk")
                fw.dma(lqk[:], lqk_d[l], writes=[R_lqk])
                sub = sbt(ps, "df_sub", [P, 1], F32)
                R_sub = Res("df_sub")
                fw.dma(sub[:], subln_d[l], writes=[R_sub])
                pr1 = sbt(ps, "df_pr1", [P, 64], F32)
                sm = sbt(ps, "df_sm", [P, 4], F32)
                R_lam = Res("df_lam")
                fw.op("dve", lambda h: h.tensor_tensor(out=pr1[:], in0=lqk[:, 0, :], in1=lqk[:, 1, :], op=ALU.mult), reads=[R_lqk], writes=[R_lam])
                fw.op("dve", lambda h: h.reduce_sum(out=sm[:, 0:1], in_=pr1[:], axis=AX.X), reads=[R_lam], writes=[R_lam])
                fw.op("dve", lambda h: h.tensor_tensor(out=pr1[:], in0=lqk[:, 2, :], in1=lqk[:, 3, :], op=ALU.mult), reads=[R_lqk, R_lam], writes=[R_lam])
                fw.op("dve", lambda h: h.reduce_sum(out=sm[:, 1:2], in_=pr1[:], axis=AX.X), reads=[R_lam], writes=[R_lam])
                fw.op("act", lambda h: h.activation(out=sm[:, 0:2], in_=sm[:, 0:2], func=AF.Exp), reads=[R_lam], writes=[R_lam])
                fw.op("dve", lambda h: h.tensor_tensor(out=sm[:, 2:3], in0=sm[:, 1:2], in1=sm[:, 0:1], op=ALU.subtract), reads=[R_lam], writes=[R_lam])
                fw.op("dve", lambda h: h.tensor_scalar(out=sm[:, 2:3], in0=sm[:, 2:3], scalar1=-lam_init, scalar2=None, op0=ALU.add), reads=[R_lam], writes=[R_lam])
                fw.op("dve", lambda h: h.tensor_scalar(out=sub[:], in0=sub[:], scalar1=(1.0 - lam_init), scalar2=None, op0=ALU.mult), reads=[R_sub], writes=[R_sub])
                nlam = sm[:, 2:3]
                tcount = [0]
                gcount = [0]
                for hd in range(4):
                    load_wblock(l, 24 + hd, wt["wq"][:], R_w["wq"])
                    load_wblock(l, 24 + hd, wt["wqr"][:], R_w["wqr"], rot=True)
                    load_wblock(l, 28 + hd, wt["wk"][:], R_w["wk"])
                    load_wblock(l, 28 + hd, wt["wkr"][:], R_w["wkr"], rot=True)
                    load_wblock(l, 32 + hd, wt["wv"][:], R_w["wv"])
                    load_wblock(l, 36 + hd, wt["wg"][:], R_w["wg"])
                    for j in range(NT):
                        i = j % 2
                        fw.dma(rc[i][:], ropeC_d[:, j * TT:(j + 1) * TT], reads=[R_ropeC[j]], writes=[R_rc[i]])
                        fw.dma(rs[i][:], ropeS_d[:, j * TT:(j + 1) * TT], reads=[R_ropeS[j]], writes=[R_rs[i]])
                        for which, (wn, wr, dstT, R_dst, scl) in enumerate((("wq", "wqr", qT, R_q, 0.125), ("wk", "wkr", kT, R_k, 1.0))):
                            k = which
                            proj_fm(6, wt[wn], R_w[wn], j)
                            proj_fm(7, wt[wr], R_w[wr], j)
                            fw.op("dve", lambda h, k=k, scl=scl: h.scalar_tensor_tensor(out=ta[k][:], in0=pb[6][:, :], scalar=scl, in1=rc[i][:], op0=ALU.mult, op1=ALU.mult), reads=[R_pb[6], R_rc[i]], writes=[R_ta[k]])
                            fw.op("dve", lambda h, k=k, scl=scl: h.scalar_tensor_tensor(out=tb[k][:], in0=pb[7][:, :], scalar=scl, in1=rs[i][:], op0=ALU.mult, op1=ALU.mult), reads=[R_pb[7], R_rs[i]], writes=[R_ta[k]])
                            fw.op("pool", lambda h, k=k, dstT=dstT, j=j: h.tensor_tensor(out=dstT[:, j * TT:(j + 1) * TT], in0=ta[k][:], in1=tb[k][:], op=ALU.add), reads=[R_ta[k]], writes=[R_dst[j]])
                    for j in range(NT):
                        bank = 6 + (j % 2)
                        for b4 in range(4):
                            blk = j * 4 + b4
                            for c in range(DC):
                                fw.op("pe", lambda h, c=c, blk=blk, b4=b4, bank=bank: h.matmul(pb[bank][:, b4 * P:(b4 + 1) * P], lhsT=hT[:, c, blk * P:(blk + 1) * P], rhs=wt["wv"][:, c, :], start=(c == 0), stop=(c == DC - 1)),
                                      reads=[R_w["wv"], R_hT[j]], writes=[R_pb[bank]])
                        src = pb[bank][:, :].rearrange("p (b n) -> p b n", b=4)
                        if j % 2 == 0:
                            fw.op("act", lambda h, j=j, src=src: h.copy(out=V[:, j * 4:(j + 1) * 4, :], in_=src), reads=[R_pb[bank]], writes=[R_v[j]])
                        else:
                            fw.op("dve", lambda h, j=j, src=src: h.tensor_copy(out=V[:, j * 4:(j + 1) * 4, :], in_=src), reads=[R_pb[bank]], writes=[R_v[j]])
                    if "dbg_dfq" in dbg_out and hd == 0:
                        pass
                    units = []
                    for qt in range(NT):
                        for i in range(2):
                            gi = gcount[0] % 2
                            gcount[0] += 1
                            nkb = 4 * qt + 4
                            for kb in range(nkb):
                                jd = kb - 4 * qt
                                units.append(dict(qt=qt, i=i, kb=kb, col0=(P * jd if jd >= 0 else 0), diag=(jd >= 0), first=(kb == 0), last=(kb == nkb - 1),
                                                  n=tcount[0], bN=2 + 2 * gi, bD=3 + 2 * gi, gi=gi))
                                tcount[0] += 1

                    def s0(u):
                        k = u["n"] % 2
                        u["bA"] = k
                        bA, i, kb, qt, c0 = u["bA"], u["i"], u["kb"], u["qt"], u["col0"]
                        pr = slice(64 * i, 64 * i + 64)
                        t0 = qt * TT
                        fw.op("pe", lambda h: h.matmul(pb[bA][:, c0:TT], lhsT=kT[pr, kb * P:(kb + 1) * P], rhs=qT[pr, t0 + c0:t0 + TT], start=True, stop=not u["diag"]),
                              reads=[R_k[kb // 4], R_q[qt]], writes=[R_pb[bA]])
                        if u["diag"]:
                            fw.op("pe", lambda h: h.matmul(pb[bA][:, c0:c0 + P], lhsT=ident_b[:], rhs=mask_df[:], start=False, stop=True), reads=[R_const], writes=[R_pb[bA]])

                    def s1(u):
                        ii = u["n"] % NR
                        bA, c0 = u["bA"], u["col0"]
                        fw.op("act", lambda h: h.activation(out=pp[ii][:, c0:TT], in_=pb[bA][:, c0:TT], func=AF.Exp), reads=[R_pb[bA]], writes=[R_pp_[ii]])

                    def s2(u):
                        ii = u["n"] % NR
                        kb, qt, c0, bN, bD, i = u["kb"], u["qt"], u["col0"], u["bN"], u["bD"], u["i"]
                        fw.op("pe", lambda h: h.matmul(pb[bN][:, c0:TT], lhsT=V[:, kb, :], rhs=pp[ii][:, c0:TT], start=u["first"], stop=u["last"]), reads=[R_v[kb // 4], R_pp_[ii]], writes=[R_pb[bN]])
                        fw.op("pe", lambda h: h.matmul(pb[bD][:, c0:TT], lhsT=ones_b[:], rhs=pp[ii][:, c0:TT], start=u["first"], stop=u["last"]), reads=[R_const, R_pp_[ii]], writes=[R_pb[bD]])
                        if u["last"]:
                            k2 = qt % 2
                            o_i = o12[k2][i]
                            fw.op("dve", lambda h: h.reciprocal(out=rr[:], in_=pb[bD][:, :]), reads=[R_pb[bD]], writes=[R_rr])
                            fw.op("dve", lambda h: h.tensor_tensor(out=o_i[:], in0=pb[bN][:, :], in1=rr[:], op=ALU.mult), reads=[R_pb[bN], R_rr], writes=[R_o12[k2][i]])
                            if i == 1:
                                o1, o2 = o12[k2][0], o12[k2][1]
                                fw.op("dve", lambda h: h.scalar_tensor_tensor(out=oo[:], in0=o2[:], scalar=nlam, in1=o1[:], op0=ALU.mult, op1=ALU.add), reads=[R_o12[k2][0], R_o12[k2][1], R_lam], writes=[R_fin])
                                fw.op("act", lambda h: h.activation(out=osq[:], in_=oo[:], func=AF.Square), reads=[R_fin], writes=[R_fin])
                                fw.op("pe", lambda h: h.matmul(pb[6][:, :], lhsT=ones_f[:], rhs=osq[:], start=True, stop=True), reads=[R_fin, R_const], writes=[R_pb[6]])
                                rsqrt_from_psum(6, 1.0 / P, rstd[:], R_fin, rstd[:])
                                proj_fm(7, wt["wg"], R_w["wg"], qt)
                                fw.op("act", lambda h: h.activation(out=sg[:], in_=pb[7][:, :], func=AF.Silu), reads=[R_pb[7]], writes=[R_sgd])
                                fw.op("dve", lambda h: h.tensor_tensor(out=oo[:], in0=oo[:], in1=rstd[:], op=ALU.mult), reads=[R_fin], writes=[R_fin])
                                fw.op("dve", lambda h: h.scalar_tensor_tensor(out=ofin[k2][:], in0=oo[:], scalar=sub[:, 0:1], in1=sg[:], op0=ALU.mult, op1=ALU.mult), reads=[R_fin, R_sub, R_sgd], writes=[R_ofin[k2]])
                                fw.dma(obr_d[2, :, hd, qt * TT:(qt + 1) * TT], ofin[k2][:], reads=[R_ofin[k2]], writes=[R_obr[2][qt]], semres=R_ofin[k2])

                    run_pipeline(units, [s0, s1, s2])
                fw.barrier()

        def merge_phase(l, last):
            with ExitStack() as ps:
                wm = sbt(ps, "mg_wm", [P, DC, 3 * D], BF16)
                R_wm = [Res(f"mg_wm{i}") for i in range(24)]
                wbr = sbt(ps, "mg_wbr", [P, 3, 4, D], BF16)
                R_wbr = Res("mg_wbr")
                bm = sbt(ps, "mg_bm", [P, 24], F32)
                R_bm = Res("mg_bm")
                fw.dma(bm[:], bmerge_d[l], writes=[R_bm])
                for cb in range(24):
                    load_wblock(l, 40 + cb, wm[:, :, cb * P:(cb + 1) * P], R_wm[cb], eng=("pool" if cb % 2 == 0 else "dve"))
                for n in range(3):
                    for mb in range(4):
                        load_generic(w_br_d[l, n, :, :, mb * 256:(mb + 1) * 256], wbr[:, n, :, mb * 256:(mb + 1) * 256], R_wbr, (4, 256), eng=("pool" if mb % 2 == 0 else "dve"))
                ot = [[sbt(ps, f"mg_ot{n}_{i}", [P, 4, TT], BF16) for n in range(3)] for i in range(2)]
                R_ot = [[Res(f"mg_ot{n}_{i}") for n in range(3)] for i in range(2)]
                mg = [sbt(ps, f"mg_mg{i}", [P, DC, TT], BF16) for i in range(2)]
                R_mgt = [Res(f"mg_mg{i}") for i in range(2)]
                gt = [sbt(ps, f"mg_gt{i}", [P, TT], F32) for i in range(3)]
                R_gt = [Res(f"mg_gt{i}") for i in range(3)]
                macc = [sbt(ps, f"mg_macc{i}", [P, TT], F32) for i in range(2)]
                R_macc = [Res(f"mg_macc{i}") for i in range(2)]
                pcount = 0
                for j in range(NT):
                    ji = j % 2
                    for n in range(3):
                        fw.dma(ot[ji][n][:], obr_d[n, :, :, j * TT:(j + 1) * TT], reads=[R_obr[n][j]], writes=[R_ot[ji][n]])
                    for db in range(DC):
                        ma = macc[db % 2]
                        R_ma = R_macc[db % 2]
                        for n in range(3):
                            k = pcount % 2
                            pcount += 1
                            bP, bL = 0 + k, 2 + k
                            for c in range(4):
                                fw.op("pe", lambda h: h.matmul(pb[bP][:, :], lhsT=wbr[:, n, c, db * P:(db + 1) * P], rhs=ot[ji][n][:, c, :], start=(c == 0), stop=(c == 3)),
                                      reads=[R_wbr, R_ot[ji][n]], writes=[R_pb[bP]])
                            cbm = n * 8 + db
                            for c in range(DC):
                                fw.op("pe", lambda h: h.matmul(pb[bL][:, :], lhsT=wm[:, c, cbm * P:(cbm + 1) * P], rhs=hT[:, c, j * TT:(j + 1) * TT], start=(c == 0), stop=(c == DC - 1)),
                                      reads=[R_wm[cbm], R_hT[j]], writes=[R_pb[bL]])
                            fw.op("act", lambda h: h.activation(out=gt[n][:], in_=pb[bL][:, :], func=AF.Sigmoid, bias=bm[:, cbm:cbm + 1]), reads=[R_pb[bL], R_bm], writes=[R_gt[n]])
                            if n == 0:
                                fw.op("dve", lambda h: h.tensor_tensor(out=ma[:], in0=pb[bP][:, :], in1=gt[n][:], op=ALU.mult), reads=[R_pb[bP], R_gt[n]], writes=[R_ma])
                            else:
                                fw.op("dve", lambda h: h.tensor_tensor(out=gt[n][:], in0=pb[bP][:, :], in1=gt[n][:], op=ALU.mult), reads=[R_pb[bP], R_gt[n]], writes=[R_gt[n]])
                                if n == 1:
                                    fw.op("pool", lambda h: h.tensor_tensor(out=ma[:], in0=ma[:], in1=gt[n][:], op=ALU.add), reads=[R_gt[n], R_ma], writes=[R_ma])
                                else:
                                    fw.op("pool", lambda h: h.tensor_tensor(out=mg[ji][:, db, :], in0=ma[:], in1=gt[n][:], op=ALU.add), reads=[R_gt[n], R_ma], writes=[R_mgt[ji]])
                    fw.dma(mg_d[:, :, j * TT:(j + 1) * TT], mg[ji][:], reads=[R_mgt[ji]], writes=[R_mgd[j]], semres=R_mgt[ji])
                fw.barrier()
            with ExitStack() as ps:
                wo = sbt(ps, "mg_wo", [P, DC, D], BF16)
                R_wo = Res("mg_wo")
                for mb in range(8):
                    load_generic(w_out_d[l, :, :, mb * P:(mb + 1) * P], wo[:, :, mb * P:(mb + 1) * P], R_wo, (DC, P), eng=("pool" if mb % 2 == 0 else "dve"))
                fnw = sbt(ps, "mg_fnw", [P, DC], F32)
                R_fnw = Res("mg_fnw")
                if last:
                    fw.dma(fnw[:], fnorm_d[:, :], writes=[R_fnw])
                xt = [sbt(ps, f"mg_xt{i}", [P, DC, TT], F32) for i in range(2)]
                R_xt = [Res(f"mg_xt{i}") for i in range(2)]
                mgi = [sbt(ps, f"mg_mgi{i}", [P, DC, TT], BF16) for i in range(2)]
                R_mgi = [Res(f"mg_mgi{i}") for i in range(2)]
                sq = [sbt(ps, f"mg_sq{i}", [P, TT], F32) for i in range(2)]
                R_sq = [Res(f"mg_sq{i}") for i in range(2)]
                rstd = sbt(ps, "mg_rstd", [P, TT], F32)
                R_rstd = Res("mg_rstd")
                ostage = [sbt(ps, f"mg_ost{i}", [P, D], F32) for i in range(2)] if last else None
                R_ost = [Res(f"mg_ost{i}") for i in range(2)]
                for j in range(NT):
                    ji = j % 2
                    x_, R_x = xt[ji], R_xt[ji]
                    fw.dma(mgi[ji][:], mg_d[:, :, j * TT:(j + 1) * TT], reads=[R_mgd[j]], writes=[R_mgi[ji]])
                    fw.dma(x_[:], xT_d[:, :, j * TT:(j + 1) * TT], reads=[R_xT[j]], writes=[R_x])
                    for ob in range(DC):
                        bO = 4 + (ob % 2)
                        for c in range(DC):
                            fw.op("pe", lambda h: h.matmul(pb[bO][:, :], lhsT=wo[:, c, ob * P:(ob + 1) * P], rhs=mgi[ji][:, c, :], start=(c == 0), stop=(c == DC - 1)),
                                  reads=[R_wo, R_mgi[ji]], writes=[R_pb[bO]])
                        fw.op("dve", lambda h: h.tensor_tensor(out=x_[:, ob, :], in0=x_[:, ob, :], in1=pb[bO][:, :], op=ALU.add), reads=[R_pb[bO], R_x], writes=[R_x])
                    dump("dbg_x1", x_[:], R_x, lambda o: o[:, :, j * TT:(j + 1) * TT])
                    norm_tile(x_, R_x, j, sq, R_sq, rstd, R_rstd, 6)
                    if not last:
                        fw.dma(xT_d[:, :, j * TT:(j + 1) * TT], x_[:], reads=[R_x], writes=[R_xT[j]], semres=R_x)
                        fw.op("dve", lambda h: h.tensor_tensor(out=hT[:, :, j * TT:(j + 1) * TT], in0=x_[:], in1=rstd[:].unsqueeze(1).broadcast_to([P, DC, TT]), op=ALU.mult),
                              reads=[R_x, R_rstd], writes=[R_hT[j]])
                    else:
                        for c in range(DC):
                            fw.op("dve", lambda h: h.scalar_tensor_tensor(out=x_[:, c, :], in0=x_[:, c, :], scalar=fnw[:, c:c + 1], in1=rstd[:], op0=ALU.mult, op1=ALU.mult), reads=[R_x, R_rstd, R_fnw], writes=[R_x])
                        for blk in range(4):
                            oi = blk % 2
                            for half in range(2):
                                bank = 0 + 2 * (blk % 2) + half
                                for cc in range(4):
                                    c = half * 4 + cc
                                    fw.op("pe", lambda h: h.transpose(pb[bank][:, cc * P:(cc + 1) * P], x_[:, c, blk * P:(blk + 1) * P], ident_f[:]),
                                          reads=[R_x, R_const], writes=[R_pb[bank]])
                                if half == 0:
                                    fw.op("act", lambda h: h.copy(out=ostage[oi][:, 0:512], in_=pb[bank][:, :]), reads=[R_pb[bank]], writes=[R_ost[oi]])
                                else:
                                    fw.op("dve", lambda h: h.tensor_copy(out=ostage[oi][:, 512:1024], in_=pb[bank][:, :]), reads=[R_pb[bank]], writes=[R_ost[oi]])
                            g = j * 4 + blk
                            fw.dma(y_out[g * P:(g + 1) * P, :], ostage[oi][:], reads=[R_ost[oi]], final=True, semres=R_ost[oi])
                fw.barrier()

        consts()
        phase0()
        for l in range(nlayers):
            if l > 0:
                fw.dma(nw[:], normw_d[l], writes=[R_nw], persistent=True)
            ssm_phase(l)
            if stop_after == "ssm":
                break
            sb_phase(l)
            if stop_after == "sb":
                break
            diff_phase(l)
            if stop_after == "diff":
                break
            merge_phase(l, last=(l == nlayers - 1))
        fw.finish()
        build_program.ninst = fw.ninst
        build_program.nsem = fw.nsem
    return nc


def prep_inputs(x, norm_w, w_in, b_merge, ssm_a_re, ssm_a_im, ssm_log_dt, ssm_b_re, ssm_b_im, ssm_c_re, ssm_c_im, ssm_d,
                ssm_w_glu, ssm_b_glu, diff_lq1, diff_lk1, diff_lq2, diff_lk2, diff_subln_w, w_branch, w_out, final_norm_w):
    f = lambda a: np.ascontiguousarray(np.asarray(a, dtype=np.float32))
    L = DEPTH
    w_in = np.asarray(w_in, dtype=np.float32)
    shared = {}
    shared["w_in"] = f(w_in.reshape(L, DC, P, 64, P).transpose(0, 3, 2, 1, 4))
    shared["w_glu"] = f(np.asarray(ssm_w_glu, np.float32).reshape(L, 4, P, 512).transpose(0, 2, 1, 3))
    shared["w_br"] = f(np.asarray(w_branch, np.float32).reshape(L, 3, 4, P, D).transpose(0, 1, 3, 2, 4))
    shared["w_out"] = f(np.asarray(w_out, np.float32).reshape(L, DC, P, D).transpose(0, 2, 1, 3))
    shared["norm_w"] = f(np.asarray(norm_w, np.float32).reshape(L, DC, P).transpose(0, 2, 1))
    shared["b_merge"] = f(np.asarray(b_merge, np.float32).reshape(L, 24, P).transpose(0, 2, 1))
    shared["b_glu"] = f(np.asarray(ssm_b_glu, np.float32).reshape(L, 4, P).transpose(0, 2, 1))
    shared["subln_w"] = f(np.asarray(diff_subln_w, np.float32).reshape(L, P, 1))
    shared["fnorm_w"] = f(np.asarray(final_norm_w, np.float32).reshape(DC, P).T)
    def gp(a):
        return f(np.asarray(a, np.float32).reshape(L, 16, 2, 64).transpose(0, 2, 3, 1).reshape(L, P, 16))
    shared["a_re"] = gp(ssm_a_re)
    shared["a_im"] = gp(ssm_a_im)
    shared["log_dt"] = gp(np.broadcast_to(np.asarray(ssm_log_dt, np.float32)[:, :, None], (L, 32, 64)))
    def gpc(a):
        return f(np.asarray(a, np.float32).reshape(L, 16, 2, 64, 16).transpose(0, 2, 3, 1, 4).reshape(L, P, 16, 16))
    shared["b_re"] = gpc(ssm_b_re)
    shared["b_im"] = gpc(ssm_b_im)
    shared["c_re"] = gpc(np.asarray(ssm_c_re, np.float32).transpose(0, 1, 3, 2))
    shared["c_im"] = gpc(np.asarray(ssm_c_im, np.float32).transpose(0, 1, 3, 2))
    shared["d_skip"] = f(np.asarray(ssm_d, np.float32).reshape(L, 4, 8, 16).transpose(0, 2, 3, 1).reshape(L, P, 4))
    lqk = np.stack([np.asarray(a, np.float32) for a in (diff_lq1, diff_lk1, diff_lq2, diff_lk2)], axis=1)
    shared["lqk"] = f(np.broadcast_to(lqk[:, None, :, :], (L, P, 4, 64)))
    xs = np.asarray(x, dtype=np.float32)
    return shared, xs


_PROGRAM_CACHE = {}


def kernel(**inputs):
    shared, xs = prep_inputs(**inputs)
    B = xs.shape[0]
    if "nc" not in _PROGRAM_CACHE:
        _PROGRAM_CACHE["nc"] = build_program()
    nc = _PROGRAM_CACHE["nc"]
    n_cores = 8
    in_maps = []
    for c in range(n_cores):
        m = dict(shared)
        m["x"] = np.ascontiguousarray(xs[c % B])
        in_maps.append(m)
    res = run_bass_kernel_spmd(nc, in_maps, core_ids=list(range(n_cores)))
    out = np.stack([np.asarray(res.results[b]["y"], dtype=np.float32) for b in range(B)], axis=0)
    return out
```

```python
import math
from contextlib import ExitStack

import numpy as np
import concourse.bass as bass
import concourse.mybir as mybir
from concourse.bass_utils import run_bass_kernel_spmd

F32 = mybir.dt.float32
BF16 = mybir.dt.bfloat16
I32 = mybir.dt.int32
AF = mybir.ActivationFunctionType
ALU = mybir.AluOpType
AX = mybir.AxisListType

P = 128
S = 4096
D = 1024
TT = 512
NT = S // TT
NB = S // P
DC = D // P
DEPTH = 4
NPROJ = 8192
EPS = 1e-6
MAGIC = 12582912.0
TWO_PI = 2.0 * math.pi
NEG = -30000.0
SEM_CAP = 30000


class Res:
    __slots__ = ("name", "lw", "readers", "excl", "slot")

    def __init__(self, name, excl=False):
        self.name = name
        self.lw = None
        self.readers = []
        self.excl = excl
        self.slot = None


class SemSlot:
    __slots__ = ("sem", "cnt")

    def __init__(self, sem):
        self.sem = sem
        self.cnt = 0


class Eng:
    def __init__(self, fw, name, handle):
        self.fw = fw
        self.name = name
        self.h = handle
        self.sems = []
        self.count = 0
        self.waited = {}

    def cur_event(self):
        n = self.count + 1
        k = (n - 1) // SEM_CAP
        while len(self.sems) <= k:
            self.sems.append(self.fw.new_sem(f"{self.name}{len(self.sems)}"))
        return (self.sems[k], n - k * SEM_CAP, self.name)

    def last_event(self):
        if self.count == 0:
            return None
        n = self.count
        k = (n - 1) // SEM_CAP
        return (self.sems[k], n - k * SEM_CAP, self.name)


class FW:
    def __init__(self, nc, es):
        self.nc = nc
        self.es = es
        self.nsem = 0
        self.E = {
            "pe": Eng(self, "pe", nc.tensor),
            "act": Eng(self, "act", nc.scalar),
            "dve": Eng(self, "dve", nc.vector),
            "pool": Eng(self, "pool", nc.gpsimd),
            "sp": Eng(self, "sp", nc.sync),
        }
        self.out_events = []
        self.free_slots = []
        self.all_slots = []
        self.phase_res = []
        self.ninst = 0

    def new_sem(self, name):
        self.nsem += 1
        return self.es.enter_context(self.nc.semaphore(f"s{self.nsem}_{name}"))

    def get_slot(self, res, phase_local=True):
        if res.slot is None:
            if self.free_slots:
                res.slot = self.free_slots.pop()
            else:
                res.slot = SemSlot(self.new_sem("dma"))
                self.all_slots.append(res.slot)
            if phase_local:
                self.phase_res.append(res)
        return res.slot

    def _need(self, eng, ev, waits):
        if ev is None:
            return
        sem, val, _ = ev
        key = id(sem)
        if eng.waited.get(key, 0) >= val:
            return
        cur = waits.get(key)
        if cur is None or cur[1] < val:
            waits[key] = (sem, val)

    def _collect(self, eng, reads, writes, waits):
        for r in reads:
            if r.lw is not None:
                self._need(eng, r.lw, waits)
            if r.excl:
                for ev in r.readers:
                    if ev[2] != eng.name:
                        self._need(eng, ev, waits)
        for w in writes:
            if w.lw is not None and w.lw[2] != eng.name:
                self._need(eng, w.lw, waits)
            for ev in w.readers:
                if ev[2] != eng.name:
                    self._need(eng, ev, waits)

    def _emit_waits(self, eng, waits):
        for key, (sem, val) in waits.items():
            eng.h.wait_ge(sem, val)
            eng.waited[key] = val
            self.ninst += 1

    def _update(self, ev, reads, writes):
        for r in reads:
            r.readers.append(ev)
            if len(r.readers) > 16:
                last = {}
                for e in r.readers:
                    k = (e[2], id(e[0]))
                    if k not in last or last[k][1] < e[1]:
                        last[k] = e
                r.readers = list(last.values())
        for w in writes:
            w.lw = ev
            w.readers = []

    def op(self, engname, fn, reads=(), writes=()):
        eng = self.E[engname]
        waits = {}
        self._collect(eng, reads, writes, waits)
        self._emit_waits(eng, waits)
        ins = fn(eng.h)
        ev = eng.cur_event()
        ins.then_inc(ev[0], 1)
        eng.count += 1
        self._update(ev, reads, writes)
        self.ninst += 1
        return ev

    def dma(self, out, in_, reads=(), writes=(), q="sp", semres=None, final=False, persistent=False, **kw):
        eng = self.E[q]
        if semres is None:
            semres = writes[0] if writes else reads[0]
        slot = self.get_slot(semres, phase_local=not persistent)
        if slot.cnt >= 1800:
            slot2 = SemSlot(self.new_sem("dma"))
            self.all_slots.append(slot2)
            semres.slot = slot2
            old = slot
            slot = slot2
            waits0 = {}
            self._need(eng, (old.sem, 16 * old.cnt, "dma"), waits0)
            self._emit_waits(eng, waits0)
        waits = {}
        self._collect(eng, reads, writes, waits)
        if slot.cnt > 0:
            self._need(eng, (slot.sem, 16 * slot.cnt, "dma"), waits)
        self._emit_waits(eng, waits)
        eng.h.dma_start(out=out, in_=in_, **kw).then_inc(slot.sem, 16)
        slot.cnt += 1
        ev = (slot.sem, 16 * slot.cnt, "dma")
        self._update(ev, reads, writes)
        if final:
            self.out_events.append(ev)
        self.ninst += 1
        return ev

    def barrier(self):
        evs = []
        for e in self.E.values():
            le = e.last_event()
            if le is not None:
                evs.append(le)
        for s in self.all_slots:
            if s.cnt > 0:
                evs.append((s.sem, 16 * s.cnt, "dma"))
        for e in self.E.values():
            waits = {}
            for ev in evs:
                if ev[2] == e.name:
                    continue
                self._need(e, ev, waits)
            self._emit_waits(e, waits)
        for r in self.phase_res:
            if r.slot is not None:
                self.free_slots.append(r.slot)
                r.slot = None
        self.phase_res = []

    def finish(self):
        eng = self.E["sp"]
        waits = {}
        for ev in self.out_events:
            self._need(eng, ev, waits)
        self._emit_waits(eng, waits)


def run_pipeline(units, stages):
    n = len(units)
    ns = len(stages)
    for step in range(n + ns - 1):
        for s in range(ns - 1, -1, -1):
            u = step - s
            if 0 <= u < n:
                stages[s](units[u])


def build_program(nlayers=DEPTH, dbg=None, stop_after=None):
    nc = bass.Bass("TRN2", target_bir_lowering=False)
    dbg = dbg or {}

    def din(name, shape):
        return nc.dram_tensor(name, list(shape), F32, kind="ExternalInput").ap()

    x_in = din("x", [S, D])
    w_in_d = din("w_in", [DEPTH, 64, P, DC, P])
    w_glu_d = din("w_glu", [DEPTH, P, 4, 512])
    w_br_d = din("w_br", [DEPTH, 3, P, 4, D])
    w_out_d = din("w_out", [DEPTH, P, DC, D])
    normw_d = din("norm_w", [DEPTH, P, DC])
    bmerge_d = din("b_merge", [DEPTH, P, 24])
    bglu_d = din("b_glu", [DEPTH, P, 4])
    subln_d = din("subln_w", [DEPTH, P, 1])
    fnorm_d = din("fnorm_w", [P, DC])
    are_d = din("a_re", [DEPTH, P, 16])
    aim_d = din("a_im", [DEPTH, P, 16])
    ldt_d = din("log_dt", [DEPTH, P, 16])
    bre_d = din("b_re", [DEPTH, P, 16, 16])
    bim_d = din("b_im", [DEPTH, P, 16, 16])
    cre_d = din("c_re", [DEPTH, P, 16, 16])
    cim_d = din("c_im", [DEPTH, P, 16, 16])
    dsk_d = din("d_skip", [DEPTH, P, 4])
    lqk_d = din("lqk", [DEPTH, P, 4, 64])

    y_out = nc.dram_tensor("y", [S, D], F32, kind="ExternalOutput").ap()
    dbg_out = {k: nc.dram_tensor(k, list(v), F32, kind="ExternalOutput").ap() for k, v in dbg.items()}

    xT_d = nc.dram_tensor("xT_scr", [P, DC, S], F32).ap()
    obr_d = nc.dram_tensor("obr_scr", [3, P, 4, S], BF16).ap()
    yg_d = nc.dram_tensor("yg_scr", [P, 4, S], BF16).ap()
    mg_d = nc.dram_tensor("mg_scr", [P, DC, S], BF16).ap()
    ropeC_d = nc.dram_tensor("ropeC_scr", [P, S], F32).ap()
    ropeS_d = nc.dram_tensor("ropeS_scr", [P, S], F32).ap()

    es = ExitStack()
    with es:
        fw = FW(nc, es)

        _uid = [0]

        def sbt(stack, name, shape, dt):
            _uid[0] += 1
            return stack.enter_context(nc.sbuf_tensor(f"{name}_{_uid[0]}", list(shape), dt))

        hT = sbt(es, "hT", [P, DC, S], BF16)
        R_hT = [Res(f"hT{j}") for j in range(NT)]
        ident_f = sbt(es, "ident_f", [P, P], F32)
        ident_b = sbt(es, "ident_b", [P, P], BF16)
        ones_f = sbt(es, "ones_f", [P, P], F32)
        ones_b = sbt(es, "ones_b", [P, P], BF16)
        nones_b = sbt(es, "nones_b", [P, P], BF16)
        ntri_b = sbt(es, "ntri_b", [P, P], BF16)
        mask_sb = sbt(es, "mask_sb", [P, P], BF16)
        mask_df = sbt(es, "mask_df", [P, P], BF16)
        iota_t = sbt(es, "iota_t", [P, TT], F32)
        halfpi = sbt(es, "halfpi", [P, 1], F32)
        R_const = Res("const")
        NST = 2
        wst = [sbt(es, f"wst{i}", [P, DC, P], F32) for i in range(NST)]
        R_wst = [Res(f"wst{i}") for i in range(NST)]
        wst_i = [0]
        nw = sbt(es, "nw", [P, DC], F32)
        R_nw = Res("nw")

        pb = [es.enter_context(nc.psum_tensor(f"pb{i}", [P, TT], F32)) for i in range(8)]
        R_pb = [Res(f"pb{i}", excl=True) for i in range(8)]

        def consts():
            with ExitStack() as ps:
                io_i = sbt(ps, "io_i", [P, P], I32)
                io_f = sbt(ps, "io_f", [P, P], F32)
                io2_i = sbt(ps, "io2_i", [P, TT], I32)
                R_a, R_b, R_c = Res("io_i"), Res("io_f"), Res("io2")
                fw.op("pool", lambda h: h.iota(io_i[:], pattern=[[1, P]], base=0, channel_multiplier=-1), writes=[R_a])
                fw.op("dve", lambda h: h.tensor_copy(out=io_f[:], in_=io_i[:]), reads=[R_a], writes=[R_b])
                W = [R_const]
                fw.op("dve", lambda h: h.tensor_scalar(out=ident_f[:], in0=io_f[:], scalar1=0.0, scalar2=None, op0=ALU.is_equal), reads=[R_b], writes=W)
                fw.op("dve", lambda h: h.tensor_scalar(out=ident_b[:], in0=io_f[:], scalar1=0.0, scalar2=None, op0=ALU.is_equal), reads=[R_b], writes=W)
                fw.op("dve", lambda h: h.tensor_scalar(out=ntri_b[:], in0=io_f[:], scalar1=0.0, scalar2=-1.0, op0=ALU.is_le, op1=ALU.mult), reads=[R_b], writes=W)
                fw.op("dve", lambda h: h.tensor_scalar(out=mask_sb[:], in0=io_f[:], scalar1=0.0, scalar2=NEG, op0=ALU.is_le, op1=ALU.mult), reads=[R_b], writes=W)
                fw.op("dve", lambda h: h.tensor_scalar(out=mask_df[:], in0=io_f[:], scalar1=0.0, scalar2=NEG, op0=ALU.is_lt, op1=ALU.mult), reads=[R_b], writes=W)
                fw.op("pool", lambda h: h.memset(ones_f[:], 1.0), writes=W)
                fw.op("pool", lambda h: h.memset(ones_b[:], 1.0), writes=W)
                fw.op("pool", lambda h: h.memset(nones_b[:], -1.0), writes=W)
                fw.op("pool", lambda h: h.memset(halfpi[:], math.pi / 2), writes=W)
                fw.op("pool", lambda h: h.iota(io2_i[:], pattern=[[1, TT]], base=0, channel_multiplier=0), writes=[R_c])
                fw.op("dve", lambda h: h.tensor_copy(out=iota_t[:], in_=io2_i[:]), reads=[R_c], writes=W)
                pidx_i = sbt(ps, "pidx_i", [P, 1], I32)
                pidx = sbt(ps, "pidx", [P, 1], F32)
                wfreq = sbt(ps, "wfreq", [P, 1], F32)
                R_p = Res("pidx")
                fw.op("pool", lambda h: h.iota(pidx_i[:], pattern=[[0, 1]], base=0, channel_multiplier=1), writes=[R_p])
                fw.op("dve", lambda h: h.tensor_copy(out=pidx[:], in_=pidx_i[:]), reads=[R_p], writes=[R_p])
                t0 = sbt(ps, "t0", [P, 1], F32)
                fw.op("dve", lambda h: h.tensor_scalar(out=t0[:], in0=pidx[:], scalar1=-15.5, scalar2=1.0 / 32, op0=ALU.add, op1=ALU.mult), reads=[R_p], writes=[R_p])
                fw.op("dve", lambda h: h.tensor_scalar(out=t0[:], in0=t0[:], scalar1=MAGIC, scalar2=MAGIC, op0=ALU.add, op1=ALU.subtract), reads=[R_p], writes=[R_p])
                fw.op("dve", lambda h: h.scalar_tensor_tensor(out=pidx[:], in0=t0[:], scalar=-32.0, in1=pidx[:], op0=ALU.mult, op1=ALU.add), reads=[R_p], writes=[R_p])
                fw.op("act", lambda h: h.activation(out=wfreq[:], in_=pidx[:], func=AF.Exp, scale=-math.log(10000.0) / 32), reads=[R_p], writes=[R_p])
                fw.op("dve", lambda h: h.tensor_scalar(out=wfreq[:], in0=wfreq[:], scalar1=1.0 / TWO_PI, scalar2=None, op0=ALU.mult), reads=[R_p], writes=[R_p])
                tpos = sbt(ps, "tpos", [P, TT], F32)
                t1 = sbt(ps, "t1", [P, TT], F32)
                kk = sbt(ps, "kk", [P, TT], F32)
                ff = sbt(ps, "ff", [P, TT], F32)
                tS = sbt(ps, "tS", [P, TT], F32)
                tC = sbt(ps, "tC", [P, TT], F32)
                R_t = Res("ropetmp")
                R_tS, R_tC = Res("tS"), Res("tC")
                for j in range(NT):
                    fw.op("dve", lambda h: h.tensor_scalar(out=tpos[:], in0=iota_t[:], scalar1=float(j * TT), scalar2=None, op0=ALU.add), reads=[R_const], writes=[R_t])
                    fw.op("dve", lambda h: h.tensor_scalar(out=t1[:], in0=tpos[:], scalar1=wfreq[:, 0:1], scalar2=MAGIC, op0=ALU.mult, op1=ALU.add), reads=[R_t, R_p], writes=[R_t])
                    fw.op("dve", lambda h: h.tensor_scalar(out=kk[:], in0=t1[:], scalar1=MAGIC, scalar2=None, op0=ALU.subtract), reads=[R_t], writes=[R_t])
                    fw.op("dve", lambda h: h.scalar_tensor_tensor(out=ff[:], in0=tpos[:], scalar=wfreq[:, 0:1], in1=kk[:], op0=ALU.mult, op1=ALU.subtract), reads=[R_t, R_p], writes=[R_t])
                    fw.op("act", lambda h: h.activation(out=tS[:], in_=ff[:], func=AF.Sin, scale=TWO_PI), reads=[R_t], writes=[R_tS])
                    fw.op("dve", lambda h: h.scalar_tensor_tensor(out=ff[:], in0=ff[:], scalar=-1.0, in1=ff[:], op0=ALU.mult, op1=ALU.max), reads=[R_t], writes=[R_t])
                    fw.op("act", lambda h: h.activation(out=tC[:], in_=ff[:], func=AF.Sin, scale=-TWO_PI, bias=halfpi[:]), reads=[R_t, R_const], writes=[R_tC])
                    fw.dma(ropeS_d[:, j * TT:(j + 1) * TT], tS[:], reads=[R_tS], writes=[R_ropeS[j]], semres=R_tS)
                    fw.dma(ropeC_d[:, j * TT:(j + 1) * TT], tC[:], reads=[R_tC], writes=[R_ropeC[j]], semres=R_tC)
                fw.barrier()

        R_ropeS = [Res(f"ropeS{j}") for j in range(NT)]
        R_ropeC = [Res(f"ropeC{j}") for j in range(NT)]
        R_xT = [Res(f"xT{j}") for j in range(NT)]
        R_obr = [[Res(f"obr{n}_{j}") for j in range(NT)] for n in range(3)]
        R_yg = [Res(f"yg{j}") for j in range(NT)]
        R_mgd = [Res(f"mgd{j}") for j in range(NT)]

        def load_wblock(l, cb, dst, R_dst, rot=False, scale_nw=True, eng="pool"):
            i = wst_i[0] % NST
            wst_i[0] += 1
            st, R_st = wst[i], R_wst[i]
            fw.dma(st[:], w_in_d[l, cb], writes=[R_st], persistent=True)
            nwb = nw[:].unsqueeze(2).broadcast_to([P, DC, P])
            if not rot:
                fw.op(eng, lambda h: h.tensor_tensor(out=dst, in0=st[:], in1=nwb, op=ALU.mult), reads=[R_st, R_nw], writes=[R_dst])
            else:
                for hh in range(2):
                    b0 = hh * 64
                    nwb32 = nw[:].unsqueeze(2).broadcast_to([P, DC, 32])
                    fw.op("dve", lambda h: h.scalar_tensor_tensor(out=dst[:, :, b0:b0 + 32], in0=st[:, :, b0 + 32:b0 + 64], scalar=-1.0, in1=nwb32, op0=ALU.mult, op1=ALU.mult),
                          reads=[R_st, R_nw], writes=[R_dst])
                    fw.op(eng, lambda h: h.tensor_tensor(out=dst[:, :, b0 + 32:b0 + 64], in0=st[:, :, b0:b0 + 32], in1=nwb32, op=ALU.mult),
                          reads=[R_st, R_nw], writes=[R_dst])

        def load_generic(src_ap, dst, R_dst, shape, eng="pool"):
            i = wst_i[0] % NST
            wst_i[0] += 1
            st, R_st = wst[i], R_wst[i]
            a, b = shape
            sv = st[:].rearrange("p c n -> p (c n)")[:, 0:a * b].rearrange("p (a b) -> p a b", a=a)
            fw.dma(sv, src_ap, writes=[R_st], persistent=True)
            fw.op(eng, lambda h: h.tensor_copy(out=dst, in_=sv), reads=[R_st], writes=[R_dst])

        def proj_fm(bank, wbf, R_w, j, M=P, c0=0):
            for c in range(DC):
                fw.op("pe", lambda h, c=c: h.matmul(pb[bank][0:M, :], lhsT=wbf[:, c, c0:c0 + M], rhs=hT[:, c, j * TT:(j + 1) * TT], start=(c == 0), stop=(c == DC - 1)),
                      reads=[R_w, R_hT[j]], writes=[R_pb[bank]])

        def dump(name, src_ap, R_src, dst_slice):
            if name in dbg_out:
                fw.dma(dst_slice(dbg_out[name]), src_ap, reads=[R_src], final=True, semres=R_src)

        def rsqrt_from_psum(bank, scale, dst, R_dst, tmp):
            fw.op("dve", lambda h: h.tensor_scalar(out=tmp, in0=pb[bank][:, :], scalar1=scale, scalar2=EPS, op0=ALU.mult, op1=ALU.add), reads=[R_pb[bank]], writes=[R_dst])
            fw.op("act", lambda h: h.activation(out=tmp, in_=tmp, func=AF.Sqrt), reads=[R_dst], writes=[R_dst])
            fw.op("dve", lambda h: h.reciprocal(out=dst, in_=tmp), reads=[R_dst], writes=[R_dst])

        def norm_tile(xt, R_xt, j, sq, R_sq, rstd, R_rstd, bank):
            for c in range(DC):
                i = c % 2
                fw.op("act", lambda h, c=c, i=i: h.activation(out=sq[i][:], in_=xt[:, c, :], func=AF.Square), reads=[R_xt], writes=[R_sq[i]])
                fw.op("pe", lambda h, c=c, i=i: h.matmul(pb[bank][:, :], lhsT=ones_f[:], rhs=sq[i][:], start=(c == 0), stop=(c == DC - 1)),
                      reads=[R_sq[i], R_const], writes=[R_pb[bank]])
            rsqrt_from_psum(bank, 1.0 / D, rstd[:], R_rstd, rstd[:])

        def phase0():
            with ExitStack() as ps:
                xin = [sbt(ps, f"xin{i}", [P, D], F32) for i in range(2)]
                R_xin = [Res(f"xin{i}") for i in range(2)]
                xt = [sbt(ps, f"xt0_{i}", [P, DC, TT], F32) for i in range(2)]
                R_xt = [Res(f"xt0_{i}") for i in range(2)]
                sq = [sbt(ps, f"sq0_{i}", [P, TT], F32) for i in range(2)]
                R_sq = [Res(f"sq0_{i}") for i in range(2)]
                rstd = sbt(ps, "rstd0", [P, TT], F32)
                R_rstd = Res("rstd0")
                fw.dma(nw[:], normw_d[0], writes=[R_nw], persistent=True)
                for j in range(NT):
                    xb_, R_xb = xt[j % 2], R_xt[j % 2]
                    for blk in range(4):
                        g = j * 4 + blk
                        xi, R_xi = xin[g % 2], R_xin[g % 2]
                        fw.dma(xi[:], x_in[g * P:(g + 1) * P, :], writes=[R_xi])
                        for half in range(2):
                            bank = (g % 2) * 2 + half
                            for cc in range(4):
                                c = half * 4 + cc
                                fw.op("pe", lambda h, c=c, cc=cc, bank=bank: h.transpose(pb[bank][:, cc * P:(cc + 1) * P], xi[:, c * P:(c + 1) * P], ident_f[:]),
                                      reads=[R_xi, R_const], writes=[R_pb[bank]])
                            eng = "act" if half == 0 else "dve"
                            src = pb[bank][:, :].rearrange("p (c t) -> p c t", c=4)
                            dst = xb_[:, half * 4:half * 4 + 4, blk * P:(blk + 1) * P]
                            if eng == "act":
                                fw.op("act", lambda h, src=src, dst=dst: h.copy(out=dst, in_=src), reads=[R_pb[bank]], writes=[R_xb])
                            else:
                                fw.op("dve", lambda h, src=src, dst=dst: h.tensor_copy(out=dst, in_=src), reads=[R_pb[bank]], writes=[R_xb])
                    fw.dma(xT_d[:, :, j * TT:(j + 1) * TT], xb_[:], reads=[R_xb], writes=[R_xT[j]], semres=R_xb)
                    norm_tile(xb_, R_xb, j, sq, R_sq, rstd, R_rstd, 4)
                    fw.op("dve", lambda h, xb_=xb_, j=j: h.tensor_tensor(out=hT[:, :, j * TT:(j + 1) * TT], in0=xb_[:], in1=rstd[:].unsqueeze(1).broadcast_to([P, DC, TT]), op=ALU.mult),
                          reads=[R_xb, R_rstd], writes=[R_hT[j]])
                fw.barrier()

        def ssm_phase(l):
            with ExitStack() as ps:
                w_u = [sbt(ps, f"w_u{i}", [P, DC, P], BF16) for i in range(4)]
                R_wu = [Res(f"w_u{i}") for i in range(4)]
                for cb in range(4):
                    load_wblock(l, cb, w_u[cb][:], R_wu[cb])
                def t16(name):
                    return sbt(ps, name, [P, 16], F32)
                are, aim, ldt = t16("are"), t16("aim"), t16("ldt")
                R_pp = Res("ssm_params")
                R_ld = [Res(f"ssm_ld{i}") for i in range(8)]
                fw.dma(are[:], are_d[l], writes=[R_ld[0]])
                fw.dma(aim[:], aim_d[l], writes=[R_ld[1]])
                fw.dma(ldt[:], ldt_d[l], writes=[R_ld[2]])
                cre = sbt(ps, "cre", [P, 16, 16], F32)
                cim = sbt(ps, "cim", [P, 16, 16], F32)
                dsk = sbt(ps, "dsk", [P, 4], F32)
                fw.dma(cre[:], cre_d[l], writes=[R_ld[5]])
                fw.dma(cim[:], cim_d[l], writes=[R_ld[6]])
                fw.dma(dsk[:], dsk_d[l], writes=[R_ld[7]])
                RL = R_ld
                dt_, th, lr, rho, wturn = t16("dt_"), t16("th"), t16("lr"), t16("rho"), t16("wturn")
                W = [R_pp]

                def dv(fn, reads=()):
                    fw.op("dve", fn, reads=list(reads) + [R_pp], writes=W)

                def ac(fn, reads=()):
                    fw.op("act", fn, reads=list(reads) + [R_pp], writes=W)

                ac(lambda h: h.activation(out=dt_[:], in_=ldt[:], func=AF.Exp), [RL[2]])
                dv(lambda h: h.tensor_tensor(out=th[:], in0=dt_[:], in1=aim[:], op=ALU.mult), [RL[1]])
                dv(lambda h: h.tensor_tensor(out=lr[:], in0=dt_[:], in1=are[:], op=ALU.mult), [RL[0]])
                ac(lambda h: h.activation(out=rho[:], in_=lr[:], func=AF.Exp))
                dv(lambda h: h.tensor_scalar(out=wturn[:], in0=th[:], scalar1=1.0 / TWO_PI, scalar2=None, op0=ALU.mult))

                def sincos(wt, s_out, c_out, tmpk, tmpf, mult=1.0):
                    dv(lambda h: h.tensor_scalar(out=tmpf[:], in0=wt[:], scalar1=mult, scalar2=None, op0=ALU.mult))
                    dv(lambda h: h.tensor_scalar(out=tmpk[:], in0=tmpf[:], scalar1=MAGIC, scalar2=MAGIC, op0=ALU.add, op1=ALU.subtract))
                    dv(lambda h: h.tensor_tensor(out=tmpf[:], in0=tmpf[:], in1=tmpk[:], op=ALU.subtract))
                    ac(lambda h: h.activation(out=s_out[:], in_=tmpf[:], func=AF.Sin, scale=TWO_PI))
                    dv(lambda h: h.scalar_tensor_tensor(out=tmpf[:], in0=tmpf[:], scalar=-1.0, in1=tmpf[:], op0=ALU.mult, op1=ALU.max))
                    ac(lambda h: h.activation(out=c_out[:], in_=tmpf[:], func=AF.Sin, scale=-TWO_PI, bias=halfpi[:]), [R_const])

                sin1, cos1, s512, c512, tk, tf = t16("sin1"), t16("cos1"), t16("s512"), t16("c512"), t16("tk"), t16("tf")
                sincos(wturn, sin1, cos1, tk, tf, 1.0)
                sincos(wturn, s512, c512, tk, tf, float(TT))
                CTr = sbt(ps, "CTr", [P, 4, 4, 4, 32], BF16)
                CTi = sbt(ps, "CTi", [P, 4, 4, 4, 32], BF16)
                CTn = sbt(ps, "CTn", [P, 4, 4, 4, 32], BF16)
                R_CT = Res("CT")
                BTr = sbt(ps, "BTr", [P, 16, P], BF16)
                BTi = sbt(ps, "BTi", [P, 16, P], BF16)
                R_BT = Res("BT")
                carry = sbt(ps, "carry", [P, 16, 2], F32)
                R_carry = [Res(f"carry{i}") for i in range(16)]
                ctmp = sbt(ps, "ctmp", [P, 2], F32)
                R_ctmp = Res("ctmp")
                fw.op("pool", lambda h: h.memset(carry[:], 0.0), writes=R_carry)
                with ExitStack() as pp_:
                    bre = sbt(pp_, "bre", [P, 16, 16], F32)
                    bim = sbt(pp_, "bim", [P, 16, 16], F32)
                    fw.dma(bre[:], bre_d[l], writes=[R_ld[3]])
                    fw.dma(bim[:], bim_d[l], writes=[R_ld[4]])

                    def t16p(name):
                        return sbt(pp_, name, [P, 16], F32)
                    Are, Aim, den, am1, cr, ci, tmpa, tmpb = (t16p(n) for n in ("Are", "Aim", "den", "am1", "cr", "ci", "tmpa", "tmpb"))
                    dv(lambda h: h.tensor_tensor(out=Are[:], in0=rho[:], in1=cos1[:], op=ALU.mult))
                    dv(lambda h: h.tensor_tensor(out=Aim[:], in0=rho[:], in1=sin1[:], op=ALU.mult))
                    dv(lambda h: h.tensor_tensor(out=den[:], in0=are[:], in1=are[:], op=ALU.mult))
                    dv(lambda h: h.tensor_tensor(out=tmpa[:], in0=aim[:], in1=aim[:], op=ALU.mult))
                    dv(lambda h: h.tensor_tensor(out=den[:], in0=den[:], in1=tmpa[:], op=ALU.add))
                    dv(lambda h: h.reciprocal(out=den[:], in_=den[:]))
                    dv(lambda h: h.tensor_scalar(out=am1[:], in0=Are[:], scalar1=-1.0, scalar2=None, op0=ALU.add))
                    dv(lambda h: h.tensor_tensor(out=tmpa[:], in0=am1[:], in1=are[:], op=ALU.mult))
                    dv(lambda h: h.tensor_tensor(out=tmpb[:], in0=Aim[:], in1=aim[:], op=ALU.mult))
                    dv(lambda h: h.tensor_tensor(out=tmpa[:], in0=tmpa[:], in1=tmpb[:], op=ALU.add))
                    dv(lambda h: h.tensor_tensor(out=cr[:], in0=tmpa[:], in1=den[:], op=ALU.mult))
                    dv(lambda h: h.tensor_tensor(out=tmpa[:], in0=Aim[:], in1=are[:], op=ALU.mult))
                    dv(lambda h: h.tensor_tensor(out=tmpb[:], in0=am1[:], in1=aim[:], op=ALU.mult))
                    dv(lambda h: h.tensor_tensor(out=tmpa[:], in0=tmpa[:], in1=tmpb[:], op=ALU.subtract))
                    dv(lambda h: h.tensor_tensor(out=ci[:], in0=tmpa[:], in1=den[:], op=ALU.mult))
                    bbr = sbt(pp_, "bbr", [P, 16, 16], F32)
                    bbi = sbt(pp_, "bbi", [P, 16, 16], F32)
                    tb = sbt(pp_, "tb", [P, 16, 16], F32)
                    crb = cr[:].unsqueeze(2).broadcast_to([P, 16, 16])
                    cib = ci[:].unsqueeze(2).broadcast_to([P, 16, 16])
                    dv(lambda h: h.tensor_tensor(out=bbr[:], in0=bre[:], in1=crb, op=ALU.mult), [RL[3]])
                    dv(lambda h: h.tensor_tensor(out=tb[:], in0=bim[:], in1=cib, op=ALU.mult), [RL[4]])
                    dv(lambda h: h.tensor_tensor(out=bbr[:], in0=bbr[:], in1=tb[:], op=ALU.subtract))
                    dv(lambda h: h.tensor_tensor(out=bbi[:], in0=bim[:], in1=crb, op=ALU.mult))
                    dv(lambda h: h.tensor_tensor(out=tb[:], in0=bre[:], in1=cib, op=ALU.mult))
                    dv(lambda h: h.tensor_tensor(out=bbi[:], in0=bbi[:], in1=tb[:], op=ALU.add))
                    Zr = sbt(pp_, "Zr", [P, 4, 4, 4, 32], F32)
                    Zi = sbt(pp_, "Zi", [P, 4, 4, 4, 32], F32)
                    R_Z = Res("Z")
                    for z in (Zr, Zi):
                        fw.op("pool", lambda h, z=z: h.memset(z[:], 0.0), writes=[R_Z])
                    for z in (CTr, CTi, CTn):
                        fw.op("pool", lambda h, z=z: h.memset(z[:], 0.0), writes=[R_CT])
                    for q in range(4):
                        for gi in range(2):
                            p0 = gi * 64

                            def v4(t, p0=p0, q=q):
                                return t[p0:p0 + 64, :, :].rearrange("p (cb q) c -> p cb q c", q=4)[:, :, q, :]
                            fw.op("dve", lambda h: h.tensor_copy(out=Zr[p0:p0 + 64, :, q, q, gi * 16:gi * 16 + 16], in_=v4(bbr)), reads=[R_pp], writes=[R_Z])
                            fw.op("dve", lambda h: h.tensor_copy(out=Zi[p0:p0 + 64, :, q, q, gi * 16:gi * 16 + 16], in_=v4(bbi)), reads=[R_pp], writes=[R_Z])
                            fw.op("dve", lambda h: h.tensor_copy(out=CTr[p0:p0 + 64, :, q, q, gi * 16:gi * 16 + 16], in_=v4(cre)), reads=[RL[5]], writes=[R_CT])
                            fw.op("dve", lambda h: h.tensor_scalar(out=CTi[p0:p0 + 64, :, q, q, gi * 16:gi * 16 + 16], in0=v4(cim), scalar1=-1.0, scalar2=None, op0=ALU.mult), reads=[RL[6]], writes=[R_CT])
                            fw.op("dve", lambda h: h.tensor_scalar(out=CTn[p0:p0 + 64, :, q, q, gi * 16:gi * 16 + 16], in0=v4(cre), scalar1=-1.0, scalar2=None, op0=ALU.mult), reads=[RL[5]], writes=[R_CT])
                    for ri, (Z, BT) in enumerate(((Zr, BTr), (Zi, BTi))):
                        for g4 in range(4):
                            bank = 6 + (g4 % 2)
                            for k in range(4):
                                pair = g4 * 4 + k
                                cbi, qi = pair // 4, pair % 4
                                src = Z[:, cbi, qi, :, :].rearrange("p a b -> p (a b)")
                                fw.op("pe", lambda h, src=src, k=k, bank=bank: h.transpose(pb[bank][:, k * P:(k + 1) * P], src, ident_f[:]), reads=[R_Z, R_const], writes=[R_pb[bank]])
                            fw.op("act", lambda h, BT=BT, g4=g4, bank=bank: h.copy(out=BT[:, g4 * 4:g4 * 4 + 4, :], in_=pb[bank][:, :].rearrange("p (k n) -> p k n", k=4)), reads=[R_pb[bank]], writes=[R_BT])
                    fw.barrier()
                with ExitStack() as pw:
                    NR = 2
                    NU = 3
                    tcos = [sbt(pw, f"tcos{i}", [P, TT], F32) for i in range(4)]
                    tsin = [sbt(pw, f"tsin{i}", [P, TT], F32) for i in range(4)]
                    R_tab = [Res(f"tab{i}") for i in range(4)]
                    tA = sbt(pw, "tabA", [P, TT], F32)
                    tB = sbt(pw, "tabB", [P, TT], F32)
                    R_tt = Res("tabtmp")
                    utf = [sbt(pw, f"utf{i}", [P, TT], F32) for i in range(NU)]
                    utb = [sbt(pw, f"utb{i}", [P, TT], BF16) for i in range(NU)]
                    R_utf = [Res(f"utf{i}") for i in range(NU)]
                    R_utb = [Res(f"utb{i}") for i in range(NU)]
                    mm = [[sbt(pw, f"m{k}_{i}", [P, TT], F32) for k in range(4)] for i in range(NR)]
                    R_mm = [Res(f"mm{i}") for i in range(NR)]
                    bp = [[sbt(pw, f"bp{k}_{i}", [P, TT], F32) for k in range(2)] for i in range(NR)]
                    R_bp = [Res(f"bp{i}") for i in range(NR)]
                    yy = [[sbt(pw, f"yy{k}_{i}", [P, TT], F32) for k in range(2)] for i in range(NR)]
                    R_yy = [Res(f"yy{i}") for i in range(NR)]
                    xA = [sbt(pw, f"xA{i}", [P, 4, TT], BF16) for i in range(2)]
                    xB = [sbt(pw, f"xB{i}", [P, 4, TT], BF16) for i in range(2)]
                    xC = [sbt(pw, f"xC{i}", [P, 4, TT], BF16) for i in range(2)]
                    xD = [sbt(pw, f"xD{i}", [P, 4, TT], BF16) for i in range(2)]
                    R_xx = [[Res(f"xx{i}_{q}") for q in range(4)] for i in range(2)]
                    yv = [sbt(pw, f"yv{i}", [P, TT], F32) for i in range(2)]
                    ygb = [sbt(pw, f"ygb{i}", [P, TT], BF16) for i in range(2)]
                    g1, g2 = tA, tB
                    R_yv = [Res(f"yv{i}") for i in range(2)]
                    R_g = R_tt
                    R_ygb = [Res(f"ygb{i}") for i in range(2)]

                    ucount = [0]
                    for cb in range(4):
                        for q in range(4):
                            pair = cb * 4 + q
                            wcol = wturn[:, pair:pair + 1]
                            fw.op("dve", lambda h: h.tensor_scalar(out=tA[:], in0=iota_t[:], scalar1=wcol, scalar2=MAGIC, op0=ALU.mult, op1=ALU.add), reads=[R_const, R_pp], writes=[R_tt])
                            fw.op("dve", lambda h: h.tensor_scalar(out=tB[:], in0=tA[:], scalar1=MAGIC, scalar2=None, op0=ALU.subtract), reads=[R_tt], writes=[R_tt])
                            fw.op("dve", lambda h: h.scalar_tensor_tensor(out=tA[:], in0=iota_t[:], scalar=wcol, in1=tB[:], op0=ALU.mult, op1=ALU.subtract), reads=[R_const, R_pp, R_tt], writes=[R_tt])
                            fw.op("act", lambda h: h.activation(out=tsin[q][:], in_=tA[:], func=AF.Sin, scale=TWO_PI), reads=[R_tt], writes=[R_tab[q]])
                            fw.op("dve", lambda h: h.scalar_tensor_tensor(out=tB[:], in0=tA[:], scalar=-1.0, in1=tA[:], op0=ALU.mult, op1=ALU.max), reads=[R_tt], writes=[R_tt])
                            fw.op("act", lambda h: h.activation(out=tcos[q][:], in_=tB[:], func=AF.Sin, scale=-TWO_PI, bias=halfpi[:]), reads=[R_tt, R_const], writes=[R_tab[q]])
                        units = []
                        for seg in range(NT):
                            for q in range(4):
                                units.append(dict(q=q, pair=cb * 4 + q, n=ucount[0], seg=seg, sg=cb * NT + seg))
                                ucount[0] += 1

                        def sA(u):
                            if u["q"] != 0:
                                return
                            si = u["sg"] % NU
                            proj_fm(0, w_u[cb], R_wu[cb], u["seg"])
                            fw.op("act", lambda h: h.copy(out=utf[si][:], in_=pb[0][:, :]), reads=[R_pb[0]], writes=[R_utf[si]])
                            fw.op("pool", lambda h: h.tensor_copy(out=utb[si][:], in_=utf[si][:]), reads=[R_utf[si]], writes=[R_utb[si]])

                        def st0(u):
                            k = u["n"] % 2
                            bR, bI = 1 + 2 * k, 2 + 2 * k
                            u["bR"], u["bI"] = bR, bI
                            pair = u["pair"]
                            si = u["sg"] % NU
                            fw.op("pe", lambda h: h.matmul(pb[bR][:, :], lhsT=BTr[:, pair, :], rhs=utb[si][:], start=True, stop=True), reads=[R_BT, R_utb[si]], writes=[R_pb[bR]])
                            fw.op("pe", lambda h: h.matmul(pb[bI][:, :], lhsT=BTi[:, pair, :], rhs=utb[si][:], start=True, stop=True), reads=[R_BT, R_utb[si]], writes=[R_pb[bI]])

                        def st1(u):
                            i = u["n"] % NR
                            q = u["q"]
                            bR, bI = u["bR"], u["bI"]
                            m = mm[i]
                            fw.op("dve", lambda h: h.tensor_tensor(out=m[0][:], in0=pb[bR][:, :], in1=tcos[q][:], op=ALU.mult), reads=[R_pb[bR], R_tab[q]], writes=[R_mm[i]])
                            fw.op("dve", lambda h: h.tensor_tensor(out=m[3][:], in0=pb[bR][:, :], in1=tsin[q][:], op=ALU.mult), reads=[R_pb[bR], R_tab[q]], writes=[R_mm[i]])
                            fw.op("dve", lambda h: h.tensor_tensor(out=m[1][:], in0=pb[bI][:, :], in1=tsin[q][:], op=ALU.mult), reads=[R_pb[bI], R_tab[q]], writes=[R_mm[i]])
                            fw.op("dve", lambda h: h.tensor_tensor(out=m[2][:], in0=pb[bI][:, :], in1=tcos[q][:], op=ALU.mult), reads=[R_pb[bI], R_tab[q]], writes=[R_mm[i]])

                        def st2(u):
                            i = u["n"] % NR
                            m = mm[i]
                            fw.op("pool", lambda h: h.tensor_tensor(out=bp[i][0][:], in0=m[0][:], in1=m[1][:], op=ALU.add), reads=[R_mm[i]], writes=[R_bp[i]])
                            fw.op("pool", lambda h: h.tensor_tensor(out=bp[i][1][:], in0=m[2][:], in1=m[3][:], op=ALU.subtract), reads=[R_mm[i]], writes=[R_bp[i]])

                        def st3(u):
                            i = u["n"] % NR
                            pair = u["pair"]
                            rb = rho[:, pair:pair + 1].broadcast_to([P, TT])
                            Rc = R_carry[pair]
                            fw.op("dve", lambda h: h.tensor_tensor_scan(out=yy[i][0][:], data0=rb, data1=bp[i][0][:], initial=carry[:, pair, 0:1], op0=ALU.mult, op1=ALU.add), reads=[R_bp[i], R_pp, Rc], writes=[R_yy[i]])
                            fw.op("dve", lambda h: h.tensor_tensor_scan(out=yy[i][1][:], data0=rb, data1=bp[i][1][:], initial=carry[:, pair, 1:2], op0=ALU.mult, op1=ALU.add), reads=[R_bp[i], R_pp, Rc], writes=[R_yy[i]])
                            yl_r = yy[i][0][:, TT - 1:TT]
                            yl_i = yy[i][1][:, TT - 1:TT]
                            fw.op("dve", lambda h: h.tensor_tensor(out=ctmp[:, 0:1], in0=yl_i, in1=s512[:, pair:pair + 1], op=ALU.mult), reads=[R_yy[i], R_pp], writes=[R_ctmp])
                            fw.op("dve", lambda h: h.tensor_tensor(out=ctmp[:, 1:2], in0=yl_r, in1=s512[:, pair:pair + 1], op=ALU.mult), reads=[R_yy[i], R_pp], writes=[R_ctmp])
                            fw.op("dve", lambda h: h.scalar_tensor_tensor(out=carry[:, pair, 0:1], in0=yl_r, scalar=c512[:, pair:pair + 1], in1=ctmp[:, 0:1], op0=ALU.mult, op1=ALU.subtract), reads=[R_yy[i], R_pp, R_ctmp], writes=[Rc])
                            fw.op("dve", lambda h: h.scalar_tensor_tensor(out=carry[:, pair, 1:2], in0=yl_i, scalar=c512[:, pair:pair + 1], in1=ctmp[:, 1:2], op0=ALU.mult, op1=ALU.add), reads=[R_yy[i], R_pp, R_ctmp], writes=[Rc])

                        def st4(u):
                            i = u["n"] % NR
                            q = u["q"]
                            xi = u["sg"] % 2
                            yr, yi = yy[i][0], yy[i][1]
                            fw.op("pool", lambda h: h.tensor_tensor(out=xA[xi][:, q, :], in0=yr[:], in1=tcos[q][:], op=ALU.mult), reads=[R_yy[i], R_tab[q]], writes=[R_xx[xi][q]])
                            fw.op("pool", lambda h: h.tensor_tensor(out=xB[xi][:, q, :], in0=yi[:], in1=tsin[q][:], op=ALU.mult), reads=[R_yy[i], R_tab[q]], writes=[R_xx[xi][q]])
                            fw.op("pool", lambda h: h.tensor_tensor(out=xC[xi][:, q, :], in0=yi[:], in1=tcos[q][:], op=ALU.mult), reads=[R_yy[i], R_tab[q]], writes=[R_xx[xi][q]])
                            fw.op("pool", lambda h: h.tensor_tensor(out=xD[xi][:, q, :], in0=yr[:], in1=tsin[q][:], op=ALU.mult), reads=[R_yy[i], R_tab[q]], writes=[R_xx[xi][q]])

                        def sG(u):
                            if u["q"] != 3:
                                return
                            seg = u["seg"]
                            si = u["sg"] % NU
                            xi = u["sg"] % 2
                            yi_ = u["sg"] % 2
                            for q in range(4):
                                lr_ = CTr[:, cb, q, :, :].rearrange("p a b -> p (a b)")
                                ln_ = CTn[:, cb, q, :, :].rearrange("p a b -> p (a b)")
                                li_ = CTi[:, cb, q, :, :].rearrange("p a b -> p (a b)")
                                fw.op("pe", lambda h: h.matmul(pb[5][:, :], lhsT=lr_, rhs=xA[xi][:, q, :], start=(q == 0), stop=False), reads=[R_CT, R_xx[xi][q]], writes=[R_pb[5]])
                                fw.op("pe", lambda h: h.matmul(pb[5][:, :], lhsT=ln_, rhs=xB[xi][:, q, :], start=False, stop=False), reads=[R_CT, R_xx[xi][q]], writes=[R_pb[5]])
                                fw.op("pe", lambda h: h.matmul(pb[5][:, :], lhsT=li_, rhs=xC[xi][:, q, :], start=False, stop=False), reads=[R_CT, R_xx[xi][q]], writes=[R_pb[5]])
                                fw.op("pe", lambda h: h.matmul(pb[5][:, :], lhsT=li_, rhs=xD[xi][:, q, :], start=False, stop=(q == 3)), reads=[R_CT, R_xx[xi][q]], writes=[R_pb[5]])
                            fw.op("dve", lambda h: h.scalar_tensor_tensor(out=yv[yi_][:], in0=utf[si][:], scalar=dsk[:, cb:cb + 1], in1=pb[5][:, :], op0=ALU.mult, op1=ALU.add),
                                  reads=[R_utf[si], R_pb[5], RL[7]], writes=[R_yv[yi_]])
                            dump("dbg_ssm_y", yv[yi_][:], R_yv[yi_], lambda o: o[cb * P:(cb + 1) * P, seg * TT:(seg + 1) * TT])
                            fw.op("act", lambda h: h.activation(out=g1[:], in_=yv[yi_][:], func=AF.Square), reads=[R_yv[yi_]], writes=[R_g])
                            fw.op("dve", lambda h: h.tensor_scalar(out=g1[:], in0=g1[:], scalar1=0.044715, scalar2=1.0, op0=ALU.mult, op1=ALU.add), reads=[R_g], writes=[R_g])
                            fw.op("pool", lambda h: h.tensor_tensor(out=g1[:], in0=g1[:], in1=yv[yi_][:], op=ALU.mult), reads=[R_g, R_yv[yi_]], writes=[R_g])
                            fw.op("act", lambda h: h.activation(out=g2[:], in_=g1[:], func=AF.Sigmoid, scale=1.5957691216057308), reads=[R_g], writes=[R_g])
                            fw.op("pool", lambda h: h.tensor_tensor(out=ygb[yi_][:], in0=g2[:], in1=yv[yi_][:], op=ALU.mult), reads=[R_g, R_yv[yi_]], writes=[R_ygb[yi_]])
                            fw.dma(yg_d[:, cb, seg * TT:(seg + 1) * TT], ygb[yi_][:], reads=[R_ygb[yi_]], writes=[R_yg[seg]], semres=R_ygb[yi_])

                        run_pipeline(units, [sA, st0, st1, st2, st3, st4, sG])
                    fw.barrier()

            with ExitStack() as ps:
                w_sg = [sbt(ps, f"w_sg{i}", [P, DC, P], BF16) for i in range(4)]
                R_wsg = [Res(f"w_sg{i}") for i in range(4)]
                for cb in range(4):
                    load_wblock(l, 4 + cb, w_sg[cb][:], R_wsg[cb])
                wglu = sbt(ps, "wglu", [P, 4, 512], BF16)
                R_wglu = Res("wglu")
                for ob in range(4):
                    load_generic(w_glu_d[l, :, :, ob * P:(ob + 1) * P], wglu[:, :, ob * P:(ob + 1) * P], R_wglu, (4, P))
                bglu = sbt(ps, "bglu", [P, 4], F32)
                R_bglu = Res("bglu")
                fw.dma(bglu[:], bglu_d[l], writes=[R_bglu])
                ygt = [sbt(ps, f"ygt{i}", [P, 4, TT], BF16) for i in range(2)]
                R_ygt = [Res(f"ygt{i}") for i in range(2)]
                s1 = [sbt(ps, f"s1_{i}", [P, TT], F32) for i in range(2)]
                s2 = [sbt(ps, f"s2_{i}", [P, TT], F32) for i in range(2)]
                R_s = [Res(f"s12_{i}") for i in range(2)]
                osb = [sbt(ps, f"osb{i}", [P, 4, TT], BF16) for i in range(2)]
                R_osb = [Res(f"osb{i}") for i in range(2)]
                cnt = 0
                for j in range(NT):
                    yi_ = j % 2
                    fw.dma(ygt[yi_][:], yg_d[:, :, j * TT:(j + 1) * TT], reads=[R_yg[j]], writes=[R_ygt[yi_]])
                    for ob in range(4):
                        k = cnt % 2
                        cnt += 1
                        bz, bg = 6 + k, 0 + k
                        for c in range(4):
                            fw.op("pe", lambda h, c=c, ob=ob, bz=bz: h.matmul(pb[bz][:, :], lhsT=wglu[:, c, ob * P:(ob + 1) * P], rhs=ygt[yi_][:, c, :], start=(c == 0), stop=(c == 3)),
                                  reads=[R_wglu, R_ygt[yi_]], writes=[R_pb[bz]])
                        proj_fm(bg, w_sg[ob], R_wsg[ob], j)
                        fw.op("act", lambda h, k=k, ob=ob, bz=bz: h.activation(out=s1[k][:], in_=pb[bz][:, :], func=AF.Sigmoid, bias=bglu[:, ob:ob + 1]), reads=[R_pb[bz], R_bglu], writes=[R_s[k]])
                        fw.op("act", lambda h, k=k, bg=bg: h.activation(out=s2[k][:], in_=pb[bg][:, :], func=AF.Silu), reads=[R_pb[bg]], writes=[R_s[k]])
                        fw.op("dve", lambda h, k=k, ob=ob: h.tensor_tensor(out=s1[k][:], in0=s1[k][:], in1=ygt[yi_][:, ob, :], op=ALU.mult), reads=[R_s[k], R_ygt[yi_]], writes=[R_s[k]])
                        fw.op("dve", lambda h, k=k, ob=ob: h.tensor_tensor(out=osb[yi_][:, ob, :], in0=s1[k][:], in1=s2[k][:], op=ALU.mult), reads=[R_s[k]], writes=[R_osb[yi_]])
                    fw.dma(obr_d[0, :, :, j * TT:(j + 1) * TT], osb[yi_][:], reads=[R_osb[yi_]], writes=[R_obr[0][j]], semres=R_osb[yi_])
                    if "dbg_o0" in dbg_out:
                        pass
                fw.barrier()

        def sb_phase(l):
            with ExitStack() as ps:
                wq = sbt(ps, "sb_wq", [P, DC, P], BF16)
                wk = sbt(ps, "sb_wk", [P, DC, P], BF16)
                wv = sbt(ps, "sb_wv", [P, DC, P], BF16)
                wg = sbt(ps, "sb_wg", [P, DC, P], BF16)
                R_wq, R_wk, R_wv, R_wg = Res("sb_wq"), Res("sb_wk"), Res("sb_wv"), Res("sb_wg")
                qT = sbt(ps, "sb_qT", [P, S], BF16)
                kT = sbt(ps, "sb_kT", [P, S], BF16)
                V = sbt(ps, "sb_V", [P, NB, P], BF16)
                R_q = [Res(f"sb_q{j}") for j in range(NT)]
                R_k = [Res(f"sb_k{j}") for j in range(NT)]
                R_v = [Res(f"sb_v{j}") for j in range(NT)]
                NR = 3
                e1 = [[sbt(ps, f"sb_e1_{i}_{e}", [P, TT], F32) for e in range(2)] for i in range(NR)]
                sp = [[sbt(ps, f"sb_sp_{i}_{e}", [P, TT], BF16) for e in range(2)] for i in range(NR)]
                ww = [[sbt(ps, f"sb_w_{i}_{e}", [P, TT], BF16) for e in range(2)] for i in range(NR)]
                R_e1 = [[Res(f"sb_e1_{i}_{e}") for e in range(2)] for i in range(NR)]
                R_sp = [[Res(f"sb_sp_{i}_{e}") for e in range(2)] for i in range(NR)]
                R_ww = [[Res(f"sb_w_{i}_{e}") for e in range(2)] for i in range(NR)]
                acc = [[sbt(ps, f"sb_acc{i}_{e}", [P, TT], BF16) for e in range(2)] for i in range(2)]
                R_acc = [[Res(f"sb_acc{i}_{e}") for e in range(2)] for i in range(2)]
                sg = [sbt(ps, f"sb_sg{i}", [P, TT], F32) for i in range(2)]
                R_sg = [Res(f"sb_sg{i}") for i in range(2)]
                og = [sbt(ps, f"sb_og{i}", [P, TT], BF16) for i in range(2)]
                R_og = [Res(f"sb_og{i}") for i in range(2)]
                tcount = [0]
                gcount = [0]
                for hp in range(4):
                    load_wblock(l, 8 + hp, wq[:], R_wq)
                    load_wblock(l, 12 + hp, wk[:], R_wk)
                    load_wblock(l, 16 + hp, wv[:], R_wv)
                    load_wblock(l, 20 + hp, wg[:], R_wg)
                    for j in range(NT):
                        proj_fm(6, wq, R_wq, j)
                        fw.op("act", lambda h, j=j: h.activation(out=qT[:, j * TT:(j + 1) * TT], in_=pb[6][:, :], func=AF.Copy, scale=0.125), reads=[R_pb[6]], writes=[R_q[j]])
                        proj_fm(7, wk, R_wk, j)
                        fw.op("dve", lambda h, j=j: h.tensor_copy(out=kT[:, j * TT:(j + 1) * TT], in_=pb[7][:, :]), reads=[R_pb[7]], writes=[R_k[j]])
                    for j in range(NT):
                        bank = 6 + (j % 2)
                        for b4 in range(4):
                            blk = j * 4 + b4
                            for c in range(DC):
                                fw.op("pe", lambda h, c=c, blk=blk, b4=b4, bank=bank: h.matmul(pb[bank][:, b4 * P:(b4 + 1) * P], lhsT=hT[:, c, blk * P:(blk + 1) * P], rhs=wv[:, c, :], start=(c == 0), stop=(c == DC - 1)),
                                      reads=[R_wv, R_hT[j]], writes=[R_pb[bank]])
                        src = pb[bank][:, :].rearrange("p (b n) -> p b n", b=4)
                        if j % 2 == 0:
                            fw.op("act", lambda h, j=j, src=src: h.copy(out=V[:, j * 4:(j + 1) * 4, :], in_=src), reads=[R_pb[bank]], writes=[R_v[j]])
                        else:
                            fw.op("dve", lambda h, j=j, src=src: h.tensor_copy(out=V[:, j * 4:(j + 1) * 4, :], in_=src), reads=[R_pb[bank]], writes=[R_v[j]])
                    units = []
                    for qt in range(NT):
                        ai = gcount[0] % 2
                        gcount[0] += 1
                        kbs = list(range(4 * qt + 3, -1, -1))
                        for idx, kb in enumerate(kbs):
                            jd = kb - 4 * qt
                            units.append(dict(qt=qt, kb=kb, col0=(P * jd if jd >= 0 else 0), diag=(jd >= 0), first=(idx == 0), last=(kb == 0),
                                              ai=ai, n=tcount[0], ob=4 + (qt % 2)))
                            tcount[0] += 1

                    def s0(u):
                        kb, qt, c0 = u["kb"], u["qt"], u["col0"]
                        t0 = qt * TT
                        if u["first"]:
                            for e in range(2):
                                fw.op("pool", lambda h: h.memset(acc[u["ai"]][e][:], 0.0), writes=[R_acc[u["ai"]][e]])
                        for e in range(2):
                            pr = slice(64 * e, 64 * e + 64)
                            fw.op("pe", lambda h: h.matmul(pb[e][:, c0:TT], lhsT=kT[pr, kb * P:(kb + 1) * P], rhs=qT[pr, t0 + c0:t0 + TT], start=True, stop=not u["diag"]),
                                  reads=[R_k[kb // 4], R_q[qt]], writes=[R_pb[e]])
                        if u["diag"]:
                            for e in range(2):
                                fw.op("pe", lambda h: h.matmul(pb[e][:, c0:c0 + P], lhsT=ident_b[:], rhs=mask_sb[:], start=False, stop=True), reads=[R_const], writes=[R_pb[e]])

                    def s1(u):
                        i = u["n"] % NR
                        c0 = u["col0"]
                        for e in range(2):
                            fw.op("act", lambda h: h.activation(out=e1[i][e][:, c0:TT], in_=pb[e][:, c0:TT], func=AF.Exp), reads=[R_pb[e]], writes=[R_e1[i][e]])
                        for e in range(2):
                            fw.op("act", lambda h: h.activation(out=sp[i][e][:, c0:TT], in_=e1[i][e][:, c0:TT], func=AF.Ln, bias=1.0), reads=[R_e1[i][e]], writes=[R_sp[i][e]])

                    def s2(u):
                        i = u["n"] % NR
                        kb, qt, c0 = u["kb"], u["qt"], u["col0"]
                        t0 = qt * TT
                        for e in range(2):
                            pr = slice(64 * e, 64 * e + 64)
                            fw.op("pe", lambda h: h.matmul(pb[2 + e][:, c0:TT], lhsT=kT[pr, kb * P:(kb + 1) * P], rhs=qT[pr, t0 + c0:t0 + TT], start=True, stop=False),
                                  reads=[R_k[kb // 4], R_q[qt]], writes=[R_pb[2 + e]])
                        lastmm = u["first"] and not u["diag"]
                        for e in range(2):
                            fw.op("pe", lambda h: h.matmul(pb[2 + e][:, c0:TT], lhsT=ntri_b[:], rhs=sp[i][e][:, c0:TT], start=False, stop=lastmm), reads=[R_const, R_sp[i][e]], writes=[R_pb[2 + e]])
                        if not u["first"]:
                            for e in range(2):
                                fw.op("pe", lambda h: h.matmul(pb[2 + e][:, c0:TT], lhsT=nones_b[:], rhs=acc[u["ai"]][e][:, c0:TT], start=False, stop=not u["diag"]),
                                      reads=[R_const, R_acc[u["ai"]][e]], writes=[R_pb[2 + e]])
                        if u["diag"]:
                            for e in range(2):
                                fw.op("pe", lambda h: h.matmul(pb[2 + e][:, c0:c0 + P], lhsT=ident_b[:], rhs=mask_sb[:], start=False, stop=True), reads=[R_const], writes=[R_pb[2 + e]])

                    def s3(u):
                        i = u["n"] % NR
                        c0 = u["col0"]
                        for e in range(2):
                            fw.op("act", lambda h: h.activation(out=ww[i][e][:, c0:TT], in_=pb[2 + e][:, c0:TT], func=AF.Exp), reads=[R_pb[2 + e]], writes=[R_ww[i][e]])
                        if not u["last"]:
                            for e in range(2):
                                a_ = acc[u["ai"]][e]
                                fw.op("dve", lambda h: h.tensor_tensor(out=a_[:, c0:TT], in0=a_[:, c0:TT], in1=sp[i][e][:, c0:TT], op=ALU.add), reads=[R_sp[i][e], R_acc[u["ai"]][e]], writes=[R_acc[u["ai"]][e]])

                    def s4(u):
                        i = u["n"] % NR
                        kb, qt, c0, ob = u["kb"], u["qt"], u["col0"], u["ob"]
                        for e in range(2):
                            pr = slice(64 * e, 64 * e + 64)
                            fw.op("pe", lambda h: h.matmul(pb[ob][pr, c0:TT], lhsT=V[:, kb, 64 * e:64 * e + 64], rhs=ww[i][e][:, c0:TT], start=u["first"], stop=u["last"]),
                                  reads=[R_v[kb // 4], R_ww[i][e]], writes=[R_pb[ob]])
                        if u["last"]:
                            k = qt % 2
                            proj_fm(6 + k, wg, R_wg, qt)
                            fw.op("act", lambda h: h.activation(out=sg[k][:], in_=pb[6 + k][:, :], func=AF.Silu), reads=[R_pb[6 + k]], writes=[R_sg[k]])
                            fw.op("dve", lambda h: h.tensor_tensor(out=og[k][:], in0=pb[ob][:, :], in1=sg[k][:], op=ALU.mult), reads=[R_pb[ob], R_sg[k]], writes=[R_og[k]])
                            fw.dma(obr_d[1, :, hp, qt * TT:(qt + 1) * TT], og[k][:], reads=[R_og[k]], writes=[R_obr[1][qt]], semres=R_og[k])

                    run_pipeline(units, [s0, s1, s2, s3, s4])
                fw.barrier()

        def diff_phase(l):
            lam_init = 0.8 - 0.6 * math.exp(-0.3 * l)
            with ExitStack() as ps:
                names = ["wq", "wqr", "wk", "wkr", "wv", "wg"]
                wt = {n: sbt(ps, "df_" + n, [P, DC, P], BF16) for n in names}
                R_w = {n: Res("df_" + n) for n in names}
                qT = sbt(ps, "df_qT", [P, S], BF16)
                kT = sbt(ps, "df_kT", [P, S], BF16)
                V = sbt(ps, "df_V", [P, NB, P], BF16)
                R_q = [Res(f"df_q{j}") for j in range(NT)]
                R_k = [Res(f"df_k{j}") for j in range(NT)]
                R_v = [Res(f"df_v{j}") for j in range(NT)]
                rc = [sbt(ps, f"df_rc{i}", [P, TT], F32) for i in range(2)]
                rs = [sbt(ps, f"df_rs{i}", [P, TT], F32) for i in range(2)]
                R_rc = [Res(f"df_rc{i}") for i in range(2)]
                R_rs = [Res(f"df_rs{i}") for i in range(2)]
                ta = [sbt(ps, f"df_ta{i}", [P, TT], F32) for i in range(2)]
                tb = [sbt(ps, f"df_tb{i}", [P, TT], F32) for i in range(2)]
                R_ta = [Res(f"df_ta{i}") for i in range(2)]
                NR = 3
                pp = [[sbt(ps, f"df_p{i}_{k}", [P, TT], BF16) for k in range(2)] for i in range(NR)]
                R_pp_ = [[Res(f"df_p{i}_{k}") for k in range(2)] for i in range(NR)]
                rr = sbt(ps, "df_rr", [P, TT], F32)
                R_rr = Res("df_rr")
                rb = [sbt(ps, f"df_rb{i}", [P, TT], F32) for i in range(2)]
                R_rb = [Res(f"df_rb{i}") for i in range(2)]
                bsel = sbt(ps, "df_bsel", [64, P], F32)
                R_bsel = Res("df_bsel")
                fw.op("pool", lambda h: h.memset(bsel[:], 1.0 / 32), writes=[R_bsel])
                o12 = [[sbt(ps, f"df_o{i}_{k}", [P, TT], F32) for i in range(2)] for k in range(2)]
                R_o12 = [[Res(f"df_o{i}_{k}") for i in range(2)] for k in range(2)]
                oo = sbt(ps, "df_oo", [P, TT], F32)
                osq = sbt(ps, "df_osq", [P, TT], F32)
                rstd = sbt(ps, "df_rstd", [P, TT], F32)
                sg = sbt(ps, "df_sg", [P, TT], F32)
                R_fin = Res("df_fin")
                R_sgd = Res("df_sg")
                ofin = [sbt(ps, f"df_ofin{i}", [P, TT], BF16) for i in range(2)]
                R_ofin = [Res(f"df_ofin{i}") for i in range(2)]
                lqk = sbt(ps, "df_lqk", [P, 4, 64], F32)
                R_lqk = Res("df_lqk")
                fw.dma(lqk[:], lqk_d[l], writes=[R_lqk])
                sub = sbt(ps, "df_sub", [P, 1], F32)
                R_sub = Res("df_sub")
                fw.dma(sub[:], subln_d[l], writes=[R_sub])
                pr1 = sbt(ps, "df_pr1", [P, 64], F32)
                sm = sbt(ps, "df_sm", [P, 4], F32)
                R_lam = Res("df_lam")
                fw.op("dve", lambda h: h.tensor_tensor(out=pr1[:], in0=lqk[:, 0, :], in1=lqk[:, 1, :], op=ALU.mult), reads=[R_lqk], writes=[R_lam])
                fw.op("dve", lambda h: h.reduce_sum(out=sm[:, 0:1], in_=pr1[:], axis=AX.X), reads=[R_lam], writes=[R_lam])
                fw.op("dve", lambda h: h.tensor_tensor(out=pr1[:], in0=lqk[:, 2, :], in1=lqk[:, 3, :], op=ALU.mult), reads=[R_lqk, R_lam], writes=[R_lam])
                fw.op("dve", lambda h: h.reduce_sum(out=sm[:, 1:2], in_=pr1[:], axis=AX.X), reads=[R_lam], writes=[R_lam])
                fw.op("act", lambda h: h.activation(out=sm[:, 0:2], in_=sm[:, 0:2], func=AF.Exp), reads=[R_lam], writes=[R_lam])
                fw.op("dve", lambda h: h.tensor_tensor(out=sm[:, 2:3], in0=sm[:, 1:2], in1=sm[:, 0:1], op=ALU.subtract), reads=[R_lam], writes=[R_lam])
                fw.op("dve", lambda h: h.tensor_scalar(out=sm[:, 2:3], in0=sm[:, 2:3], scalar1=-lam_init, scalar2=None, op0=ALU.add), reads=[R_lam], writes=[R_lam])
                fw.op("dve", lambda h: h.tensor_scalar(out=sub[:], in0=sub[:], scalar1=(1.0 - lam_init), scalar2=None, op0=ALU.mult), reads=[R_sub], writes=[R_sub])
                nlam = sm[:, 2:3]
                tcount = [0]
                gcount = [0]
                for hd in range(4):
                    load_wblock(l, 24 + hd, wt["wq"][:], R_w["wq"])
                    load_wblock(l, 24 + hd, wt["wqr"][:], R_w["wqr"], rot=True)
                    load_wblock(l, 28 + hd, wt["wk"][:], R_w["wk"])
                    load_wblock(l, 28 + hd, wt["wkr"][:], R_w["wkr"], rot=True)
                    load_wblock(l, 32 + hd, wt["wv"][:], R_w["wv"])
                    load_wblock(l, 36 + hd, wt["wg"][:], R_w["wg"])
                    for j in range(NT):
                        i = j % 2
                        fw.dma(rc[i][:], ropeC_d[:, j * TT:(j + 1) * TT], reads=[R_ropeC[j]], writes=[R_rc[i]])
                        fw.dma(rs[i][:], ropeS_d[:, j * TT:(j + 1) * TT], reads=[R_ropeS[j]], writes=[R_rs[i]])
                        for which, (wn, wr, dstT, R_dst, scl) in enumerate((("wq", "wqr", qT, R_q, 0.125), ("wk", "wkr", kT, R_k, 1.0))):
                            k = which
                            proj_fm(6, wt[wn], R_w[wn], j)
                            proj_fm(7, wt[wr], R_w[wr], j)
                            fw.op("dve", lambda h, k=k, scl=scl: h.scalar_tensor_tensor(out=ta[k][:], in0=pb[6][:, :], scalar=scl, in1=rc[i][:], op0=ALU.mult, op1=ALU.mult), reads=[R_pb[6], R_rc[i]], writes=[R_ta[k]])
                            fw.op("dve", lambda h, k=k, scl=scl: h.scalar_tensor_tensor(out=tb[k][:], in0=pb[7][:, :], scalar=scl, in1=rs[i][:], op0=ALU.mult, op1=ALU.mult), reads=[R_pb[7], R_rs[i]], writes=[R_ta[k]])
                            fw.op("pool", lambda h, k=k, dstT=dstT, j=j: h.tensor_tensor(out=dstT[:, j * TT:(j + 1) * TT], in0=ta[k][:], in1=tb[k][:], op=ALU.add), reads=[R_ta[k]], writes=[R_dst[j]])
                    for j in range(NT):
                        bank = 6 + (j % 2)
                        for b4 in range(4):
                            blk = j * 4 + b4
                            for c in range(DC):
                                fw.op("pe", lambda h, c=c, blk=blk, b4=b4, bank=bank: h.matmul(pb[bank][:, b4 * P:(b4 + 1) * P], lhsT=hT[:, c, blk * P:(blk + 1) * P], rhs=wt["wv"][:, c, :], start=(c == 0), stop=(c == DC - 1)),
                                      reads=[R_w["wv"], R_hT[j]], writes=[R_pb[bank]])
                        src = pb[bank][:, :].rearrange("p (b n) -> p b n", b=4)
                        if j % 2 == 0:
                            fw.op("act", lambda h, j=j, src=src: h.copy(out=V[:, j * 4:(j + 1) * 4, :], in_=src), reads=[R_pb[bank]], writes=[R_v[j]])
                        else:
                            fw.op("dve", lambda h, j=j, src=src: h.tensor_copy(out=V[:, j * 4:(j + 1) * 4, :], in_=src), reads=[R_pb[bank]], writes=[R_v[j]])
                    if "dbg_dfq" in dbg_out and hd == 0:
                        pass
                    units = []
                    for qt in range(NT):
                        nkb = 4 * qt + 4
                        for kb in range(nkb):
                            jd = kb - 4 * qt
                            units.append(dict(qt=qt, kb=kb, col0=(P * jd if jd >= 0 else 0), diag=(jd >= 0), first=(kb == 0), last=(kb == nkb - 1), n=tcount[0]))
                            tcount[0] += 1

                    def s0(u):
                        kb, qt, c0 = u["kb"], u["qt"], u["col0"]
                        t0 = qt * TT
                        for i in range(2):
                            pr = slice(64 * i, 64 * i + 64)
                            fw.op("pe", lambda h: h.matmul(pb[i][:, c0:TT], lhsT=kT[pr, kb * P:(kb + 1) * P], rhs=qT[pr, t0 + c0:t0 + TT], start=True, stop=not u["diag"]),
                                  reads=[R_k[kb // 4], R_q[qt]], writes=[R_pb[i]])
                        if u["diag"]:
                            for i in range(2):
                                fw.op("pe", lambda h: h.matmul(pb[i][:, c0:c0 + P], lhsT=ident_b[:], rhs=mask_df[:], start=False, stop=True), reads=[R_const], writes=[R_pb[i]])

                    def s1(u):
                        ii = u["n"] % NR
                        c0 = u["col0"]
                        for i in range(2):
                            fw.op("act", lambda h: h.activation(out=pp[ii][i][:, c0:TT], in_=pb[i][:, c0:TT], func=AF.Exp), reads=[R_pb[i]], writes=[R_pp_[ii][i]])

                    def s2(u):
                        ii = u["n"] % NR
                        kb, qt, c0 = u["kb"], u["qt"], u["col0"]
                        for i in range(2):
                            fw.op("pe", lambda h: h.matmul(pb[2 + i][:, c0:TT], lhsT=V[:, kb, :], rhs=pp[ii][i][:, c0:TT], start=u["first"], stop=u["last"]), reads=[R_v[kb // 4], R_pp_[ii][i]], writes=[R_pb[2 + i]])
                        for i in range(2):
                            fw.op("pe", lambda h: h.matmul(pb[4][32 * i:32 * i + 32, c0:TT], lhsT=ones_b[:, 0:32], rhs=pp[ii][i][:, c0:TT], start=u["first"], stop=u["last"]), reads=[R_const, R_pp_[ii][i]], writes=[R_pb[4]])
                        if u["last"]:
                            k2 = qt % 2
                            fw.op("dve", lambda h: h.reciprocal(out=rr[0:64, :], in_=pb[4][0:64, :]), reads=[R_pb[4]], writes=[R_rr])
                            for i in range(2):
                                fw.op("pe", lambda h: h.matmul(pb[5][:, :], lhsT=bsel[32 * i:32 * i + 32, :], rhs=rr[32 * i:32 * i + 32, :], start=True, stop=True), reads=[R_rr, R_bsel], writes=[R_pb[5]])
                                fw.op("act", lambda h: h.copy(out=rb[i][:], in_=pb[5][:, :]), reads=[R_pb[5]], writes=[R_rb[i]])
                                fw.op("dve", lambda h: h.tensor_tensor(out=o12[k2][i][:], in0=pb[2 + i][:, :], in1=rb[i][:], op=ALU.mult), reads=[R_pb[2 + i], R_rb[i]], writes=[R_o12[k2][i]])
                            o1, o2 = o12[k2][0], o12[k2][1]
                            fw.op("dve", lambda h: h.scalar_tensor_tensor(out=oo[:], in0=o2[:], scalar=nlam, in1=o1[:], op0=ALU.mult, op1=ALU.add), reads=[R_o12[k2][0], R_o12[k2][1], R_lam], writes=[R_fin])
                            fw.op("act", lambda h: h.activation(out=osq[:], in_=oo[:], func=AF.Square), reads=[R_fin], writes=[R_fin])
                            fw.op("pe", lambda h: h.matmul(pb[6][:, :], lhsT=ones_f[:], rhs=osq[:], start=True, stop=True), reads=[R_fin, R_const], writes=[R_pb[6]])
                            rsqrt_from_psum(6, 1.0 / P, rstd[:], R_fin, rstd[:])
                            proj_fm(7, wt["wg"], R_w["wg"], qt)
                            fw.op("act", lambda h: h.activation(out=sg[:], in_=pb[7][:, :], func=AF.Silu), reads=[R_pb[7]], writes=[R_sgd])
                            fw.op("dve", lambda h: h.tensor_tensor(out=oo[:], in0=oo[:], in1=rstd[:], op=ALU.mult), reads=[R_fin], writes=[R_fin])
                            fw.op("dve", lambda h: h.scalar_tensor_tensor(out=ofin[k2][:], in0=oo[:], scalar=sub[:, 0:1], in1=sg[:], op0=ALU.mult, op1=ALU.mult), reads=[R_fin, R_sub, R_sgd], writes=[R_ofin[k2]])
                            fw.dma(obr_d[2, :, hd, qt * TT:(qt + 1) * TT], ofin[k2][:], reads=[R_ofin[k2]], writes=[R_obr[2][qt]], semres=R_ofin[k2])

                    run_pipeline(units, [s0, s1, s2])
                fw.barrier()

        def merge_phase(l, last):
            with ExitStack() as ps:
                wm = sbt(ps, "mg_wm", [P, DC, 3 * D], BF16)
                R_wm = [Res(f"mg_wm{i}") for i in range(24)]
                wbr = sbt(ps, "mg_wbr", [P, 3, 4, D], BF16)
                R_wbr = Res("mg_wbr")
                bm = sbt(ps, "mg_bm", [P, 24], F32)
                R_bm = Res("mg_bm")
                fw.dma(bm[:], bmerge_d[l], writes=[R_bm])
                for cb in range(24):
                    load_wblock(l, 40 + cb, wm[:, :, cb * P:(cb + 1) * P], R_wm[cb], eng=("pool" if cb % 2 == 0 else "dve"))
                for n in range(3):
                    for mb in range(4):
                        load_generic(w_br_d[l, n, :, :, mb * 256:(mb + 1) * 256], wbr[:, n, :, mb * 256:(mb + 1) * 256], R_wbr, (4, 256), eng=("pool" if mb % 2 == 0 else "dve"))
                ot = [[sbt(ps, f"mg_ot{n}_{i}", [P, 4, TT], BF16) for n in range(3)] for i in range(2)]
                R_ot = [[Res(f"mg_ot{n}_{i}") for n in range(3)] for i in range(2)]
                mg = [sbt(ps, f"mg_mg{i}", [P, DC, TT], BF16) for i in range(2)]
                R_mgt = [Res(f"mg_mg{i}") for i in range(2)]
                gt = [sbt(ps, f"mg_gt{i}", [P, TT], F32) for i in range(3)]
                R_gt = [Res(f"mg_gt{i}") for i in range(3)]
                macc = [sbt(ps, f"mg_macc{i}", [P, TT], F32) for i in range(2)]
                R_macc = [Res(f"mg_macc{i}") for i in range(2)]
                pcount = 0
                for j in range(NT):
                    ji = j % 2
                    for n in range(3):
                        fw.dma(ot[ji][n][:], obr_d[n, :, :, j * TT:(j + 1) * TT], reads=[R_obr[n][j]], writes=[R_ot[ji][n]])
                    for db in range(DC):
                        ma = macc[db % 2]
                        R_ma = R_macc[db % 2]
                        for n in range(3):
                            k = pcount % 2
                            pcount += 1
                            bP, bL = 0 + k, 2 + k
                            for c in range(4):
                                fw.op("pe", lambda h: h.matmul(pb[bP][:, :], lhsT=wbr[:, n, c, db * P:(db + 1) * P], rhs=ot[ji][n][:, c, :], start=(c == 0), stop=(c == 3)),
                                      reads=[R_wbr, R_ot[ji][n]], writes=[R_pb[bP]])
                            cbm = n * 8 + db
                            for c in range(DC):
                                fw.op("pe", lambda h: h.matmul(pb[bL][:, :], lhsT=wm[:, c, cbm * P:(cbm + 1) * P], rhs=hT[:, c, j * TT:(j + 1) * TT], start=(c == 0), stop=(c == DC - 1)),
                                      reads=[R_wm[cbm], R_hT[j]], writes=[R_pb[bL]])
                            fw.op("act", lambda h: h.activation(out=gt[n][:], in_=pb[bL][:, :], func=AF.Sigmoid, bias=bm[:, cbm:cbm + 1]), reads=[R_pb[bL], R_bm], writes=[R_gt[n]])
                            if n == 0:
                                fw.op("dve", lambda h: h.tensor_tensor(out=ma[:], in0=pb[bP][:, :], in1=gt[n][:], op=ALU.mult), reads=[R_pb[bP], R_gt[n]], writes=[R_ma])
                            else:
                                fw.op("dve", lambda h: h.tensor_tensor(out=gt[n][:], in0=pb[bP][:, :], in1=gt[n][:], op=ALU.mult), reads=[R_pb[bP], R_gt[n]], writes=[R_gt[n]])
                                if n == 1:
                                    fw.op("pool", lambda h: h.tensor_tensor(out=ma[:], in0=ma[:], in1=gt[n][:], op=ALU.add), reads=[R_gt[n], R_ma], writes=[R_ma])
                                else:
                                    fw.op("pool", lambda h: h.tensor_tensor(out=mg[ji][:, db, :], in0=ma[:], in1=gt[n][:], op=ALU.add), reads=[R_gt[n], R_ma], writes=[R_mgt[ji]])
                    fw.dma(mg_d[:, :, j * TT:(j + 1) * TT], mg[ji][:], reads=[R_mgt[ji]], writes=[R_mgd[j]], semres=R_mgt[ji])
                fw.barrier()
            with ExitStack() as ps:
                wo = sbt(ps, "mg_wo", [P, DC, D], BF16)
                R_wo = Res("mg_wo")
                for mb in range(8):
                    load_generic(w_out_d[l, :, :, mb * P:(mb + 1) * P], wo[:, :, mb * P:(mb + 1) * P], R_wo, (DC, P), eng=("pool" if mb % 2 == 0 else "dve"))
                fnw = sbt(ps, "mg_fnw", [P, DC], F32)
                R_fnw = Res("mg_fnw")
                if last:
                    fw.dma(fnw[:], fnorm_d[:, :], writes=[R_fnw])
                xt = [sbt(ps, f"mg_xt{i}", [P, DC, TT], F32) for i in range(2)]
                R_xt = [Res(f"mg_xt{i}") for i in range(2)]
                mgi = [sbt(ps, f"mg_mgi{i}", [P, DC, TT], BF16) for i in range(2)]
                R_mgi = [Res(f"mg_mgi{i}") for i in range(2)]
                sq = [sbt(ps, f"mg_sq{i}", [P, TT], F32) for i in range(2)]
                R_sq = [Res(f"mg_sq{i}") for i in range(2)]
                rstd = sbt(ps, "mg_rstd", [P, TT], F32)
                R_rstd = Res("mg_rstd")
                ostage = [sbt(ps, f"mg_ost{i}", [P, D], F32) for i in range(2)] if last else None
                R_ost = [Res(f"mg_ost{i}") for i in range(2)]
                for j in range(NT):
                    ji = j % 2
                    x_, R_x = xt[ji], R_xt[ji]
                    fw.dma(mgi[ji][:], mg_d[:, :, j * TT:(j + 1) * TT], reads=[R_mgd[j]], writes=[R_mgi[ji]])
                    fw.dma(x_[:], xT_d[:, :, j * TT:(j + 1) * TT], reads=[R_xT[j]], writes=[R_x])
                    for ob in range(DC):
                        bO = 4 + (ob % 2)
                        for c in range(DC):
                            fw.op("pe", lambda h: h.matmul(pb[bO][:, :], lhsT=wo[:, c, ob * P:(ob + 1) * P], rhs=mgi[ji][:, c, :], start=(c == 0), stop=(c == DC - 1)),
                                  reads=[R_wo, R_mgi[ji]], writes=[R_pb[bO]])
                        fw.op("dve", lambda h: h.tensor_tensor(out=x_[:, ob, :], in0=x_[:, ob, :], in1=pb[bO][:, :], op=ALU.add), reads=[R_pb[bO], R_x], writes=[R_x])
                    dump("dbg_x1", x_[:], R_x, lambda o: o[:, :, j * TT:(j + 1) * TT])
                    norm_tile(x_, R_x, j, sq, R_sq, rstd, R_rstd, 6)
                    if not last:
                        fw.dma(xT_d[:, :, j * TT:(j + 1) * TT], x_[:], reads=[R_x], writes=[R_xT[j]], semres=R_x)
                        fw.op("dve", lambda h: h.tensor_tensor(out=hT[:, :, j * TT:(j + 1) * TT], in0=x_[:], in1=rstd[:].unsqueeze(1).broadcast_to([P, DC, TT]), op=ALU.mult),
                              reads=[R_x, R_rstd], writes=[R_hT[j]])
                    else:
                        for c in range(DC):
                            fw.op("dve", lambda h: h.scalar_tensor_tensor(out=x_[:, c, :], in0=x_[:, c, :], scalar=fnw[:, c:c + 1], in1=rstd[:], op0=ALU.mult, op1=ALU.mult), reads=[R_x, R_rstd, R_fnw], writes=[R_x])
                        for blk in range(4):
                            oi = blk % 2
                            for half in range(2):
                                bank = 0 + 2 * (blk % 2) + half
                                for cc in range(4):
                                    c = half * 4 + cc
                                    fw.op("pe", lambda h: h.transpose(pb[bank][:, cc * P:(cc + 1) * P], x_[:, c, blk * P:(blk + 1) * P], ident_f[:]),
                                          reads=[R_x, R_const], writes=[R_pb[bank]])
                                if half == 0:
                                    fw.op("act", lambda h: h.copy(out=ostage[oi][:, 0:512], in_=pb[bank][:, :]), reads=[R_pb[bank]], writes=[R_ost[oi]])
                                else:
                                    fw.op("dve", lambda h: h.tensor_copy(out=ostage[oi][:, 512:1024], in_=pb[bank][:, :]), reads=[R_pb[bank]], writes=[R_ost[oi]])
                            g = j * 4 + blk
                            fw.dma(y_out[g * P:(g + 1) * P, :], ostage[oi][:], reads=[R_ost[oi]], final=True, semres=R_ost[oi])
                fw.barrier()

        consts()
        phase0()
        for l in range(nlayers):
            if l > 0:
                fw.dma(nw[:], normw_d[l], writes=[R_nw], persistent=True)
            ssm_phase(l)
            if stop_after == "ssm":
                break
            sb_phase(l)
            if stop_after == "sb":
                break
            diff_phase(l)
            if stop_after == "diff":
                break
            merge_phase(l, last=(l == nlayers - 1))
        fw.finish()
        build_program.ninst = fw.ninst
        build_program.nsem = fw.nsem
    return nc


def prep_inputs(x, norm_w, w_in, b_merge, ssm_a_re, ssm_a_im, ssm_log_dt, ssm_b_re, ssm_b_im, ssm_c_re, ssm_c_im, ssm_d,
                ssm_w_glu, ssm_b_glu, diff_lq1, diff_lk1, diff_lq2, diff_lk2, diff_subln_w, w_branch, w_out, final_norm_w):
    f = lambda a: np.ascontiguousarray(np.asarray(a, dtype=np.float32))
    L = DEPTH
    w_in = np.asarray(w_in, dtype=np.float32)
    shared = {}
    shared["w_in"] = f(w_in.reshape(L, DC, P, 64, P).transpose(0, 3, 2, 1, 4))
    shared["w_glu"] = f(np.asarray(ssm_w_glu, np.float32).reshape(L, 4, P, 512).transpose(0, 2, 1, 3))
    shared["w_br"] = f(np.asarray(w_branch, np.float32).reshape(L, 3, 4, P, D).transpose(0, 1, 3, 2, 4))
    shared["w_out"] = f(np.asarray(w_out, np.float32).reshape(L, DC, P, D).transpose(0, 2, 1, 3))
    shared["norm_w"] = f(np.asarray(norm_w, np.float32).reshape(L, DC, P).transpose(0, 2, 1))
    shared["b_merge"] = f(np.asarray(b_merge, np.float32).reshape(L, 24, P).transpose(0, 2, 1))
    shared["b_glu"] = f(np.asarray(ssm_b_glu, np.float32).reshape(L, 4, P).transpose(0, 2, 1))
    shared["subln_w"] = f(np.asarray(diff_subln_w, np.float32).reshape(L, P, 1))
    shared["fnorm_w"] = f(np.asarray(final_norm_w, np.float32).reshape(DC, P).T)
    def gp(a):
        return f(np.asarray(a, np.float32).reshape(L, 16, 2, 64).transpose(0, 2, 3, 1).reshape(L, P, 16))
    shared["a_re"] = gp(ssm_a_re)
    shared["a_im"] = gp(ssm_a_im)
    shared["log_dt"] = gp(np.broadcast_to(np.asarray(ssm_log_dt, np.float32)[:, :, None], (L, 32, 64)))
    def gpc(a):
        return f(np.asarray(a, np.float32).reshape(L, 16, 2, 64, 16).transpose(0, 2, 3, 1, 4).reshape(L, P, 16, 16))
    shared["b_re"] = gpc(ssm_b_re)
    shared["b_im"] = gpc(ssm_b_im)
    shared["c_re"] = gpc(np.asarray(ssm_c_re, np.float32).transpose(0, 1, 3, 2))
    shared["c_im"] = gpc(np.asarray(ssm_c_im, np.float32).transpose(0, 1, 3, 2))
    shared["d_skip"] = f(np.asarray(ssm_d, np.float32).reshape(L, 4, 8, 16).transpose(0, 2, 3, 1).reshape(L, P, 4))
    lqk = np.stack([np.asarray(a, np.float32) for a in (diff_lq1, diff_lk1, diff_lq2, diff_lk2)], axis=1)
    shared["lqk"] = f(np.broadcast_to(lqk[:, None, :, :], (L, P, 4, 64)))
    xs = np.asarray(x, dtype=np.float32)
    return shared, xs


_PROGRAM_CACHE = {}


def kernel(**inputs):
    shared, xs = prep_inputs(**inputs)
    B = xs.shape[0]
    if "nc" not in _PROGRAM_CACHE:
        _PROGRAM_CACHE["nc"] = build_program()
    nc = _PROGRAM_CACHE["nc"]
    n_cores = 8
    in_maps = []
    for c in range(n_cores):
        m = dict(shared)
        m["x"] = np.ascontiguousarray(xs[c % B])
        in_maps.append(m)
    res = run_bass_kernel_spmd(nc, in_maps, core_ids=list(range(n_cores)))
    out = np.stack([np.asarray(res.results[b]["y"], dtype=np.float32) for b in range(B)], axis=0)
    return out
```

```python
import math
from contextlib import ExitStack

import numpy as np
import concourse.bass as bass
import concourse.mybir as mybir
from concourse.bass_utils import run_bass_kernel_spmd

F32 = mybir.dt.float32
BF16 = mybir.dt.bfloat16
I32 = mybir.dt.int32
AF = mybir.ActivationFunctionType
ALU = mybir.AluOpType
AX = mybir.AxisListType

P = 128
S = 4096
D = 1024
TT = 512
NT = S // TT
NB = S // P
DC = D // P
DEPTH = 4
NPROJ = 8192
EPS = 1e-6
MAGIC = 12582912.0
TWO_PI = 2.0 * math.pi
NEG = -30000.0
SEM_CAP = 30000


class Res:
    __slots__ = ("name", "lw", "readers", "excl", "slot")

    def __init__(self, name, excl=False):
        self.name = name
        self.lw = None
        self.readers = []
        self.excl = excl
        self.slot = None


class SemSlot:
    __slots__ = ("sem", "cnt")

    def __init__(self, sem):
        self.sem = sem
        self.cnt = 0


class Eng:
    def __init__(self, fw, name, handle):
        self.fw = fw
        self.name = name
        self.h = handle
        self.sems = []
        self.count = 0
        self.waited = {}

    def cur_event(self):
        n = self.count + 1
        k = (n - 1) // SEM_CAP
        while len(self.sems) <= k:
            self.sems.append(self.fw.new_sem(f"{self.name}{len(self.sems)}"))
        return (self.sems[k], n - k * SEM_CAP, self.name)

    def last_event(self):
        if self.count == 0:
            return None
        n = self.count
        k = (n - 1) // SEM_CAP
        return (self.sems[k], n - k * SEM_CAP, self.name)


class FW:
    def __init__(self, nc, es):
        self.nc = nc
        self.es = es
        self.nsem = 0
        self.E = {
            "pe": Eng(self, "pe", nc.tensor),
            "act": Eng(self, "act", nc.scalar),
            "dve": Eng(self, "dve", nc.vector),
            "pool": Eng(self, "pool", nc.gpsimd),
            "sp": Eng(self, "sp", nc.sync),
        }
        self.out_events = []
        self.free_slots = []
        self.all_slots = []
        self.phase_res = []
        self.ninst = 0

    def new_sem(self, name):
        self.nsem += 1
        return self.es.enter_context(self.nc.semaphore(f"s{self.nsem}_{name}"))

    def get_slot(self, res, phase_local=True):
        if res.slot is None:
            if self.free_slots:
                res.slot = self.free_slots.pop()
            else:
                res.slot = SemSlot(self.new_sem("dma"))
                self.all_slots.append(res.slot)
            if phase_local:
                self.phase_res.append(res)
        return res.slot

    def _need(self, eng, ev, waits):
        if ev is None:
            return
        sem, val, _ = ev
        key = id(sem)
        if eng.waited.get(key, 0) >= val:
            return
        cur = waits.get(key)
        if cur is None or cur[1] < val:
            waits[key] = (sem, val)

    def _collect(self, eng, reads, writes, waits):
        for r in reads:
            if r.lw is not None:
                self._need(eng, r.lw, waits)
            if r.excl:
                for ev in r.readers:
                    if ev[2] != eng.name:
                        self._need(eng, ev, waits)
        for w in writes:
            if w.lw is not None and w.lw[2] != eng.name:
                self._need(eng, w.lw, waits)
            for ev in w.readers:
                if ev[2] != eng.name:
                    self._need(eng, ev, waits)

    def _emit_waits(self, eng, waits):
        for key, (sem, val) in waits.items():
            eng.h.wait_ge(sem, val)
            eng.waited[key] = val
            self.ninst += 1

    def _update(self, ev, reads, writes):
        for r in reads:
            r.readers.append(ev)
            if len(r.readers) > 16:
                last = {}
                for e in r.readers:
                    k = (e[2], id(e[0]))
                    if k not in last or last[k][1] < e[1]:
                        last[k] = e
                r.readers = list(last.values())
        for w in writes:
            w.lw = ev
            w.readers = []

    def op(self, engname, fn, reads=(), writes=()):
        eng = self.E[engname]
        waits = {}
        self._collect(eng, reads, writes, waits)
        self._emit_waits(eng, waits)
        ins = fn(eng.h)
        ev = eng.cur_event()
        ins.then_inc(ev[0], 1)
        eng.count += 1
        self._update(ev, reads, writes)
        self.ninst += 1
        return ev

    def dma(self, out, in_, reads=(), writes=(), q="sp", semres=None, final=False, persistent=False, **kw):
        eng = self.E[q]
        if semres is None:
            semres = writes[0] if writes else reads[0]
        slot = self.get_slot(semres, phase_local=not persistent)
        if slot.cnt >= 1800:
            slot2 = SemSlot(self.new_sem("dma"))
            self.all_slots.append(slot2)
            semres.slot = slot2
            old = slot
            slot = slot2
            waits0 = {}
            self._need(eng, (old.sem, 16 * old.cnt, "dma"), waits0)
            self._emit_waits(eng, waits0)
        waits = {}
        self._collect(eng, reads, writes, waits)
        if slot.cnt > 0:
            self._need(eng, (slot.sem, 16 * slot.cnt, "dma"), waits)
        self._emit_waits(eng, waits)
        eng.h.dma_start(out=out, in_=in_, **kw).then_inc(slot.sem, 16)
        slot.cnt += 1
        ev = (slot.sem, 16 * slot.cnt, "dma")
        self._update(ev, reads, writes)
        if final:
            self.out_events.append(ev)
        self.ninst += 1
        return ev

    def barrier(self):
        evs = []
        for e in self.E.values():
            le = e.last_event()
            if le is not None:
                evs.append(le)
        for s in self.all_slots:
            if s.cnt > 0:
                evs.append((s.sem, 16 * s.cnt, "dma"))
        for e in self.E.values():
            waits = {}
            for ev in evs:
                if ev[2] == e.name:
                    continue
                self._need(e, ev, waits)
            self._emit_waits(e, waits)
        for r in self.phase_res:
            if r.slot is not None:
                self.free_slots.append(r.slot)
                r.slot = None
        self.phase_res = []

    def finish(self):
        eng = self.E["sp"]
        waits = {}
        for ev in self.out_events:
            self._need(eng, ev, waits)
        self._emit_waits(eng, waits)


def pipeline_steps(units, stages):
    n = len(units)
    ns = len(stages)
    for step in range(n + ns - 1):
        for s_ in range(ns - 1, -1, -1):
            u = step - s_
            if 0 <= u < n:
                stages[s_](units[u])
        yield


def drive(ga, gb, pattern="ababa"):
    live = {"a": True, "b": True}
    gens = {"a": ga, "b": gb}
    while live["a"] or live["b"]:
        for ch in pattern:
            if live[ch]:
                try:
                    next(gens[ch])
                except StopIteration:
                    live[ch] = False


def run_pipeline(units, stages):
    n = len(units)
    ns = len(stages)
    for step in range(n + ns - 1):
        for s in range(ns - 1, -1, -1):
            u = step - s
            if 0 <= u < n:
                stages[s](units[u])


def build_program(nlayers=DEPTH, dbg=None, stop_after=None):
    nc = bass.Bass("TRN2", target_bir_lowering=False)
    dbg = dbg or {}

    def din(name, shape):
        return nc.dram_tensor(name, list(shape), F32, kind="ExternalInput").ap()

    x_in = din("x", [S, D])
    w_in_d = din("w_in", [DEPTH, 64, P, DC, P])
    w_glu_d = din("w_glu", [DEPTH, P, 4, 512])
    w_br_d = din("w_br", [DEPTH, 3, P, 4, D])
    w_out_d = din("w_out", [DEPTH, P, DC, D])
    normw_d = din("norm_w", [DEPTH, P, DC])
    bmerge_d = din("b_merge", [DEPTH, P, 24])
    bglu_d = din("b_glu", [DEPTH, P, 4])
    subln_d = din("subln_w", [DEPTH, P, 1])
    fnorm_d = din("fnorm_w", [P, DC])
    are_d = din("a_re", [DEPTH, P, 16])
    aim_d = din("a_im", [DEPTH, P, 16])
    ldt_d = din("log_dt", [DEPTH, P, 16])
    bre_d = din("b_re", [DEPTH, P, 16, 16])
    bim_d = din("b_im", [DEPTH, P, 16, 16])
    cre_d = din("c_re", [DEPTH, P, 16, 16])
    cim_d = din("c_im", [DEPTH, P, 16, 16])
    dsk_d = din("d_skip", [DEPTH, P, 4])
    lqk_d = din("lqk", [DEPTH, P, 4, 64])

    y_out = nc.dram_tensor("y", [S, D], F32, kind="ExternalOutput").ap()
    dbg_out = {k: nc.dram_tensor(k, list(v), F32, kind="ExternalOutput").ap() for k, v in dbg.items()}

    xT_d = nc.dram_tensor("xT_scr", [P, DC, S], F32).ap()
    obr_d = nc.dram_tensor("obr_scr", [3, P, 4, S], BF16).ap()
    yg_d = nc.dram_tensor("yg_scr", [P, 4, S], BF16).ap()
    mg_d = nc.dram_tensor("mg_scr", [P, DC, S], BF16).ap()
    ropeC_d = nc.dram_tensor("ropeC_scr", [P, S], F32).ap()
    ropeS_d = nc.dram_tensor("ropeS_scr", [P, S], F32).ap()

    es = ExitStack()
    with es:
        fw = FW(nc, es)

        _uid = [0]

        def sbt(stack, name, shape, dt):
            _uid[0] += 1
            return stack.enter_context(nc.sbuf_tensor(f"{name}_{_uid[0]}", list(shape), dt))

        hT = sbt(es, "hT", [P, DC, S], BF16)
        R_hT = [Res(f"hT{j}") for j in range(NT)]
        ident_f = sbt(es, "ident_f", [P, P], F32)
        ident_b = sbt(es, "ident_b", [P, P], BF16)
        ones_f = sbt(es, "ones_f", [P, P], F32)
        ones_b = sbt(es, "ones_b", [P, P], BF16)
        nones_b = sbt(es, "nones_b", [P, P], BF16)
        ntri_b = sbt(es, "ntri_b", [P, P], BF16)
        mask_sb = sbt(es, "mask_sb", [P, P], BF16)
        mask_df = sbt(es, "mask_df", [P, P], BF16)
        iota_t = sbt(es, "iota_t", [P, TT], F32)
        halfpi = sbt(es, "halfpi", [P, 1], F32)
        R_const = Res("const")
        NST = 2
        wst = [sbt(es, f"wst{i}", [P, DC, P], F32) for i in range(NST)]
        R_wst = [Res(f"wst{i}") for i in range(NST)]
        wst_i = [0]
        nw = sbt(es, "nw", [P, DC], F32)
        R_nw = Res("nw")

        pb = [es.enter_context(nc.psum_tensor(f"pb{i}", [P, TT], F32)) for i in range(8)]
        R_pb = [Res(f"pb{i}", excl=True) for i in range(8)]

        def consts():
            with ExitStack() as ps:
                io_i = sbt(ps, "io_i", [P, P], I32)
                io_f = sbt(ps, "io_f", [P, P], F32)
                io2_i = sbt(ps, "io2_i", [P, TT], I32)
                R_a, R_b, R_c = Res("io_i"), Res("io_f"), Res("io2")
                fw.op("pool", lambda h: h.iota(io_i[:], pattern=[[1, P]], base=0, channel_multiplier=-1), writes=[R_a])
                fw.op("dve", lambda h: h.tensor_copy(out=io_f[:], in_=io_i[:]), reads=[R_a], writes=[R_b])
                W = [R_const]
                fw.op("dve", lambda h: h.tensor_scalar(out=ident_f[:], in0=io_f[:], scalar1=0.0, scalar2=None, op0=ALU.is_equal), reads=[R_b], writes=W)
                fw.op("dve", lambda h: h.tensor_scalar(out=ident_b[:], in0=io_f[:], scalar1=0.0, scalar2=None, op0=ALU.is_equal), reads=[R_b], writes=W)
                fw.op("dve", lambda h: h.tensor_scalar(out=ntri_b[:], in0=io_f[:], scalar1=0.0, scalar2=-1.0, op0=ALU.is_le, op1=ALU.mult), reads=[R_b], writes=W)
                fw.op("dve", lambda h: h.tensor_scalar(out=mask_sb[:], in0=io_f[:], scalar1=0.0, scalar2=NEG, op0=ALU.is_le, op1=ALU.mult), reads=[R_b], writes=W)
                fw.op("dve", lambda h: h.tensor_scalar(out=mask_df[:], in0=io_f[:], scalar1=0.0, scalar2=NEG, op0=ALU.is_lt, op1=ALU.mult), reads=[R_b], writes=W)
                fw.op("pool", lambda h: h.memset(ones_f[:], 1.0), writes=W)
                fw.op("pool", lambda h: h.memset(ones_b[:], 1.0), writes=W)
                fw.op("pool", lambda h: h.memset(nones_b[:], -1.0), writes=W)
                fw.op("pool", lambda h: h.memset(halfpi[:], math.pi / 2), writes=W)
                fw.op("pool", lambda h: h.iota(io2_i[:], pattern=[[1, TT]], base=0, channel_multiplier=0), writes=[R_c])
                fw.op("dve", lambda h: h.tensor_copy(out=iota_t[:], in_=io2_i[:]), reads=[R_c], writes=W)
                pidx_i = sbt(ps, "pidx_i", [P, 1], I32)
                pidx = sbt(ps, "pidx", [P, 1], F32)
                wfreq = sbt(ps, "wfreq", [P, 1], F32)
                R_p = Res("pidx")
                fw.op("pool", lambda h: h.iota(pidx_i[:], pattern=[[0, 1]], base=0, channel_multiplier=1), writes=[R_p])
                fw.op("dve", lambda h: h.tensor_copy(out=pidx[:], in_=pidx_i[:]), reads=[R_p], writes=[R_p])
                t0 = sbt(ps, "t0", [P, 1], F32)
                fw.op("dve", lambda h: h.tensor_scalar(out=t0[:], in0=pidx[:], scalar1=-15.5, scalar2=1.0 / 32, op0=ALU.add, op1=ALU.mult), reads=[R_p], writes=[R_p])
                fw.op("dve", lambda h: h.tensor_scalar(out=t0[:], in0=t0[:], scalar1=MAGIC, scalar2=MAGIC, op0=ALU.add, op1=ALU.subtract), reads=[R_p], writes=[R_p])
                fw.op("dve", lambda h: h.scalar_tensor_tensor(out=pidx[:], in0=t0[:], scalar=-32.0, in1=pidx[:], op0=ALU.mult, op1=ALU.add), reads=[R_p], writes=[R_p])
                fw.op("act", lambda h: h.activation(out=wfreq[:], in_=pidx[:], func=AF.Exp, scale=-math.log(10000.0) / 32), reads=[R_p], writes=[R_p])
                fw.op("dve", lambda h: h.tensor_scalar(out=wfreq[:], in0=wfreq[:], scalar1=1.0 / TWO_PI, scalar2=None, op0=ALU.mult), reads=[R_p], writes=[R_p])
                tpos = sbt(ps, "tpos", [P, TT], F32)
                t1 = sbt(ps, "t1", [P, TT], F32)
                kk = sbt(ps, "kk", [P, TT], F32)
                ff = sbt(ps, "ff", [P, TT], F32)
                tS = sbt(ps, "tS", [P, TT], F32)
                tC = sbt(ps, "tC", [P, TT], F32)
                R_t = Res("ropetmp")
                R_tS, R_tC = Res("tS"), Res("tC")
                for j in range(NT):
                    fw.op("dve", lambda h: h.tensor_scalar(out=tpos[:], in0=iota_t[:], scalar1=float(j * TT), scalar2=None, op0=ALU.add), reads=[R_const], writes=[R_t])
                    fw.op("dve", lambda h: h.tensor_scalar(out=t1[:], in0=tpos[:], scalar1=wfreq[:, 0:1], scalar2=MAGIC, op0=ALU.mult, op1=ALU.add), reads=[R_t, R_p], writes=[R_t])
                    fw.op("dve", lambda h: h.tensor_scalar(out=kk[:], in0=t1[:], scalar1=MAGIC, scalar2=None, op0=ALU.subtract), reads=[R_t], writes=[R_t])
                    fw.op("dve", lambda h: h.scalar_tensor_tensor(out=ff[:], in0=tpos[:], scalar=wfreq[:, 0:1], in1=kk[:], op0=ALU.mult, op1=ALU.subtract), reads=[R_t, R_p], writes=[R_t])
                    fw.op("act", lambda h: h.activation(out=tS[:], in_=ff[:], func=AF.Sin, scale=TWO_PI), reads=[R_t], writes=[R_tS])
                    fw.op("dve", lambda h: h.scalar_tensor_tensor(out=ff[:], in0=ff[:], scalar=-1.0, in1=ff[:], op0=ALU.mult, op1=ALU.max), reads=[R_t], writes=[R_t])
                    fw.op("act", lambda h: h.activation(out=tC[:], in_=ff[:], func=AF.Sin, scale=-TWO_PI, bias=halfpi[:]), reads=[R_t, R_const], writes=[R_tC])
                    fw.dma(ropeS_d[:, j * TT:(j + 1) * TT], tS[:], reads=[R_tS], writes=[R_ropeS[j]], semres=R_tS)
                    fw.dma(ropeC_d[:, j * TT:(j + 1) * TT], tC[:], reads=[R_tC], writes=[R_ropeC[j]], semres=R_tC)
                fw.barrier()

        R_ropeS = [Res(f"ropeS{j}") for j in range(NT)]
        R_ropeC = [Res(f"ropeC{j}") for j in range(NT)]
        R_xT = [Res(f"xT{j}") for j in range(NT)]
        R_obr = [[Res(f"obr{n}_{j}") for j in range(NT)] for n in range(3)]
        R_yg = [Res(f"yg{j}") for j in range(NT)]
        R_mgd = [Res(f"mgd{j}") for j in range(NT)]

        def load_wblock(l, cb, dst, R_dst, rot=False, scale_nw=True, eng="pool"):
            i = wst_i[0] % NST
            wst_i[0] += 1
            st, R_st = wst[i], R_wst[i]
            fw.dma(st[:], w_in_d[l, cb], writes=[R_st], persistent=True)
            nwb = nw[:].unsqueeze(2).broadcast_to([P, DC, P])
            if not rot:
                fw.op(eng, lambda h: h.tensor_tensor(out=dst, in0=st[:], in1=nwb, op=ALU.mult), reads=[R_st, R_nw], writes=[R_dst])
            else:
                for hh in range(2):
                    b0 = hh * 64
                    nwb32 = nw[:].unsqueeze(2).broadcast_to([P, DC, 32])
                    fw.op("dve", lambda h: h.scalar_tensor_tensor(out=dst[:, :, b0:b0 + 32], in0=st[:, :, b0 + 32:b0 + 64], scalar=-1.0, in1=nwb32, op0=ALU.mult, op1=ALU.mult),
                          reads=[R_st, R_nw], writes=[R_dst])
                    fw.op(eng, lambda h: h.tensor_tensor(out=dst[:, :, b0 + 32:b0 + 64], in0=st[:, :, b0:b0 + 32], in1=nwb32, op=ALU.mult),
                          reads=[R_st, R_nw], writes=[R_dst])

        def load_generic(src_ap, dst, R_dst, shape, eng="pool"):
            i = wst_i[0] % NST
            wst_i[0] += 1
            st, R_st = wst[i], R_wst[i]
            a, b = shape
            sv = st[:].rearrange("p c n -> p (c n)")[:, 0:a * b].rearrange("p (a b) -> p a b", a=a)
            fw.dma(sv, src_ap, writes=[R_st], persistent=True)
            fw.op(eng, lambda h: h.tensor_copy(out=dst, in_=sv), reads=[R_st], writes=[R_dst])

        def proj_fm(bank, wbf, R_w, j, M=P, c0=0):
            for c in range(DC):
                fw.op("pe", lambda h, c=c: h.matmul(pb[bank][0:M, :], lhsT=wbf[:, c, c0:c0 + M], rhs=hT[:, c, j * TT:(j + 1) * TT], start=(c == 0), stop=(c == DC - 1)),
                      reads=[R_w, R_hT[j]], writes=[R_pb[bank]])

        def dump(name, src_ap, R_src, dst_slice):
            if name in dbg_out:
                fw.dma(dst_slice(dbg_out[name]), src_ap, reads=[R_src], final=True, semres=R_src)

        def rsqrt_from_psum(bank, scale, dst, R_dst, tmp):
            fw.op("dve", lambda h: h.tensor_scalar(out=tmp, in0=pb[bank][:, :], scalar1=scale, scalar2=EPS, op0=ALU.mult, op1=ALU.add), reads=[R_pb[bank]], writes=[R_dst])
            fw.op("act", lambda h: h.activation(out=tmp, in_=tmp, func=AF.Sqrt), reads=[R_dst], writes=[R_dst])
            fw.op("dve", lambda h: h.reciprocal(out=dst, in_=tmp), reads=[R_dst], writes=[R_dst])

        def norm_tile(xt, R_xt, j, sq, R_sq, rstd, R_rstd, bank):
            for c in range(DC):
                i = c % 2
                fw.op("act", lambda h, c=c, i=i: h.activation(out=sq[i][:], in_=xt[:, c, :], func=AF.Square), reads=[R_xt], writes=[R_sq[i]])
                fw.op("pe", lambda h, c=c, i=i: h.matmul(pb[bank][:, :], lhsT=ones_f[:], rhs=sq[i][:], start=(c == 0), stop=(c == DC - 1)),
                      reads=[R_sq[i], R_const], writes=[R_pb[bank]])
            rsqrt_from_psum(bank, 1.0 / D, rstd[:], R_rstd, rstd[:])

        def phase0():
            with ExitStack() as ps:
                xin = [sbt(ps, f"xin{i}", [P, D], F32) for i in range(2)]
                R_xin = [Res(f"xin{i}") for i in range(2)]
                xt = [sbt(ps, f"xt0_{i}", [P, DC, TT], F32) for i in range(2)]
                R_xt = [Res(f"xt0_{i}") for i in range(2)]
                sq = [sbt(ps, f"sq0_{i}", [P, TT], F32) for i in range(2)]
                R_sq = [Res(f"sq0_{i}") for i in range(2)]
                rstd = sbt(ps, "rstd0", [P, TT], F32)
                R_rstd = Res("rstd0")
                fw.dma(nw[:], normw_d[0], writes=[R_nw], persistent=True)
                for j in range(NT):
                    xb_, R_xb = xt[j % 2], R_xt[j % 2]
                    for blk in range(4):
                        g = j * 4 + blk
                        xi, R_xi = xin[g % 2], R_xin[g % 2]
                        fw.dma(xi[:], x_in[g * P:(g + 1) * P, :], writes=[R_xi])
                        for half in range(2):
                            bank = (g % 2) * 2 + half
                            for cc in range(4):
                                c = half * 4 + cc
                                fw.op("pe", lambda h, c=c, cc=cc, bank=bank: h.transpose(pb[bank][:, cc * P:(cc + 1) * P], xi[:, c * P:(c + 1) * P], ident_f[:]),
                                      reads=[R_xi, R_const], writes=[R_pb[bank]])
                            eng = "act" if half == 0 else "dve"
                            src = pb[bank][:, :].rearrange("p (c t) -> p c t", c=4)
                            dst = xb_[:, half * 4:half * 4 + 4, blk * P:(blk + 1) * P]
                            if eng == "act":
                                fw.op("act", lambda h, src=src, dst=dst: h.copy(out=dst, in_=src), reads=[R_pb[bank]], writes=[R_xb])
                            else:
                                fw.op("dve", lambda h, src=src, dst=dst: h.tensor_copy(out=dst, in_=src), reads=[R_pb[bank]], writes=[R_xb])
                    fw.dma(xT_d[:, :, j * TT:(j + 1) * TT], xb_[:], reads=[R_xb], writes=[R_xT[j]], semres=R_xb)
                    norm_tile(xb_, R_xb, j, sq, R_sq, rstd, R_rstd, 4)
                    fw.op("dve", lambda h, xb_=xb_, j=j: h.tensor_tensor(out=hT[:, :, j * TT:(j + 1) * TT], in0=xb_[:], in1=rstd[:].unsqueeze(1).broadcast_to([P, DC, TT]), op=ALU.mult),
                          reads=[R_xb, R_rstd], writes=[R_hT[j]])
                fw.barrier()

        ST = 256
        NSEG = S // ST

        def ssm_phase(l):
            with ExitStack() as ps:
                w_u = [sbt(ps, f"w_u{i}", [P, DC, P], BF16) for i in range(4)]
                R_wu = [Res(f"w_u{i}") for i in range(4)]
                for cb in range(4):
                    load_wblock(l, cb, w_u[cb][:], R_wu[cb])
                def t16(name):
                    return sbt(ps, name, [P, 16], F32)
                are, aim, ldt = t16("are"), t16("aim"), t16("ldt")
                R_pp = Res("ssm_params")
                R_ld = [Res(f"ssm_ld{i}") for i in range(8)]
                fw.dma(are[:], are_d[l], writes=[R_ld[0]])
                fw.dma(aim[:], aim_d[l], writes=[R_ld[1]])
                fw.dma(ldt[:], ldt_d[l], writes=[R_ld[2]])
                cre = sbt(ps, "cre", [P, 16, 16], F32)
                cim = sbt(ps, "cim", [P, 16, 16], F32)
                dsk = sbt(ps, "dsk", [P, 4], F32)
                fw.dma(cre[:], cre_d[l], writes=[R_ld[5]])
                fw.dma(cim[:], cim_d[l], writes=[R_ld[6]])
                fw.dma(dsk[:], dsk_d[l], writes=[R_ld[7]])
                RL = R_ld
                dt_, th, lr, rho, wturn = t16("dt_"), t16("th"), t16("lr"), t16("rho"), t16("wturn")
                W = [R_pp]

                def dv(fn, reads=()):
                    fw.op("dve", fn, reads=list(reads) + [R_pp], writes=W)

                def ac(fn, reads=()):
                    fw.op("act", fn, reads=list(reads) + [R_pp], writes=W)

                ac(lambda h: h.activation(out=dt_[:], in_=ldt[:], func=AF.Exp), [RL[2]])
                dv(lambda h: h.tensor_tensor(out=th[:], in0=dt_[:], in1=aim[:], op=ALU.mult), [RL[1]])
                dv(lambda h: h.tensor_tensor(out=lr[:], in0=dt_[:], in1=are[:], op=ALU.mult), [RL[0]])
                ac(lambda h: h.activation(out=rho[:], in_=lr[:], func=AF.Exp))
                dv(lambda h: h.tensor_scalar(out=wturn[:], in0=th[:], scalar1=1.0 / TWO_PI, scalar2=None, op0=ALU.mult))

                def sincos(wt, s_out, c_out, tmpk, tmpf, mult=1.0):
                    dv(lambda h: h.tensor_scalar(out=tmpf[:], in0=wt[:], scalar1=mult, scalar2=None, op0=ALU.mult))
                    dv(lambda h: h.tensor_scalar(out=tmpk[:], in0=tmpf[:], scalar1=MAGIC, scalar2=MAGIC, op0=ALU.add, op1=ALU.subtract))
                    dv(lambda h: h.tensor_tensor(out=tmpf[:], in0=tmpf[:], in1=tmpk[:], op=ALU.subtract))
                    ac(lambda h: h.activation(out=s_out[:], in_=tmpf[:], func=AF.Sin, scale=TWO_PI))
                    dv(lambda h: h.scalar_tensor_tensor(out=tmpf[:], in0=tmpf[:], scalar=-1.0, in1=tmpf[:], op0=ALU.mult, op1=ALU.max))
                    ac(lambda h: h.activation(out=c_out[:], in_=tmpf[:], func=AF.Sin, scale=-TWO_PI, bias=halfpi[:]), [R_const])

                sin1, cos1, s512, c512, tk, tf = t16("sin1"), t16("cos1"), t16("s512"), t16("c512"), t16("tk"), t16("tf")
                sincos(wturn, sin1, cos1, tk, tf, 1.0)
                sincos(wturn, s512, c512, tk, tf, float(ST))
                CTr = sbt(ps, "CTr", [P, 4, 4, 4, 32], BF16)
                CTi = sbt(ps, "CTi", [P, 4, 4, 4, 32], BF16)
                CTn = sbt(ps, "CTn", [P, 4, 4, 4, 32], BF16)
                R_CT = Res("CT")
                BTr = sbt(ps, "BTr", [P, 16, P], BF16)
                BTi = sbt(ps, "BTi", [P, 16, P], BF16)
                R_BT = Res("BT")
                carry = sbt(ps, "carry", [P, 16, 2], F32)
                R_carry = [Res(f"carry{i}") for i in range(16)]
                ctmp = sbt(ps, "ctmp", [P, 2], F32)
                R_ctmp = Res("ctmp")
                fw.op("pool", lambda h: h.memset(carry[:], 0.0), writes=R_carry)
                with ExitStack() as pp_:
                    bre = sbt(pp_, "bre", [P, 16, 16], F32)
                    bim = sbt(pp_, "bim", [P, 16, 16], F32)
                    fw.dma(bre[:], bre_d[l], writes=[R_ld[3]])
                    fw.dma(bim[:], bim_d[l], writes=[R_ld[4]])

                    def t16p(name):
                        return sbt(pp_, name, [P, 16], F32)
                    Are, Aim, den, am1, cr, ci, tmpa, tmpb = (t16p(n) for n in ("Are", "Aim", "den", "am1", "cr", "ci", "tmpa", "tmpb"))
                    dv(lambda h: h.tensor_tensor(out=Are[:], in0=rho[:], in1=cos1[:], op=ALU.mult))
                    dv(lambda h: h.tensor_tensor(out=Aim[:], in0=rho[:], in1=sin1[:], op=ALU.mult))
                    dv(lambda h: h.tensor_tensor(out=den[:], in0=are[:], in1=are[:], op=ALU.mult))
                    dv(lambda h: h.tensor_tensor(out=tmpa[:], in0=aim[:], in1=aim[:], op=ALU.mult))
                    dv(lambda h: h.tensor_tensor(out=den[:], in0=den[:], in1=tmpa[:], op=ALU.add))
                    dv(lambda h: h.reciprocal(out=den[:], in_=den[:]))
                    dv(lambda h: h.tensor_scalar(out=am1[:], in0=Are[:], scalar1=-1.0, scalar2=None, op0=ALU.add))
                    dv(lambda h: h.tensor_tensor(out=tmpa[:], in0=am1[:], in1=are[:], op=ALU.mult))
                    dv(lambda h: h.tensor_tensor(out=tmpb[:], in0=Aim[:], in1=aim[:], op=ALU.mult))
                    dv(lambda h: h.tensor_tensor(out=tmpa[:], in0=tmpa[:], in1=tmpb[:], op=ALU.add))
                    dv(lambda h: h.tensor_tensor(out=cr[:], in0=tmpa[:], in1=den[:], op=ALU.mult))
                    dv(lambda h: h.tensor_tensor(out=tmpa[:], in0=Aim[:], in1=are[:], op=ALU.mult))
                    dv(lambda h: h.tensor_tensor(out=tmpb[:], in0=am1[:], in1=aim[:], op=ALU.mult))
                    dv(lambda h: h.tensor_tensor(out=tmpa[:], in0=tmpa[:], in1=tmpb[:], op=ALU.subtract))
                    dv(lambda h: h.tensor_tensor(out=ci[:], in0=tmpa[:], in1=den[:], op=ALU.mult))
                    bbr = sbt(pp_, "bbr", [P, 16, 16], F32)
                    bbi = sbt(pp_, "bbi", [P, 16, 16], F32)
                    tb = sbt(pp_, "tb", [P, 16, 16], F32)
                    crb = cr[:].unsqueeze(2).broadcast_to([P, 16, 16])
                    cib = ci[:].unsqueeze(2).broadcast_to([P, 16, 16])
                    dv(lambda h: h.tensor_tensor(out=bbr[:], in0=bre[:], in1=crb, op=ALU.mult), [RL[3]])
                    dv(lambda h: h.tensor_tensor(out=tb[:], in0=bim[:], in1=cib, op=ALU.mult), [RL[4]])
                    dv(lambda h: h.tensor_tensor(out=bbr[:], in0=bbr[:], in1=tb[:], op=ALU.subtract))
                    dv(lambda h: h.tensor_tensor(out=bbi[:], in0=bim[:], in1=crb, op=ALU.mult))
                    dv(lambda h: h.tensor_tensor(out=tb[:], in0=bre[:], in1=cib, op=ALU.mult))
                    dv(lambda h: h.tensor_tensor(out=bbi[:], in0=bbi[:], in1=tb[:], op=ALU.add))
                    Zr = sbt(pp_, "Zr", [P, 4, 4, 4, 32], F32)
                    Zi = sbt(pp_, "Zi", [P, 4, 4, 4, 32], F32)
                    R_Z = Res("Z")
                    for z in (Zr, Zi):
                        fw.op("pool", lambda h, z=z: h.memset(z[:], 0.0), writes=[R_Z])
                    for z in (CTr, CTi, CTn):
                        fw.op("pool", lambda h, z=z: h.memset(z[:], 0.0), writes=[R_CT])
                    for q in range(4):
                        for gi in range(2):
                            p0 = gi * 64

                            def v4(t, p0=p0, q=q):
                                return t[p0:p0 + 64, :, :].rearrange("p (cb q) c -> p cb q c", q=4)[:, :, q, :]
                            fw.op("dve", lambda h: h.tensor_copy(out=Zr[p0:p0 + 64, :, q, q, gi * 16:gi * 16 + 16], in_=v4(bbr)), reads=[R_pp], writes=[R_Z])
                            fw.op("dve", lambda h: h.tensor_copy(out=Zi[p0:p0 + 64, :, q, q, gi * 16:gi * 16 + 16], in_=v4(bbi)), reads=[R_pp], writes=[R_Z])
                            fw.op("dve", lambda h: h.tensor_copy(out=CTr[p0:p0 + 64, :, q, q, gi * 16:gi * 16 + 16], in_=v4(cre)), reads=[RL[5]], writes=[R_CT])
                            fw.op("dve", lambda h: h.tensor_scalar(out=CTi[p0:p0 + 64, :, q, q, gi * 16:gi * 16 + 16], in0=v4(cim), scalar1=-1.0, scalar2=None, op0=ALU.mult), reads=[RL[6]], writes=[R_CT])
                            fw.op("dve", lambda h: h.tensor_scalar(out=CTn[p0:p0 + 64, :, q, q, gi * 16:gi * 16 + 16], in0=v4(cre), scalar1=-1.0, scalar2=None, op0=ALU.mult), reads=[RL[5]], writes=[R_CT])
                    for ri, (Z, BT) in enumerate(((Zr, BTr), (Zi, BTi))):
                        for g4 in range(4):
                            bank = 6 + (g4 % 2)
                            for k in range(4):
                                pair = g4 * 4 + k
                                cbi, qi = pair // 4, pair % 4
                                src = Z[:, cbi, qi, :, :].rearrange("p a b -> p (a b)")
                                fw.op("pe", lambda h, src=src, k=k, bank=bank: h.transpose(pb[bank][:, k * P:(k + 1) * P], src, ident_f[:]), reads=[R_Z, R_const], writes=[R_pb[bank]])
                            fw.op("act", lambda h, BT=BT, g4=g4, bank=bank: h.copy(out=BT[:, g4 * 4:g4 * 4 + 4, :], in_=pb[bank][:, :].rearrange("p (k n) -> p k n", k=4)), reads=[R_pb[bank]], writes=[R_BT])
                    fw.barrier()
                with ExitStack() as pw:
                    NR = 2
                    NU = 3
                    tcos = [sbt(pw, f"tcos{i}", [P, ST], F32) for i in range(4)]
                    tsin = [sbt(pw, f"tsin{i}", [P, ST], F32) for i in range(4)]
                    R_tab = [Res(f"tab{i}") for i in range(4)]
                    tA = sbt(pw, "tabA", [P, ST], F32)
                    tB = sbt(pw, "tabB", [P, ST], F32)
                    R_tt = Res("tabtmp")
                    utf = [sbt(pw, f"utf{i}", [P, ST], F32) for i in range(NU)]
                    utb = [sbt(pw, f"utb{i}", [P, ST], BF16) for i in range(NU)]
                    R_utf = [Res(f"utf{i}") for i in range(NU)]
                    R_utb = [Res(f"utb{i}") for i in range(NU)]
                    mm = [[sbt(pw, f"m{k}_{i}", [P, ST], F32) for k in range(4)] for i in range(NR)]
                    R_mm = [Res(f"mm{i}") for i in range(NR)]
                    bp = [[sbt(pw, f"bp{k}_{i}", [P, ST], F32) for k in range(2)] for i in range(NR)]
                    R_bp = [Res(f"bp{i}") for i in range(NR)]
                    yy = [[sbt(pw, f"yy{k}_{i}", [P, ST], F32) for k in range(2)] for i in range(NR)]
                    R_yy = [Res(f"yy{i}") for i in range(NR)]
                    xA = sbt(pw, "xA", [P, 4, ST], BF16)
                    xB = sbt(pw, "xB", [P, 4, ST], BF16)
                    xC = sbt(pw, "xC", [P, 4, ST], BF16)
                    xD = sbt(pw, "xD", [P, 4, ST], BF16)
                    R_xx = [Res(f"xx_{q}") for q in range(4)]
                    yvb = [sbt(pw, f"yvb{i}", [P, ST], BF16) for i in range(2)]
                    R_yvb = [Res(f"yvb{i}") for i in range(2)]
                    BU, BB = 5, 7
                    sbB = sb_alloc(pw)

                    def ssm_gen():
                        ucount = [0]
                        for cb in range(4):
                            for q in range(4):
                                pair = cb * 4 + q
                                wcol = wturn[:, pair:pair + 1]
                                fw.op("dve", lambda h: h.tensor_scalar(out=tA[:], in0=iota_t[:, 0:ST], scalar1=wcol, scalar2=MAGIC, op0=ALU.mult, op1=ALU.add), reads=[R_const, R_pp], writes=[R_tt])
                                fw.op("dve", lambda h: h.tensor_scalar(out=tB[:], in0=tA[:], scalar1=MAGIC, scalar2=None, op0=ALU.subtract), reads=[R_tt], writes=[R_tt])
                                fw.op("dve", lambda h: h.scalar_tensor_tensor(out=tA[:], in0=iota_t[:, 0:ST], scalar=wcol, in1=tB[:], op0=ALU.mult, op1=ALU.subtract), reads=[R_const, R_pp, R_tt], writes=[R_tt])
                                fw.op("act", lambda h: h.activation(out=tsin[q][:], in_=tA[:], func=AF.Sin, scale=TWO_PI), reads=[R_tt], writes=[R_tab[q]])
                                fw.op("dve", lambda h: h.scalar_tensor_tensor(out=tB[:], in0=tA[:], scalar=-1.0, in1=tA[:], op0=ALU.mult, op1=ALU.max), reads=[R_tt], writes=[R_tt])
                                fw.op("act", lambda h: h.activation(out=tcos[q][:], in_=tB[:], func=AF.Sin, scale=-TWO_PI, bias=halfpi[:]), reads=[R_tt, R_const], writes=[R_tab[q]])
                            units = []
                            for seg in range(NSEG):
                                for q in range(4):
                                    units.append(dict(q=q, pair=cb * 4 + q, n=ucount[0], seg=seg, sg=cb * NSEG + seg))
                                    ucount[0] += 1

                            def sA(u, cb=cb):
                                if u["q"] != 0:
                                    return
                                si = u["sg"] % NU
                                t0 = u["seg"] * ST
                                jt = t0 // TT
                                for c in range(DC):
                                    fw.op("pe", lambda h: h.matmul(pb[BU][:, 0:ST], lhsT=w_u[cb][:, c, :], rhs=hT[:, c, t0:t0 + ST], start=(c == 0), stop=(c == DC - 1)),
                                          reads=[R_wu[cb], R_hT[jt]], writes=[R_pb[BU]])
                                fw.op("dve", lambda h: h.tensor_copy(out=utf[si][:], in_=pb[BU][:, 0:ST]), reads=[R_pb[BU]], writes=[R_utf[si]])
                                fw.op("pool", lambda h: h.tensor_copy(out=utb[si][:], in_=utf[si][:]), reads=[R_utf[si]], writes=[R_utb[si]])

                            def st0(u):
                                pair = u["pair"]
                                si = u["sg"] % NU
                                fw.op("pe", lambda h: h.matmul(pb[BB][:, 0:ST], lhsT=BTr[:, pair, :], rhs=utb[si][:], start=True, stop=True), reads=[R_BT, R_utb[si]], writes=[R_pb[BB]])
                                fw.op("pe", lambda h: h.matmul(pb[BB][:, ST:2 * ST], lhsT=BTi[:, pair, :], rhs=utb[si][:], start=True, stop=True), reads=[R_BT, R_utb[si]], writes=[R_pb[BB]])

                            def st1(u):
                                i = u["n"] % NR
                                q = u["q"]
                                m = mm[i]
                                pR = pb[BB][:, 0:ST]
                                pI = pb[BB][:, ST:2 * ST]
                                fw.op("dve", lambda h: h.tensor_tensor(out=m[0][:], in0=pR, in1=tcos[q][:], op=ALU.mult), reads=[R_pb[BB], R_tab[q]], writes=[R_mm[i]])
                                fw.op("dve", lambda h: h.tensor_tensor(out=m[3][:], in0=pR, in1=tsin[q][:], op=ALU.mult), reads=[R_pb[BB], R_tab[q]], writes=[R_mm[i]])
                                fw.op("dve", lambda h: h.tensor_tensor(out=m[1][:], in0=pI, in1=tsin[q][:], op=ALU.mult), reads=[R_pb[BB], R_tab[q]], writes=[R_mm[i]])
                                fw.op("dve", lambda h: h.tensor_tensor(out=m[2][:], in0=pI, in1=tcos[q][:], op=ALU.mult), reads=[R_pb[BB], R_tab[q]], writes=[R_mm[i]])

                            def st2(u):
                                i = u["n"] % NR
                                m = mm[i]
                                fw.op("pool", lambda h: h.tensor_tensor(out=bp[i][0][:], in0=m[0][:], in1=m[1][:], op=ALU.add), reads=[R_mm[i]], writes=[R_bp[i]])
                                fw.op("pool", lambda h: h.tensor_tensor(out=bp[i][1][:], in0=m[2][:], in1=m[3][:], op=ALU.subtract), reads=[R_mm[i]], writes=[R_bp[i]])

                            def st3(u):
                                i = u["n"] % NR
                                pair = u["pair"]
                                rb = rho[:, pair:pair + 1].broadcast_to([P, ST])
                                Rc = R_carry[pair]
                                fw.op("dve", lambda h: h.tensor_tensor_scan(out=yy[i][0][:], data0=rb, data1=bp[i][0][:], initial=carry[:, pair, 0:1], op0=ALU.mult, op1=ALU.add), reads=[R_bp[i], R_pp, Rc], writes=[R_yy[i]])
                                fw.op("dve", lambda h: h.tensor_tensor_scan(out=yy[i][1][:], data0=rb, data1=bp[i][1][:], initial=carry[:, pair, 1:2], op0=ALU.mult, op1=ALU.add), reads=[R_bp[i], R_pp, Rc], writes=[R_yy[i]])
                                yl_r = yy[i][0][:, ST - 1:ST]
                                yl_i = yy[i][1][:, ST - 1:ST]
                                fw.op("dve", lambda h: h.tensor_tensor(out=ctmp[:, 0:1], in0=yl_i, in1=s512[:, pair:pair + 1], op=ALU.mult), reads=[R_yy[i], R_pp], writes=[R_ctmp])
                                fw.op("dve", lambda h: h.tensor_tensor(out=ctmp[:, 1:2], in0=yl_r, in1=s512[:, pair:pair + 1], op=ALU.mult), reads=[R_yy[i], R_pp], writes=[R_ctmp])
                                fw.op("dve", lambda h: h.scalar_tensor_tensor(out=carry[:, pair, 0:1], in0=yl_r, scalar=c512[:, pair:pair + 1], in1=ctmp[:, 0:1], op0=ALU.mult, op1=ALU.subtract), reads=[R_yy[i], R_pp, R_ctmp], writes=[Rc])
                                fw.op("dve", lambda h: h.scalar_tensor_tensor(out=carry[:, pair, 1:2], in0=yl_i, scalar=c512[:, pair:pair + 1], in1=ctmp[:, 1:2], op0=ALU.mult, op1=ALU.add), reads=[R_yy[i], R_pp, R_ctmp], writes=[Rc])

                            def st4(u):
                                i = u["n"] % NR
                                q = u["q"]
                                yr, yi = yy[i][0], yy[i][1]
                                fw.op("pool", lambda h: h.tensor_tensor(out=xA[:, q, :], in0=yr[:], in1=tcos[q][:], op=ALU.mult), reads=[R_yy[i], R_tab[q]], writes=[R_xx[q]])
                                fw.op("pool", lambda h: h.tensor_tensor(out=xB[:, q, :], in0=yi[:], in1=tsin[q][:], op=ALU.mult), reads=[R_yy[i], R_tab[q]], writes=[R_xx[q]])
                                fw.op("pool", lambda h: h.tensor_tensor(out=xC[:, q, :], in0=yi[:], in1=tcos[q][:], op=ALU.mult), reads=[R_yy[i], R_tab[q]], writes=[R_xx[q]])
                                fw.op("pool", lambda h: h.tensor_tensor(out=xD[:, q, :], in0=yr[:], in1=tsin[q][:], op=ALU.mult), reads=[R_yy[i], R_tab[q]], writes=[R_xx[q]])

                            def sG(u, cb=cb):
                                if u["q"] != 3:
                                    return
                                seg = u["seg"]
                                si = u["sg"] % NU
                                yi_ = u["sg"] % 2
                                t0 = seg * ST
                                py = pb[BU][:, ST:2 * ST]
                                for q in range(4):
                                    lr_ = CTr[:, cb, q, :, :].rearrange("p a b -> p (a b)")
                                    ln_ = CTn[:, cb, q, :, :].rearrange("p a b -> p (a b)")
                                    li_ = CTi[:, cb, q, :, :].rearrange("p a b -> p (a b)")
                                    fw.op("pe", lambda h: h.matmul(py, lhsT=lr_, rhs=xA[:, q, :], start=(q == 0), stop=False), reads=[R_CT, R_xx[q]], writes=[R_pb[BU]])
                                    fw.op("pe", lambda h: h.matmul(py, lhsT=ln_, rhs=xB[:, q, :], start=False, stop=False), reads=[R_CT, R_xx[q]], writes=[R_pb[BU]])
                                    fw.op("pe", lambda h: h.matmul(py, lhsT=li_, rhs=xC[:, q, :], start=False, stop=False), reads=[R_CT, R_xx[q]], writes=[R_pb[BU]])
                                    fw.op("pe", lambda h: h.matmul(py, lhsT=li_, rhs=xD[:, q, :], start=False, stop=(q == 3)), reads=[R_CT, R_xx[q]], writes=[R_pb[BU]])
                                fw.op("dve", lambda h: h.scalar_tensor_tensor(out=yvb[yi_][:], in0=utf[si][:], scalar=dsk[:, cb:cb + 1], in1=py, op0=ALU.mult, op1=ALU.add),
                                      reads=[R_utf[si], R_pb[BU], RL[7]], writes=[R_yvb[yi_]])
                                fw.dma(yg_d[:, cb, t0:t0 + ST], yvb[yi_][:], reads=[R_yvb[yi_]], writes=[R_yg[t0 // TT]], semres=R_yvb[yi_])

                            yield from pipeline_steps(units, [sA, st0, st1, st2, st3, st4, sG])

                    drive(sb_gen(l, sbB), ssm_gen(), "aab")
                    fw.barrier()

            with ExitStack() as ps:
                w_sg = [sbt(ps, f"w_sg{i}", [P, DC, P], BF16) for i in range(4)]
                R_wsg = [Res(f"w_sg{i}") for i in range(4)]
                for cb in range(4):
                    load_wblock(l, 4 + cb, w_sg[cb][:], R_wsg[cb])
                wglu = sbt(ps, "wglu", [P, 4, 512], BF16)
                R_wglu = Res("wglu")
                for ob in range(4):
                    load_generic(w_glu_d[l, :, :, ob * P:(ob + 1) * P], wglu[:, :, ob * P:(ob + 1) * P], R_wglu, (4, P))
                bglu = sbt(ps, "bglu", [P, 4], F32)
                R_bglu = Res("bglu")
                fw.dma(bglu[:], bglu_d[l], writes=[R_bglu])
                ygt = [sbt(ps, f"ygt{i}", [P, 4, TT], BF16) for i in range(2)]
                R_ygt = [Res(f"ygt{i}") for i in range(2)]
                s1 = [sbt(ps, f"s1_{i}", [P, TT], F32) for i in range(2)]
                s2 = [sbt(ps, f"s2_{i}", [P, TT], F32) for i in range(2)]
                R_s = [Res(f"s12_{i}") for i in range(2)]
                osb = [sbt(ps, f"osb{i}", [P, 4, TT], BF16) for i in range(2)]
                R_osb = [Res(f"osb{i}") for i in range(2)]
                gA = sbt(ps, "glu_gA", [P, 4, TT], F32)
                gB = sbt(ps, "glu_gB", [P, 4, TT], F32)
                R_gg = Res("glu_gg")
                cnt = 0
                for j in range(NT):
                    yi_ = j % 2
                    fw.dma(ygt[yi_][:], yg_d[:, :, j * TT:(j + 1) * TT], reads=[R_yg[j]], writes=[R_ygt[yi_]])
                    fw.op("act", lambda h: h.activation(out=gA[:], in_=ygt[yi_][:], func=AF.Square), reads=[R_ygt[yi_]], writes=[R_gg])
                    fw.op("dve", lambda h: h.tensor_scalar(out=gA[:], in0=gA[:], scalar1=0.044715, scalar2=1.0, op0=ALU.mult, op1=ALU.add), reads=[R_gg], writes=[R_gg])
                    fw.op("pool", lambda h: h.tensor_tensor(out=gA[:], in0=gA[:], in1=ygt[yi_][:], op=ALU.mult), reads=[R_gg, R_ygt[yi_]], writes=[R_gg])
                    fw.op("act", lambda h: h.activation(out=gB[:], in_=gA[:], func=AF.Sigmoid, scale=1.5957691216057308), reads=[R_gg], writes=[R_gg])
                    fw.op("pool", lambda h: h.tensor_tensor(out=ygt[yi_][:], in0=gB[:], in1=ygt[yi_][:], op=ALU.mult), reads=[R_gg, R_ygt[yi_]], writes=[R_ygt[yi_]])
                    for ob in range(4):
                        k = cnt % 2
                        cnt += 1
                        bz, bg = 6 + k, 0 + k
                        for c in range(4):
                            fw.op("pe", lambda h, c=c, ob=ob, bz=bz: h.matmul(pb[bz][:, :], lhsT=wglu[:, c, ob * P:(ob + 1) * P], rhs=ygt[yi_][:, c, :], start=(c == 0), stop=(c == 3)),
                                  reads=[R_wglu, R_ygt[yi_]], writes=[R_pb[bz]])
                        proj_fm(bg, w_sg[ob], R_wsg[ob], j)
                        fw.op("act", lambda h, k=k, ob=ob, bz=bz: h.activation(out=s1[k][:], in_=pb[bz][:, :], func=AF.Sigmoid, bias=bglu[:, ob:ob + 1]), reads=[R_pb[bz], R_bglu], writes=[R_s[k]])
                        fw.op("act", lambda h, k=k, bg=bg: h.activation(out=s2[k][:], in_=pb[bg][:, :], func=AF.Silu), reads=[R_pb[bg]], writes=[R_s[k]])
                        fw.op("dve", lambda h, k=k, ob=ob: h.tensor_tensor(out=s1[k][:], in0=s1[k][:], in1=ygt[yi_][:, ob, :], op=ALU.mult), reads=[R_s[k], R_ygt[yi_]], writes=[R_s[k]])
                        fw.op("dve", lambda h, k=k, ob=ob: h.tensor_tensor(out=osb[yi_][:, ob, :], in0=s1[k][:], in1=s2[k][:], op=ALU.mult), reads=[R_s[k]], writes=[R_osb[yi_]])
                    fw.dma(obr_d[0, :, :, j * TT:(j + 1) * TT], osb[yi_][:], reads=[R_osb[yi_]], writes=[R_obr[0][j]], semres=R_osb[yi_])
                    if "dbg_o0" in dbg_out:
                        pass
                fw.barrier()

        def sb_alloc(ps):
            if True:
                wq = sbt(ps, "sb_wq", [P, DC, P], BF16)
                wk = sbt(ps, "sb_wk", [P, DC, P], BF16)
                wv = sbt(ps, "sb_wv", [P, DC, P], BF16)
                wg = sbt(ps, "sb_wg", [P, DC, P], BF16)
                R_wq, R_wk, R_wv, R_wg = Res("sb_wq"), Res("sb_wk"), Res("sb_wv"), Res("sb_wg")
                qT = sbt(ps, "sb_qT", [P, S], BF16)
                kT = sbt(ps, "sb_kT", [P, S], BF16)
                V = sbt(ps, "sb_V", [P, NB, P], BF16)
                R_q = [Res(f"sb_q{j}") for j in range(NT)]
                R_k = [Res(f"sb_k{j}") for j in range(NT)]
                R_v = [Res(f"sb_v{j}") for j in range(NT)]
                NR = 3
                e1 = [sbt(ps, f"sb_e1_{e}", [P, TT], F32) for e in range(2)]
                sp = [[sbt(ps, f"sb_sp_{i}_{e}", [P, TT], BF16) for e in range(2)] for i in range(NR)]
                ww = [[sbt(ps, f"sb_w_{i}_{e}", [P, TT], BF16) for e in range(2)] for i in range(NR)]
                R_e1 = [Res(f"sb_e1_{e}") for e in range(2)]
                R_sp = [[Res(f"sb_sp_{i}_{e}") for e in range(2)] for i in range(NR)]
                R_ww = [[Res(f"sb_w_{i}_{e}") for e in range(2)] for i in range(NR)]
                acc = [[sbt(ps, f"sb_acc{i}_{e}", [P, TT], BF16) for e in range(2)] for i in range(2)]
                R_acc = [[Res(f"sb_acc{i}_{e}") for e in range(2)] for i in range(2)]
                sg = [sbt(ps, f"sb_sg{i}", [P, TT], F32) for i in range(2)]
                R_sg = [Res(f"sb_sg{i}") for i in range(2)]
                og = [sbt(ps, f"sb_og{i}", [P, TT], BF16) for i in range(2)]
                R_og = [Res(f"sb_og{i}") for i in range(2)]
                return dict(locals())

        def sb_gen(l, B_):
            (wq, wk, wv, wg, R_wq, R_wk, R_wv, R_wg, qT, kT, V, R_q, R_k, R_v, NR, e1, sp, ww, R_e1, R_sp, R_ww, acc, R_acc, sg, R_sg, og, R_og) = (
                B_[k_] for k_ in ("wq", "wk", "wv", "wg", "R_wq", "R_wk", "R_wv", "R_wg", "qT", "kT", "V", "R_q", "R_k", "R_v", "NR", "e1", "sp", "ww",
                                  "R_e1", "R_sp", "R_ww", "acc", "R_acc", "sg", "R_sg", "og", "R_og"))
            if True:
                tcount = [0]
                gcount = [0]
                for hp in range(4):
                    load_wblock(l, 8 + hp, wq[:], R_wq)
                    load_wblock(l, 12 + hp, wk[:], R_wk)
                    load_wblock(l, 16 + hp, wv[:], R_wv)
                    load_wblock(l, 20 + hp, wg[:], R_wg)
                    for j in range(NT):
                        proj_fm(6, wq, R_wq, j)
                        fw.op("act", lambda h, j=j: h.activation(out=qT[:, j * TT:(j + 1) * TT], in_=pb[6][:, :], func=AF.Copy, scale=0.125), reads=[R_pb[6]], writes=[R_q[j]])
                        proj_fm(4, wk, R_wk, j)
                        fw.op("dve", lambda h, j=j: h.tensor_copy(out=kT[:, j * TT:(j + 1) * TT], in_=pb[4][:, :]), reads=[R_pb[4]], writes=[R_k[j]])
                    for j in range(NT):
                        bank = 6 if j % 2 == 0 else 4
                        for b4 in range(4):
                            blk = j * 4 + b4
                            for c in range(DC):
                                fw.op("pe", lambda h, c=c, blk=blk, b4=b4, bank=bank: h.matmul(pb[bank][:, b4 * P:(b4 + 1) * P], lhsT=hT[:, c, blk * P:(blk + 1) * P], rhs=wv[:, c, :], start=(c == 0), stop=(c == DC - 1)),
                                      reads=[R_wv, R_hT[j]], writes=[R_pb[bank]])
                        src = pb[bank][:, :].rearrange("p (b n) -> p b n", b=4)
                        if j % 2 == 0:
                            fw.op("act", lambda h, j=j, src=src: h.copy(out=V[:, j * 4:(j + 1) * 4, :], in_=src), reads=[R_pb[bank]], writes=[R_v[j]])
                        else:
                            fw.op("dve", lambda h, j=j, src=src: h.tensor_copy(out=V[:, j * 4:(j + 1) * 4, :], in_=src), reads=[R_pb[bank]], writes=[R_v[j]])
                    units = []
                    for qt in range(NT):
                        ai = gcount[0] % 2
                        gcount[0] += 1
                        kbs = list(range(4 * qt + 3, -1, -1))
                        for idx, kb in enumerate(kbs):
                            jd = kb - 4 * qt
                            units.append(dict(qt=qt, kb=kb, col0=(P * jd if jd >= 0 else 0), diag=(jd >= 0), first=(idx == 0), last=(kb == 0),
                                              ai=ai, n=tcount[0], ob=4))
                            tcount[0] += 1

                    def s0(u):
                        kb, qt, c0 = u["kb"], u["qt"], u["col0"]
                        t0 = qt * TT
                        if u["first"]:
                            for e in range(2):
                                fw.op("pool", lambda h: h.memset(acc[u["ai"]][e][:], 0.0), writes=[R_acc[u["ai"]][e]])
                        for e in range(2):
                            pr = slice(64 * e, 64 * e + 64)
                            fw.op("pe", lambda h: h.matmul(pb[e][:, c0:TT], lhsT=kT[pr, kb * P:(kb + 1) * P], rhs=qT[pr, t0 + c0:t0 + TT], start=True, stop=not u["diag"]),
                                  reads=[R_k[kb // 4], R_q[qt]], writes=[R_pb[e]])
                        if u["diag"]:
                            for e in range(2):
                                fw.op("pe", lambda h: h.matmul(pb[e][:, c0:c0 + P], lhsT=ident_b[:], rhs=mask_sb[:], start=False, stop=True), reads=[R_const], writes=[R_pb[e]])

                    def s1(u):
                        i = u["n"] % NR
                        c0 = u["col0"]
                        for e in range(2):
                            fw.op("act", lambda h: h.activation(out=e1[e][:, c0:TT], in_=pb[e][:, c0:TT], func=AF.Exp), reads=[R_pb[e]], writes=[R_e1[e]])
                        for e in range(2):
                            fw.op("act", lambda h: h.activation(out=sp[i][e][:, c0:TT], in_=e1[e][:, c0:TT], func=AF.Ln, bias=1.0), reads=[R_e1[e]], writes=[R_sp[i][e]])

                    def s2(u):
                        i = u["n"] % NR
                        kb, qt, c0 = u["kb"], u["qt"], u["col0"]
                        t0 = qt * TT
                        for e in range(2):
                            pr = slice(64 * e, 64 * e + 64)
                            fw.op("pe", lambda h: h.matmul(pb[2 + e][:, c0:TT], lhsT=kT[pr, kb * P:(kb + 1) * P], rhs=qT[pr, t0 + c0:t0 + TT], start=True, stop=False),
                                  reads=[R_k[kb // 4], R_q[qt]], writes=[R_pb[2 + e]])
                        lastmm = u["first"] and not u["diag"]
                        for e in range(2):
                            fw.op("pe", lambda h: h.matmul(pb[2 + e][:, c0:TT], lhsT=ntri_b[:], rhs=sp[i][e][:, c0:TT], start=False, stop=lastmm), reads=[R_const, R_sp[i][e]], writes=[R_pb[2 + e]])
                        if not u["first"]:
                            for e in range(2):
                                fw.op("pe", lambda h: h.matmul(pb[2 + e][:, c0:TT], lhsT=nones_b[:], rhs=acc[u["ai"]][e][:, c0:TT], start=False, stop=not u["diag"]),
                                      reads=[R_const, R_acc[u["ai"]][e]], writes=[R_pb[2 + e]])
                        if u["diag"]:
                            for e in range(2):
                                fw.op("pe", lambda h: h.matmul(pb[2 + e][:, c0:c0 + P], lhsT=ident_b[:], rhs=mask_sb[:], start=False, stop=True), reads=[R_const], writes=[R_pb[2 + e]])

                    def s3(u):
                        i = u["n"] % NR
                        c0 = u["col0"]
                        for e in range(2):
                            fw.op("act", lambda h: h.activation(out=ww[i][e][:, c0:TT], in_=pb[2 + e][:, c0:TT], func=AF.Exp), reads=[R_pb[2 + e]], writes=[R_ww[i][e]])
                        if not u["last"]:
                            for e in range(2):
                                a_ = acc[u["ai"]][e]
                                fw.op("dve", lambda h: h.tensor_tensor(out=a_[:, c0:TT], in0=a_[:, c0:TT], in1=sp[i][e][:, c0:TT], op=ALU.add), reads=[R_sp[i][e], R_acc[u["ai"]][e]], writes=[R_acc[u["ai"]][e]])

                    def s4(u):
                        i = u["n"] % NR
                        kb, qt, c0, ob = u["kb"], u["qt"], u["col0"], u["ob"]
                        for e in range(2):
                            pr = slice(64 * e, 64 * e + 64)
                            fw.op("pe", lambda h: h.matmul(pb[ob][pr, c0:TT], lhsT=V[:, kb, 64 * e:64 * e + 64], rhs=ww[i][e][:, c0:TT], start=u["first"], stop=u["last"]),
                                  reads=[R_v[kb // 4], R_ww[i][e]], writes=[R_pb[ob]])
                        if u["last"]:
                            k = qt % 2
                            proj_fm(6, wg, R_wg, qt)
                            fw.op("act", lambda h: h.activation(out=sg[k][:], in_=pb[6][:, :], func=AF.Silu), reads=[R_pb[6]], writes=[R_sg[k]])
                            fw.op("dve", lambda h: h.tensor_tensor(out=og[k][:], in0=pb[ob][:, :], in1=sg[k][:], op=ALU.mult), reads=[R_pb[ob], R_sg[k]], writes=[R_og[k]])
                            fw.dma(obr_d[1, :, hp, qt * TT:(qt + 1) * TT], og[k][:], reads=[R_og[k]], writes=[R_obr[1][qt]], semres=R_og[k])

                    yield from pipeline_steps(units, [s0, s1, s2, s3, s4])

        def diff_phase(l):
            lam_init = 0.8 - 0.6 * math.exp(-0.3 * l)
            with ExitStack() as ps:
                names = ["wq", "wqr", "wk", "wkr", "wv", "wg"]
                wt = {n: sbt(ps, "df_" + n, [P, DC, P], BF16) for n in names}
                R_w = {n: Res("df_" + n) for n in names}
                qT = sbt(ps, "df_qT", [P, S], BF16)
                kT = sbt(ps, "df_kT", [P, S], BF16)
                V = sbt(ps, "df_V", [P, NB, P], BF16)
                R_q = [Res(f"df_q{j}") for j in range(NT)]
                R_k = [Res(f"df_k{j}") for j in range(NT)]
                R_v = [Res(f"df_v{j}") for j in range(NT)]
                rc = [sbt(ps, f"df_rc{i}", [P, TT], F32) for i in range(2)]
                rs = [sbt(ps, f"df_rs{i}", [P, TT], F32) for i in range(2)]
                R_rc = [Res(f"df_rc{i}") for i in range(2)]
                R_rs = [Res(f"df_rs{i}") for i in range(2)]
                ta = [sbt(ps, f"df_ta{i}", [P, TT], F32) for i in range(2)]
                tb = [sbt(ps, f"df_tb{i}", [P, TT], F32) for i in range(2)]
                R_ta = [Res(f"df_ta{i}") for i in range(2)]
                NR = 3
                pp = [[sbt(ps, f"df_p{i}_{k}", [P, TT], BF16) for k in range(2)] for i in range(NR)]
                R_pp_ = [[Res(f"df_p{i}_{k}") for k in range(2)] for i in range(NR)]
                rr = sbt(ps, "df_rr", [P, TT], F32)
                R_rr = Res("df_rr")
                rb = [sbt(ps, f"df_rb{i}", [P, TT], F32) for i in range(2)]
                R_rb = [Res(f"df_rb{i}") for i in range(2)]
                bsel = sbt(ps, "df_bsel", [64, P], F32)
                R_bsel = Res("df_bsel")
                fw.op("pool", lambda h: h.memset(bsel[:], 1.0 / 32), writes=[R_bsel])
                o12 = [[sbt(ps, f"df_o{i}_{k}", [P, TT], F32) for i in range(2)] for k in range(2)]
                R_o12 = [[Res(f"df_o{i}_{k}") for i in range(2)] for k in range(2)]
                oo = sbt(ps, "df_oo", [P, TT], F32)
                osq = sbt(ps, "df_osq", [P, TT], F32)
                rstd = sbt(ps, "df_rstd", [P, TT], F32)
                sg = sbt(ps, "df_sg", [P, TT], F32)
                R_fin = Res("df_fin")
                R_sgd = Res("df_sg")
                ofin = [sbt(ps, f"df_ofin{i}", [P, TT], BF16) for i in range(2)]
                R_ofin = [Res(f"df_ofin{i}") for i in range(2)]
                lqk = sbt(ps, "df_lqk", [P, 4, 64], F32)
                R_lqk = Res("df_lqk")
                fw.dma(lqk[:], lqk_d[l], writes=[R_lqk])
                sub = sbt(ps, "df_sub", [P, 1], F32)
                R_sub = Res("df_sub")
                fw.dma(sub[:], subln_d[l], writes=[R_sub])
                pr1 = sbt(ps, "df_pr1", [P, 64], F32)
                sm = sbt(ps, "df_sm", [P, 4], F32)
                R_lam = Res("df_lam")
                fw.op("dve", lambda h: h.tensor_tensor(out=pr1[:], in0=lqk[:, 0, :], in1=lqk[:, 1, :], op=ALU.mult), reads=[R_lqk], writes=[R_lam])
                fw.op("dve", lambda h: h.reduce_sum(out=sm[:, 0:1], in_=pr1[:], axis=AX.X), reads=[R_lam], writes=[R_lam])
                fw.op("dve", lambda h: h.tensor_tensor(out=pr1[:], in0=lqk[:, 2, :], in1=lqk[:, 3, :], op=ALU.mult), reads=[R_lqk, R_lam], writes=[R_lam])
                fw.op("dve", lambda h: h.reduce_sum(out=sm[:, 1:2], in_=pr1[:], axis=AX.X), reads=[R_lam], writes=[R_lam])
                fw.op("act", lambda h: h.activation(out=sm[:, 0:2], in_=sm[:, 0:2], func=AF.Exp), reads=[R_lam], writes=[R_lam])
                fw.op("dve", lambda h: h.tensor_tensor(out=sm[:, 2:3], in0=sm[:, 1:2], in1=sm[:, 0:1], op=ALU.subtract), reads=[R_lam], writes=[R_lam])
                fw.op("dve", lambda h: h.tensor_scalar(out=sm[:, 2:3], in0=sm[:, 2:3], scalar1=-lam_init, scalar2=None, op0=ALU.add), reads=[R_lam], writes=[R_lam])
                fw.op("dve", lambda h: h.tensor_scalar(out=sub[:], in0=sub[:], scalar1=(1.0 - lam_init), scalar2=None, op0=ALU.mult), reads=[R_sub], writes=[R_sub])
                nlam = sm[:, 2:3]
                tcount = [0]
                gcount = [0]
                for hd in range(4):
                    load_wblock(l, 24 + hd, wt["wq"][:], R_w["wq"])
                    load_wblock(l, 24 + hd, wt["wqr"][:], R_w["wqr"], rot=True)
                    load_wblock(l, 28 + hd, wt["wk"][:], R_w["wk"])
                    load_wblock(l, 28 + hd, wt["wkr"][:], R_w["wkr"], rot=True)
                    load_wblock(l, 32 + hd, wt["wv"][:], R_w["wv"])
                    load_wblock(l, 36 + hd, wt["wg"][:], R_w["wg"])
                    for j in range(NT):
                        i = j % 2
                        fw.dma(rc[i][:], ropeC_d[:, j * TT:(j + 1) * TT], reads=[R_ropeC[j]], writes=[R_rc[i]])
                        fw.dma(rs[i][:], ropeS_d[:, j * TT:(j + 1) * TT], reads=[R_ropeS[j]], writes=[R_rs[i]])
                        for which, (wn, wr, dstT, R_dst, scl) in enumerate((("wq", "wqr", qT, R_q, 0.125), ("wk", "wkr", kT, R_k, 1.0))):
                            k = which
                            proj_fm(6, wt[wn], R_w[wn], j)
                            proj_fm(7, wt[wr], R_w[wr], j)
                            fw.op("dve", lambda h, k=k, scl=scl: h.scalar_tensor_tensor(out=ta[k][:], in0=pb[6][:, :], scalar=scl, in1=rc[i][:], op0=ALU.mult, op1=ALU.mult), reads=[R_pb[6], R_rc[i]], writes=[R_ta[k]])
                            fw.op("dve", lambda h, k=k, scl=scl: h.scalar_tensor_tensor(out=tb[k][:], in0=pb[7][:, :], scalar=scl, in1=rs[i][:], op0=ALU.mult, op1=ALU.mult), reads=[R_pb[7], R_rs[i]], writes=[R_ta[k]])
                            fw.op("pool", lambda h, k=k, dstT=dstT, j=j: h.tensor_tensor(out=dstT[:, j * TT:(j + 1) * TT], in0=ta[k][:], in1=tb[k][:], op=ALU.add), reads=[R_ta[k]], writes=[R_dst[j]])
                    for j in range(NT):
                        bank = 6 + (j % 2)
                        for b4 in range(4):
                            blk = j * 4 + b4
                            for c in range(DC):
                                fw.op("pe", lambda h, c=c, blk=blk, b4=b4, bank=bank: h.matmul(pb[bank][:, b4 * P:(b4 + 1) * P], lhsT=hT[:, c, blk * P:(blk + 1) * P], rhs=wt["wv"][:, c, :], start=(c == 0), stop=(c == DC - 1)),
                                      reads=[R_w["wv"], R_hT[j]], writes=[R_pb[bank]])
                        src = pb[bank][:, :].rearrange("p (b n) -> p b n", b=4)
                        if j % 2 == 0:
                            fw.op("act", lambda h, j=j, src=src: h.copy(out=V[:, j * 4:(j + 1) * 4, :], in_=src), reads=[R_pb[bank]], writes=[R_v[j]])
                        else:
                            fw.op("dve", lambda h, j=j, src=src: h.tensor_copy(out=V[:, j * 4:(j + 1) * 4, :], in_=src), reads=[R_pb[bank]], writes=[R_v[j]])
                    if "dbg_dfq" in dbg_out and hd == 0:
                        pass
                    units = []
                    for qt in range(NT):
                        nkb = 4 * qt + 4
                        for kb in range(nkb):
                            jd = kb - 4 * qt
                            units.append(dict(qt=qt, kb=kb, col0=(P * jd if jd >= 0 else 0), diag=(jd >= 0), first=(kb == 0), last=(kb == nkb - 1), n=tcount[0]))
                            tcount[0] += 1

                    def s0(u):
                        kb, qt, c0 = u["kb"], u["qt"], u["col0"]
                        t0 = qt * TT
                        for i in range(2):
                            pr = slice(64 * i, 64 * i + 64)
                            fw.op("pe", lambda h: h.matmul(pb[i][:, c0:TT], lhsT=kT[pr, kb * P:(kb + 1) * P], rhs=qT[pr, t0 + c0:t0 + TT], start=True, stop=not u["diag"]),
                                  reads=[R_k[kb // 4], R_q[qt]], writes=[R_pb[i]])
                        if u["diag"]:
                            for i in range(2):
                                fw.op("pe", lambda h: h.matmul(pb[i][:, c0:c0 + P], lhsT=ident_b[:], rhs=mask_df[:], start=False, stop=True), reads=[R_const], writes=[R_pb[i]])

                    def s1(u):
                        ii = u["n"] % NR
                        c0 = u["col0"]
                        for i in range(2):
                            fw.op("act", lambda h: h.activation(out=pp[ii][i][:, c0:TT], in_=pb[i][:, c0:TT], func=AF.Exp), reads=[R_pb[i]], writes=[R_pp_[ii][i]])

                    def s2(u):
                        ii = u["n"] % NR
                        kb, qt, c0 = u["kb"], u["qt"], u["col0"]
                        for i in range(2):
                            fw.op("pe", lambda h: h.matmul(pb[2 + i][:, c0:TT], lhsT=V[:, kb, :], rhs=pp[ii][i][:, c0:TT], start=u["first"], stop=u["last"]), reads=[R_v[kb // 4], R_pp_[ii][i]], writes=[R_pb[2 + i]])
                        for i in range(2):
                            fw.op("pe", lambda h: h.matmul(pb[4][32 * i:32 * i + 32, c0:TT], lhsT=ones_b[:, 0:32], rhs=pp[ii][i][:, c0:TT], start=u["first"], stop=u["last"]), reads=[R_const, R_pp_[ii][i]], writes=[R_pb[4]])
                        if u["last"]:
                            k2 = qt % 2
                            fw.op("dve", lambda h: h.reciprocal(out=rr[0:64, :], in_=pb[4][0:64, :]), reads=[R_pb[4]], writes=[R_rr])
                            for i in range(2):
                                fw.op("pe", lambda h: h.matmul(pb[5][:, :], lhsT=bsel[32 * i:32 * i + 32, :], rhs=rr[32 * i:32 * i + 32, :], start=True, stop=True), reads=[R_rr, R_bsel], writes=[R_pb[5]])
                                fw.op("act", lambda h: h.copy(out=rb[i][:], in_=pb[5][:, :]), reads=[R_pb[5]], writes=[R_rb[i]])
                                fw.op("dve", lambda h: h.tensor_tensor(out=o12[k2][i][:], in0=pb[2 + i][:, :], in1=rb[i][:], op=ALU.mult), reads=[R_pb[2 + i], R_rb[i]], writes=[R_o12[k2][i]])
                            o1, o2 = o12[k2][0], o12[k2][1]
                            fw.op("dve", lambda h: h.scalar_tensor_tensor(out=oo[:], in0=o2[:], scalar=nlam, in1=o1[:], op0=ALU.mult, op1=ALU.add), reads=[R_o12[k2][0], R_o12[k2][1], R_lam], writes=[R_fin])
                            fw.op("act", lambda h: h.activation(out=osq[:], in_=oo[:], func=AF.Square), reads=[R_fin], writes=[R_fin])
                            fw.op("pe", lambda h: h.matmul(pb[6][:, :], lhsT=ones_f[:], rhs=osq[:], start=True, stop=True), reads=[R_fin, R_const], writes=[R_pb[6]])
                            rsqrt_from_psum(6, 1.0 / P, rstd[:], R_fin, rstd[:])
                            proj_fm(7, wt["wg"], R_w["wg"], qt)
                            fw.op("act", lambda h: h.activation(out=sg[:], in_=pb[7][:, :], func=AF.Silu), reads=[R_pb[7]], writes=[R_sgd])
                            fw.op("dve", lambda h: h.tensor_tensor(out=oo[:], in0=oo[:], in1=rstd[:], op=ALU.mult), reads=[R_fin], writes=[R_fin])
                            fw.op("dve", lambda h: h.scalar_tensor_tensor(out=ofin[k2][:], in0=oo[:], scalar=sub[:, 0:1], in1=sg[:], op0=ALU.mult, op1=ALU.mult), reads=[R_fin, R_sub, R_sgd], writes=[R_ofin[k2]])
                            fw.dma(obr_d[2, :, hd, qt * TT:(qt + 1) * TT], ofin[k2][:], reads=[R_ofin[k2]], writes=[R_obr[2][qt]], semres=R_ofin[k2])

                    run_pipeline(units, [s0, s1, s2])
                fw.barrier()

        def merge_phase(l, last):
            with ExitStack() as ps:
                wm = sbt(ps, "mg_wm", [P, DC, 3 * D], BF16)
                R_wm = [Res(f"mg_wm{i}") for i in range(24)]
                wbr = sbt(ps, "mg_wbr", [P, 3, 4, D], BF16)
                R_wbr = Res("mg_wbr")
                bm = sbt(ps, "mg_bm", [P, 24], F32)
                R_bm = Res("mg_bm")
                fw.dma(bm[:], bmerge_d[l], writes=[R_bm])
                for cb in range(24):
                    load_wblock(l, 40 + cb, wm[:, :, cb * P:(cb + 1) * P], R_wm[cb], eng=("pool" if cb % 2 == 0 else "dve"))
                for n in range(3):
                    for mb in range(4):
                        load_generic(w_br_d[l, n, :, :, mb * 256:(mb + 1) * 256], wbr[:, n, :, mb * 256:(mb + 1) * 256], R_wbr, (4, 256), eng=("pool" if mb % 2 == 0 else "dve"))
                ot = [[sbt(ps, f"mg_ot{n}_{i}", [P, 4, TT], BF16) for n in range(3)] for i in range(2)]
                R_ot = [[Res(f"mg_ot{n}_{i}") for n in range(3)] for i in range(2)]
                mg = [sbt(ps, f"mg_mg{i}", [P, DC, TT], BF16) for i in range(2)]
                R_mgt = [Res(f"mg_mg{i}") for i in range(2)]
                gt = [sbt(ps, f"mg_gt{i}", [P, TT], F32) for i in range(3)]
                R_gt = [Res(f"mg_gt{i}") for i in range(3)]
                macc = [sbt(ps, f"mg_macc{i}", [P, TT], F32) for i in range(2)]
                R_macc = [Res(f"mg_macc{i}") for i in range(2)]
                pcount = 0
                for j in range(NT):
                    ji = j % 2
                    for n in range(3):
                        fw.dma(ot[ji][n][:], obr_d[n, :, :, j * TT:(j + 1) * TT], reads=[R_obr[n][j]], writes=[R_ot[ji][n]])
                    for db in range(DC):
                        ma = macc[db % 2]
                        R_ma = R_macc[db % 2]
                        for n in range(3):
                            k = pcount % 2
                            pcount += 1
                            bP, bL = 0 + k, 2 + k
                            for c in range(4):
                                fw.op("pe", lambda h: h.matmul(pb[bP][:, :], lhsT=wbr[:, n, c, db * P:(db + 1) * P], rhs=ot[ji][n][:, c, :], start=(c == 0), stop=(c == 3)),
                                      reads=[R_wbr, R_ot[ji][n]], writes=[R_pb[bP]])
                            cbm = n * 8 + db
                            for c in range(DC):
                                fw.op("pe", lambda h: h.matmul(pb[bL][:, :], lhsT=wm[:, c, cbm * P:(cbm + 1) * P], rhs=hT[:, c, j * TT:(j + 1) * TT], start=(c == 0), stop=(c == DC - 1)),
                                      reads=[R_wm[cbm], R_hT[j]], writes=[R_pb[bL]])
                            fw.op("act", lambda h: h.activation(out=gt[n][:], in_=pb[bL][:, :], func=AF.Sigmoid, bias=bm[:, cbm:cbm + 1]), reads=[R_pb[bL], R_bm], writes=[R_gt[n]])
                            if n == 0:
                                fw.op("dve", lambda h: h.tensor_tensor(out=ma[:], in0=pb[bP][:, :], in1=gt[n][:], op=ALU.mult), reads=[R_pb[bP], R_gt[n]], writes=[R_ma])
                            else:
                                fw.op("dve", lambda h: h.tensor_tensor(out=gt[n][:], in0=pb[bP][:, :], in1=gt[n][:], op=ALU.mult), reads=[R_pb[bP], R_gt[n]], writes=[R_gt[n]])
                                if n == 1:
                                    fw.op("pool", lambda h: h.tensor_tensor(out=ma[:], in0=ma[:], in1=gt[n][:], op=ALU.add), reads=[R_gt[n], R_ma], writes=[R_ma])
                                else:
                                    fw.op("pool", lambda h: h.tensor_tensor(out=mg[ji][:, db, :], in0=ma[:], in1=gt[n][:], op=ALU.add), reads=[R_gt[n], R_ma], writes=[R_mgt[ji]])
                    fw.dma(mg_d[:, :, j * TT:(j + 1) * TT], mg[ji][:], reads=[R_mgt[ji]], writes=[R_mgd[j]], semres=R_mgt[ji])
                fw.barrier()
            with ExitStack() as ps:
                wo = sbt(ps, "mg_wo", [P, DC, D], BF16)
                R_wo = Res("mg_wo")
                for mb in range(8):
                    load_generic(w_out_d[l, :, :, mb * P:(mb + 1) * P], wo[:, :, mb * P:(mb + 1) * P], R_wo, (DC, P), eng=("pool" if mb % 2 == 0 else "dve"))
                fnw = sbt(ps, "mg_fnw", [P, DC], F32)
                R_fnw = Res("mg_fnw")
                if last:
                    fw.dma(fnw[:], fnorm_d[:, :], writes=[R_fnw])
                xt = [sbt(ps, f"mg_xt{i}", [P, DC, TT], F32) for i in range(2)]
                R_xt = [Res(f"mg_xt{i}") for i in range(2)]
                mgi = [sbt(ps, f"mg_mgi{i}", [P, DC, TT], BF16) for i in range(2)]
                R_mgi = [Res(f"mg_mgi{i}") for i in range(2)]
                sq = [sbt(ps, f"mg_sq{i}", [P, TT], F32) for i in range(2)]
                R_sq = [Res(f"mg_sq{i}") for i in range(2)]
                rstd = sbt(ps, "mg_rstd", [P, TT], F32)
                R_rstd = Res("mg_rstd")
                ostage = [sbt(ps, f"mg_ost{i}", [P, D], F32) for i in range(2)] if last else None
                R_ost = [Res(f"mg_ost{i}") for i in range(2)]
                for j in range(NT):
                    ji = j % 2
                    x_, R_x = xt[ji], R_xt[ji]
                    fw.dma(mgi[ji][:], mg_d[:, :, j * TT:(j + 1) * TT], reads=[R_mgd[j]], writes=[R_mgi[ji]])
                    fw.dma(x_[:], xT_d[:, :, j * TT:(j + 1) * TT], reads=[R_xT[j]], writes=[R_x])
                    for ob in range(DC):
                        bO = 4 + (ob % 2)
                        for c in range(DC):
                            fw.op("pe", lambda h: h.matmul(pb[bO][:, :], lhsT=wo[:, c, ob * P:(ob + 1) * P], rhs=mgi[ji][:, c, :], start=(c == 0), stop=(c == DC - 1)),
                                  reads=[R_wo, R_mgi[ji]], writes=[R_pb[bO]])
                        fw.op("dve", lambda h: h.tensor_tensor(out=x_[:, ob, :], in0=x_[:, ob, :], in1=pb[bO][:, :], op=ALU.add), reads=[R_pb[bO], R_x], writes=[R_x])
                    dump("dbg_x1", x_[:], R_x, lambda o: o[:, :, j * TT:(j + 1) * TT])
                    norm_tile(x_, R_x, j, sq, R_sq, rstd, R_rstd, 6)
                    if not last:
                        fw.dma(xT_d[:, :, j * TT:(j + 1) * TT], x_[:], reads=[R_x], writes=[R_xT[j]], semres=R_x)
                        fw.op("dve", lambda h: h.tensor_tensor(out=hT[:, :, j * TT:(j + 1) * TT], in0=x_[:], in1=rstd[:].unsqueeze(1).broadcast_to([P, DC, TT]), op=ALU.mult),
                              reads=[R_x, R_rstd], writes=[R_hT[j]])
                    else:
                        for c in range(DC):
                            fw.op("dve", lambda h: h.scalar_tensor_tensor(out=x_[:, c, :], in0=x_[:, c, :], scalar=fnw[:, c:c + 1], in1=rstd[:], op0=ALU.mult, op1=ALU.mult), reads=[R_x, R_rstd, R_fnw], writes=[R_x])
                        for blk in range(4):
                            oi = blk % 2
                            for half in range(2):
                                bank = 0 + 2 * (blk % 2) + half
                                for cc in range(4):
                                    c = half * 4 + cc
                                    fw.op("pe", lambda h: h.transpose(pb[bank][:, cc * P:(cc + 1) * P], x_[:, c, blk * P:(blk + 1) * P], ident_f[:]),
                                          reads=[R_x, R_const], writes=[R_pb[bank]])
                                if half == 0:
                                    fw.op("act", lambda h: h.copy(out=ostage[oi][:, 0:512], in_=pb[bank][:, :]), reads=[R_pb[bank]], writes=[R_ost[oi]])
                                else:
                                    fw.op("dve", lambda h: h.tensor_copy(out=ostage[oi][:, 512:1024], in_=pb[bank][:, :]), reads=[R_pb[bank]], writes=[R_ost[oi]])
                            g = j * 4 + blk
                            fw.dma(y_out[g * P:(g + 1) * P, :], ostage[oi][:], reads=[R_ost[oi]], final=True, semres=R_ost[oi])
                fw.barrier()

        consts()
        phase0()
        for l in range(nlayers):
            if l > 0:
                fw.dma(nw[:], normw_d[l], writes=[R_nw], persistent=True)
            ssm_phase(l)
            if stop_after == "ssm":
                break
            pass
            if stop_after == "sb":
                break
            diff_phase(l)
            if stop_after == "diff":
                break
            merge_phase(l, last=(l == nlayers - 1))
        fw.finish()
        build_program.ninst = fw.ninst
        build_program.nsem = fw.nsem
    return nc


def prep_inputs(x, norm_w, w_in, b_merge, ssm_a_re, ssm_a_im, ssm_log_dt, ssm_b_re, ssm_b_im, ssm_c_re, ssm_c_im, ssm_d,
                ssm_w_glu, ssm_b_glu, diff_lq1, diff_lk1, diff_lq2, diff_lk2, diff_subln_w, w_branch, w_out, final_norm_w):
    f = lambda a: np.ascontiguousarray(np.asarray(a, dtype=np.float32))
    L = DEPTH
    w_in = np.asarray(w_in, dtype=np.float32)
    shared = {}
    shared["w_in"] = f(w_in.reshape(L, DC, P, 64, P).transpose(0, 3, 2, 1, 4))
    shared["w_glu"] = f(np.asarray(ssm_w_glu, np.float32).reshape(L, 4, P, 512).transpose(0, 2, 1, 3))
    shared["w_br"] = f(np.asarray(w_branch, np.float32).reshape(L, 3, 4, P, D).transpose(0, 1, 3, 2, 4))
    shared["w_out"] = f(np.asarray(w_out, np.float32).reshape(L, DC, P, D).transpose(0, 2, 1, 3))
    shared["norm_w"] = f(np.asarray(norm_w, np.float32).reshape(L, DC, P).transpose(0, 2, 1))
    shared["b_merge"] = f(np.asarray(b_merge, np.float32).reshape(L, 24, P).transpose(0, 2, 1))
    shared["b_glu"] = f(np.asarray(ssm_b_glu, np.float32).reshape(L, 4, P).transpose(0, 2, 1))
    shared["subln_w"] = f(np.asarray(diff_subln_w, np.float32).reshape(L, P, 1))
    shared["fnorm_w"] = f(np.asarray(final_norm_w, np.float32).reshape(DC, P).T)
    def gp(a):
        return f(np.asarray(a, np.float32).reshape(L, 16, 2, 64).transpose(0, 2, 3, 1).reshape(L, P, 16))
    shared["a_re"] = gp(ssm_a_re)
    shared["a_im"] = gp(ssm_a_im)
    shared["log_dt"] = gp(np.broadcast_to(np.asarray(ssm_log_dt, np.float32)[:, :, None], (L, 32, 64)))
    def gpc(a):
        return f(np.asarray(a, np.float32).reshape(L, 16, 2, 64, 16).transpose(0, 2, 3, 1, 4).reshape(L, P, 16, 16))
    shared["b_re"] = gpc(ssm_b_re)
    shared["b_im"] = gpc(ssm_b_im)
    shared["c_re"] = gpc(np.asarray(ssm_c_re, np.float32).transpose(0, 1, 3, 2))
    shared["c_im"] = gpc(np.asarray(ssm_c_im, np.float32).transpose(0, 1, 3, 2))
    shared["d_skip"] = f(np.asarray(ssm_d, np.float32).reshape(L, 4, 8, 16).transpose(0, 2, 3, 1).reshape(L, P, 4))
    lqk = np.stack([np.asarray(a, np.float32) for a in (diff_lq1, diff_lk1, diff_lq2, diff_lk2)], axis=1)
    shared["lqk"] = f(np.broadcast_to(lqk[:, None, :, :], (L, P, 4, 64)))
    xs = np.asarray(x, dtype=np.float32)
    return shared, xs


_PROGRAM_CACHE = {}


def kernel(**inputs):
    shared, xs = prep_inputs(**inputs)
    B = xs.shape[0]
    if "nc" not in _PROGRAM_CACHE:
        _PROGRAM_CACHE["nc"] = build_program()
    nc = _PROGRAM_CACHE["nc"]
    n_cores = 8
    in_maps = []
    for c in range(n_cores):
        m = dict(shared)
        m["x"] = np.ascontiguousarray(xs[c % B])
        in_maps.append(m)
    res = run_bass_kernel_spmd(nc, in_maps, core_ids=list(range(n_cores)))
    out = np.stack([np.asarray(res.results[b]["y"], dtype=np.float32) for b in range(B)], axis=0)
    return out
```
